# Optimizing a Trainium2 kernel written in Bass

```python
import jax, jax.numpy as jnp
from jax import lax
import numpy as np

D_MODEL = 1024
BATCH = 8
SEQ = 2048
DEPTH = 1
DEC_BATCH = 128
DEC_SEQ = 4
PAST_LEN = 16384
PAGE_SIZE = 128

D_CONV = D_MODEL
CONV_A_W = 3
N_HEADS = 8
HEAD_K = 128
HEAD_V = 128
KEY_W = N_HEADS * HEAD_K
VAL_W = N_HEADS * HEAD_V
QKV_W = 2 * KEY_W + VAL_W
CONV_B_W = 4
CHUNK = 64
EPS = 1e-6
SPLITS = (D_CONV, D_CONV, D_CONV, D_CONV, QKV_W, VAL_W, N_HEADS, N_HEADS, D_MODEL, D_MODEL)
N_IN = sum(SPLITS)

kernel_name = 'hybrid_shortconv_gdn_parallel_step'


def _split_points():
    pts, s = [], 0
    for w in SPLITS[:-1]:
        s += w
        pts.append(s)
    return pts


def _rmsnorm(x, w):
    xf = x.astype(jnp.float32)
    y = xf * lax.rsqrt(jnp.mean(xf * xf, axis=-1, keepdims=True) + EPS) * w.astype(jnp.float32)
    return y.astype(x.dtype)


def _l2norm(x):
    return x * lax.rsqrt(jnp.sum(x * x, axis=-1, keepdims=True) + EPS)


def _causal_conv(u, past, w):
    W = w.shape[0]
    T = u.shape[1]
    ext = jnp.concatenate([past.astype(u.dtype), u], axis=1)
    out = ext[:, 0:T] * w[0]
    for j in range(1, W):
        out = out + ext[:, j:j + T] * w[j]
    return out, ext[:, ext.shape[1] - (W - 1):]


def _gated_delta(q, k, v, beta, g, S0):
    Bn, T, H, K = q.shape
    V = v.shape[-1]
    C = min(CHUNK, T)
    pad = (-T) % C
    if pad:
        pw = ((0, 0), (0, pad), (0, 0), (0, 0))
        q, k, v = jnp.pad(q, pw), jnp.pad(k, pw), jnp.pad(v, pw)
        beta = jnp.pad(beta, pw[:3])
        g = jnp.pad(g, pw[:3])
    Tp = T + pad
    N = Tp // C
    def chunks(a):
        a = a.reshape((Bn, N, C) + a.shape[2:])
        return jnp.moveaxis(a, 3, 1)
    q, k, v, beta, g = chunks(q), chunks(k), chunks(v), chunks(beta), chunks(g)
    gc = jnp.cumsum(g, axis=-1)
    idx = jnp.arange(C)
    causal = idx[:, None] >= idx[None, :]
    strict = idx[:, None] > idx[None, :]
    L = jnp.exp(jnp.where(causal, gc[..., :, None] - gc[..., None, :], -jnp.inf))
    kb = k * beta[..., None]
    vb = v * beta[..., None]
    M = jnp.where(strict, jnp.einsum('bhnik,bhnjk->bhnij', kb, k) * L, 0.0)
    eye = jnp.broadcast_to(jnp.eye(C, dtype=M.dtype), M.shape)
    Tm = lax.linalg.triangular_solve(eye + M, eye, left_side=True, lower=True)
    u_pre = jnp.einsum('bhnij,bhnjv->bhniv', Tm, vb)
    w_dec = jnp.einsum('bhnij,bhnjk->bhnik', Tm, kb * jnp.exp(gc)[..., None])
    a_qk = jnp.where(causal, jnp.einsum('bhnik,bhnjk->bhnij', q, k) * L, 0.0)
    q_dec = q * jnp.exp(gc)[..., None]
    g_last = gc[..., -1]
    k_dec = k * jnp.exp(g_last[..., None] - gc)[..., None]
    xs = tuple(jnp.moveaxis(a, 2, 0) for a in (q_dec, a_qk, k_dec, u_pre, w_dec, jnp.exp(g_last)))

    def step(S, inp):
        qd, aqk, kd, up, wd, dl = inp
        u = up - jnp.einsum('bhck,bhkv->bhcv', wd, S)
        o = jnp.einsum('bhck,bhkv->bhcv', qd, S) + jnp.einsum('bhij,bhjv->bhiv', aqk, u)
        S = S * dl[..., None, None] + jnp.einsum('bhck,bhcv->bhkv', kd, u)
        return S, o

    S_new, o = lax.scan(step, S0, xs)
    o = jnp.transpose(o, (1, 0, 3, 2, 4)).reshape(Bn, Tp, H, V)[:, :T]
    return o, S_new


def _layer(x, conv_a_buf, conv_qkv_buf, S0, w_in, conv_a_w, conv_b_w, a_log, dt_bias,
           onorm_w, w_out_a, w_out_b, w_o, norm_w):
    Bn, T, _ = x.shape
    u = _rmsnorm(x, norm_w)
    proj = jnp.einsum('btd,dn->btn', u, w_in)
    a_b, a_c, a_h, a_z, qkv, b_z, b_beta, b_alpha, g_a, g_b = jnp.split(proj, _split_points(), axis=-1)
    conv_out, new_a_buf = _causal_conv(a_c * a_h, conv_a_buf, conv_a_w)
    y_a = jnp.einsum('btc,cd->btd', jax.nn.silu(a_z) * a_b * conv_out, w_out_a)
    qkv_c, new_qkv_buf = _causal_conv(qkv, conv_qkv_buf, conv_b_w)
    qkv_c = jax.nn.silu(qkv_c).astype(jnp.float32)
    q = qkv_c[..., :KEY_W].reshape(Bn, T, N_HEADS, HEAD_K)
    k = qkv_c[..., KEY_W:2 * KEY_W].reshape(Bn, T, N_HEADS, HEAD_K)
    v = qkv_c[..., 2 * KEY_W:].reshape(Bn, T, N_HEADS, HEAD_V)
    q = _l2norm(q) * (HEAD_K ** -0.5)
    k = _l2norm(k)
    beta = jax.nn.sigmoid(b_beta.astype(jnp.float32))
    g = -jnp.exp(a_log.astype(jnp.float32)) * jax.nn.softplus(b_alpha.astype(jnp.float32) + dt_bias.astype(jnp.float32))
    o, S_new = _gated_delta(q, k, v, beta, g, S0.astype(jnp.float32))
    o = _rmsnorm(o, onorm_w) * jax.nn.silu(b_z.astype(jnp.float32).reshape(Bn, T, N_HEADS, HEAD_V))
    y_b = jnp.einsum('btc,cd->btd', o.reshape(Bn, T, VAL_W).astype(x.dtype), w_out_b)
    m = jax.nn.sigmoid(g_a) * y_a + jax.nn.sigmoid(g_b) * y_b
    return x + jnp.einsum('btd,de->bte', m, w_o), new_a_buf, new_qkv_buf, S_new


def setup_inputs(seed: int = 0) -> dict:
    key = jax.random.key(seed)
    ks = jax.random.split(key, 20)
    f = jnp.float32
    nrm = lambda k, s: jax.random.normal(k, s, f)
    A = jax.random.uniform(ks[8], (DEPTH, N_HEADS), f, 1.0, 16.0)
    dt = jnp.exp(jax.random.uniform(ks[9], (DEPTH, N_HEADS), f, np.log(1e-3), np.log(1e-1)))
    return {
        'x_prompt': nrm(ks[0], (BATCH, SEQ, D_MODEL)),
        'x_sample': nrm(ks[1], (DEC_BATCH, DEC_SEQ, D_MODEL)),
        'state_conv_a': nrm(ks[2], (DEPTH, DEC_BATCH, CONV_A_W - 1, D_CONV)),
        'state_conv_qkv': nrm(ks[3], (DEPTH, DEC_BATCH, CONV_B_W - 1, QKV_W)),
        'state_delta': 0.1 * nrm(ks[4], (DEPTH, DEC_BATCH, N_HEADS, HEAD_K, HEAD_V)),
        'w_in': nrm(ks[5], (DEPTH, D_MODEL, N_IN)) * D_MODEL ** -0.5,
        'conv_a_w': nrm(ks[6], (DEPTH, CONV_A_W, D_CONV)) * CONV_A_W ** -0.5,
        'conv_b_w': nrm(ks[7], (DEPTH, CONV_B_W, QKV_W)) * CONV_B_W ** -0.5,
        'a_log': jnp.log(A),
        'dt_bias': dt + jnp.log(-jnp.expm1(-dt)),
        'onorm_w': 1.0 + 0.02 * nrm(ks[10], (DEPTH, HEAD_V)),
        'w_out_a': nrm(ks[11], (DEPTH, D_CONV, D_MODEL)) * D_CONV ** -0.5,
        'w_out_b': nrm(ks[12], (DEPTH, VAL_W, D_MODEL)) * VAL_W ** -0.5,
        'w_o': nrm(ks[13], (DEPTH, D_MODEL, D_MODEL)) * D_MODEL ** -0.5,
        'norm_w': 1.0 + 0.02 * nrm(ks[14], (DEPTH, D_MODEL)),
        'final_norm_w': 1.0 + 0.02 * nrm(ks[15], (D_MODEL,)),
    }


def reference(x_prompt, x_sample, state_conv_a, state_conv_qkv, state_delta, w_in, conv_a_w,
              conv_b_w, a_log, dt_bias, onorm_w, w_out_a, w_out_b, w_o, norm_w, final_norm_w):
    hp, hs = x_prompt, x_sample
    pa, pq, pd, sa, sq, sd = [], [], [], [], [], []
    for l in range(DEPTH):
        params = (w_in[l], conv_a_w[l], conv_b_w[l], a_log[l], dt_bias[l], onorm_w[l],
                  w_out_a[l], w_out_b[l], w_o[l], norm_w[l])
        z_a = jnp.zeros((BATCH, CONV_A_W - 1, D_CONV), hp.dtype)
        z_q = jnp.zeros((BATCH, CONV_B_W - 1, QKV_W), hp.dtype)
        z_s = jnp.zeros((BATCH, N_HEADS, HEAD_K, HEAD_V), jnp.float32)
        hp, ba, bq, bs = _layer(hp, z_a, z_q, z_s, *params)
        pa.append(ba); pq.append(bq); pd.append(bs)
        hs, ba, bq, bs = _layer(hs, state_conv_a[l], state_conv_qkv[l], state_delta[l], *params)
        sa.append(ba); sq.append(bq); sd.append(bs)
    y_prompt = _rmsnorm(hp, final_norm_w)
    y_sample = _rmsnorm(hs, final_norm_w)
    new_conv_a_prompt = jnp.stack(pa)
    new_conv_qkv_prompt = jnp.stack(pq)
    new_delta_prompt = jnp.stack(pd)
    new_conv_a_sample = jnp.stack(sa)
    new_conv_qkv_sample = jnp.stack(sq)
    new_delta_sample = jnp.stack(sd)
    return (y_prompt, y_sample, new_conv_a_prompt, new_conv_qkv_prompt, new_delta_prompt,
            new_conv_a_sample, new_conv_qkv_sample, new_delta_sample)
```

```python
import os
import numpy as np
from contextlib import ExitStack
import concourse.bass as bass
import concourse.mybir as mybir
from concourse.bass_utils import run_bass_kernel_spmd

F32 = mybir.dt.float32
F32R = mybir.dt.float32r
BF16 = mybir.dt.bfloat16
ALU = mybir.AluOpType
AF = mybir.ActivationFunctionType

NCORES = 8
D = 1024
T = 2048
NS = 64
NT = T + NS
NB = 16
EPS = 1e-6
TILES = [(0, 512), (512, 512), (1024, 512), (1536, 512), (2048, 64)]
NEG = -1.0e9
PSUM_EXCL = not os.environ.get("K_NOEXCL")
DLT = BF16 if os.environ.get("K_DLT", "bf16") == "bf16" else F32R
DBL = BF16 if os.environ.get("K_DBL", "f32r") == "bf16" else F32R
ATTACH_WAIT = not os.environ.get("K_NOATTACH")

O_BA, O_CA, O_HA, O_ZA, O_Q, O_K, O_V, O_ZB, O_BETA, O_GA, O_GB = (
    0, 1024, 2048, 3072, 4096, 5120, 6144, 7168, 8192, 8208, 9232)

C_ID, C_UP, C_ONES, C_MP, C_US, C_OS, C_MS, C_BSEL, C_BM = 0, 128, 256, 384, 640, 704, 768, 896, 912
C_END = C_BM + 1024


class Tk:
    __slots__ = ("w", "r", "px")

    def __init__(self, px=False):
        self.w = None
        self.r = {}
        self.px = px


class Sched:
    ENG = ("pe", "act", "dve", "pool", "sp")

    def __init__(self, nc, sems, dma_sems):
        self.nc = nc
        self.sems = sems
        self.streams = {e: [] for e in self.ENG}
        self.cnt = {e: 0 for e in self.ENG}
        self.known = {e: {} for e in self.ENG}
        self.dma_keys = dma_sems
        self.dma_i = {q: 0 for q in dma_sems}
        self.latest = {}
        self.mute = False
        self.n_emit = 0
        self.stop_after = int(os.environ.get("K_STOP", "99"))
        self.n_ops = 0
        self.max_ph = int(os.environ.get("K_MAXPH", "-1"))
        self.max_ops = int(os.environ.get("K_MAXOPS", "0"))

    def _collect(self, eng, reads, writes):
        waits = {}
        kn = self.known[eng]

        def need(tok, is_raw):
            if tok is None:
                return
            key, val, teng = tok
            if teng == eng and eng == "pe":
                return
            if kn.get(key, 0) >= val:
                return
            if waits.get(key, 0) < val:
                waits[key] = val

        for t in reads:
            need(t.w, True)
        for t in writes:
            need(t.w, False)
            for tok in t.r.values():
                need(tok, False)
        for k, v in waits.items():
            kn[k] = v
        return list(waits.items())

    def _limit(self):
        self.n_ops += 1
        if self.n_emit == self.max_ph and self.n_ops > self.max_ops:
            self.mute = True

    def op(self, eng, fn, reads=(), writes=(), inc=True):
        self._limit()
        if self.mute:
            return None
        if PSUM_EXCL and eng != "pe":
            ex = [t for t in reads if t.px]
            if ex:
                reads = [t for t in reads if not t.px]
                writes = list(writes) + ex
        waits = self._collect(eng, reads, writes)
        seq = self.cnt[eng] + 1
        if inc:
            self.cnt[eng] = seq
            self.latest[eng] = seq
        tok = (eng, seq, eng)
        self.streams[eng].append((fn, waits, (eng, 1) if inc else None))
        for t in reads:
            t.r[eng] = tok
        for t in writes:
            t.w = tok
            t.r = {}
        return tok

    def dma(self, q, out_ap, in_ap, reads=(), writes=()):
        self._limit()
        if self.mute:
            return None
        keys = self.dma_keys[q]
        i = self.dma_i[q]
        self.dma_i[q] = i + 1
        R = len(keys)
        key = keys[i % R]
        val = 16 * (i // R + 1)
        waits = dict(self._collect(q, reads, writes))
        if i >= R and self.known[q].get(key, 0) < val - 16:
            waits[key] = max(waits.get(key, 0), val - 16)
            self.known[q][key] = val - 16
        tok = (key, val, "dma")
        self.latest[key] = val
        self.streams[q].append((lambda e: e.dma_start(out=out_ap, in_=in_ap), list(waits.items()), (key, 16)))
        for t in reads:
            t.r[key] = tok
        for t in writes:
            t.w = tok
            t.r = {}
        return tok

    def barrier(self):
        for eng in self.ENG:
            waits = []
            kn = self.known[eng]
            for key, val in self.latest.items():
                if key == eng:
                    continue
                if kn.get(key, 0) < val:
                    kn[key] = val
                    waits.append((key, val))
            if waits:
                self.streams[eng].append((None, waits, None))

    def emit(self):
        nc = self.nc
        streams = self.streams
        sems = self.sems

        def run(e, stream):
            for fn, waits, inc in stream:
                attach = None
                if fn is not None and waits and inc is not None and inc[1] == 1 and ATTACH_WAIT:
                    attach = waits[-1]
                    waits = waits[:-1]
                for key, val in waits:
                    e.wait_ge(sems[key], val)
                if fn is not None:
                    ins = fn(e)
                    if attach is not None:
                        ins._wait_ge(sems[attach[0]], attach[1])
                    if inc is not None:
                        ins.then_inc(sems[inc[0]], inc[1])

        if os.environ.get("K_DUMP"):
            for en in self.ENG:
                print("COUNT", self.n_emit, en, sum(len(w) + (1 if f is not None else 0) for f, w, i in streams[en]))
                print("STREAM", self.n_emit, en, [(w, i, f is not None) for f, w, i in streams[en]][-12:])
        with nc.Block() as block:
            @block.tensor
            def _(e):
                run(e, streams["pe"])

            @block.scalar
            def _(e):
                run(e, streams["act"])

            @block.vector
            def _(e):
                run(e, streams["dve"])

            @block.gpsimd
            def _(e):
                run(e, streams["pool"])

            @block.sync
            def _(e):
                run(e, streams["sp"])
        self.streams = {e: [] for e in self.ENG}
        self.n_emit += 1
        self.n_ops = 0
        if self.n_emit > self.stop_after:
            self.mute = True


def build_program(debug=False):
    nc = bass.Bass("TRN2", target_bir_lowering=False)
    dram = {}

    def din(name, shape):
        dram[name] = nc.dram_tensor(name, list(shape), F32, kind="ExternalInput").ap()
        return dram[name]

    def dout(name, shape):
        dram[name] = nc.dram_tensor(name, list(shape), F32, kind="ExternalOutput").ap()
        return dram[name]

    xT_d = din("xT", [8, 128, NT])
    xtok_d = din("xtok", [NT, D])
    wst_d = din("wstream", [96, 128, 1024])
    wba_d = din("wba", [128, 128])
    wo_d = din("wo", [128, 8192])
    pp_d = din("pp", [128, 136])
    rp_d = din("rp", [128, 1040])
    cst_d = din("cst", [128, C_END])
    sca_d = din("sca", [128, 256])
    scq_d = din("scq", [128, 1152])
    sdl_d = din("sdl", [8, 128, 2048])
    y_d = dout("y", [NT, D])
    cap_d = dout("ca_p", [128, 16])
    cas_d = dout("ca_s", [128, 256])
    cqp_d = dout("cq_p", [128, 72])
    cqs_d = dout("cq_s", [128, 1152])
    dlp_d = dout("dl_p", [8, 128, 128])
    dls_d = dout("dl_s", [8, 128, 2048])

    with ExitStack() as top:
        def sb(name, shape, dt=F32, stack=top):
            return stack.enter_context(nc.sbuf_tensor("s_" + name, list(shape), dt))

        sems = {}
        for e in ("pe", "act", "dve", "pool"):
            sems[e] = top.enter_context(nc.semaphore("s_" + e))
        dma_sems = {"sp": [], "pool": []}
        for i in range(8):
            k = "dsp%d" % i
            sems[k] = top.enter_context(nc.semaphore(k))
            dma_sems["sp"].append(k)
        for i in range(4):
            k = "dpl%d" % i
            sems[k] = top.enter_context(nc.semaphore(k))
            dma_sems["pool"].append(k)
        S = Sched(nc, sems, dma_sems)

        banks = [top.enter_context(nc.psum_tensor("bank%d" % i, [128, 512], F32)) for i in range(8)]
        bank_tk = [Tk(px=True) for _ in range(8)]

        uT = sb("uT", [128, 8, NT], BF16)
        A = sb("A", [128, 8, NT], BF16)
        Mb = sb("Mb", [128, 8, NT], BF16)
        NW = 8
        wsl = sb("wsl", [128, NW, 8, 128], BF16)
        cst = sb("cst", [128, C_END], F32)
        pp = sb("pp", [128, 136], F32)
        rpb = sb("rpb", [128, 1040], F32)
        identR = sb("identR", [128, 128], F32R)
        ident2 = sb("ident2", [128, 2, 128], F32)
        identB = sb("identB", [128, 128], BF16)
        onesR = sb("onesR", [128, 128], F32R)
        onesB = sb("onesB", [128, 128], BF16)
        maskPB = sb("maskPB", [128, 256], BF16)
        maskSB = sb("maskSB", [128, 128], BF16)
        halfc = sb("halfc", [128, 4], F32)
        epsc = sb("epsc", [128, 1], F32)
        onwh = sb("onwh", [128, 1], F32)
        wba = sb("wba", [128, 8, 16], BF16)
        stg_cap = sb("stg_cap", [128, 8, 2], F32)
        stg_cas = sb("stg_cas", [128, 8, 16, 2], F32)
        stg_cqp = sb("stg_cqp", [128, 24, 3], F32)
        stg_cqs = sb("stg_cqs", [128, 24, 16, 3], F32)
        NTL = 17
        t_beta = sb("t_beta", [128, NTL, 8], F32)
        t_alpha = sb("t_alpha", [128, NTL, 8], F32)
        t_g = sb("t_g", [128, NTL, 8], F32)
        t_gcgl = sb("t_gcgl", [128, NTL, 16], F32)
        t_negc = sb("t_negc", [128, NTL, 8], F32)
        t_a = sb("t_a", [128, NTL, 8], F32)
        t_nega = sb("t_nega", [128, NTL, 8], F32)
        t_dk = sb("t_dk", [128, NTL, 8], F32)
        t_dl = sb("t_dl", [128, NTL, 8], F32)
        dl_s = sb("dl_s", [128, 8, 16], F32)
        negA = sb("negA", [128, 8], F32)
        k_uT = [[Tk() for _ in range(5)] for _ in range(8)]
        k_A = [[Tk() for _ in range(5)] for _ in range(8)]
        k_Mb = [[Tk() for _ in range(5)] for _ in range(8)]
        k_wsl = [Tk() for _ in range(NW)]
        k_cst, k_pp, k_rpb, k_c2, k_ba, k_stg = Tk(), Tk(), Tk(), Tk(), Tk(), Tk()

        ident = cst[:, C_ID:C_ID + 128]
        nwc = lambda k: pp[:, k:k + 1]
        cawc = lambda c, j: pp[:, 8 + c * 3 + j: 8 + c * 3 + j + 1]
        cbwc = lambda c, j: pp[:, 32 + c * 4 + j: 32 + c * 4 + j + 1]
        onw = pp[:, 128:129]
        fnw_bc = rpb[:, 0:1024]
        alog_bc = rpb[:, 1024:1032]
        dtb_bc = rpb[:, 1032:1040]

        wcount = [0]

        def wload():
            i = wcount[0]
            wcount[0] += 1
            s = i % NW
            S.dma("pool", wsl[:, s, :, :].rearrange("p k n -> p (k n)"), wst_d[i], writes=[k_wsl[s]])
            return s

        pj_rr = [0]

        pj_n = [4]

        def pj_next():
            b = pj_rr[0] % pj_n[0]
            pj_rr[0] += 1
            return b

        def proj(bank, slot, src, src_tk, t0, W, extra_reads=()):
            tt = min(t0 // 512, 4)
            for k in range(8):
                S.op("pe", (lambda e, k=k: e.matmul(banks[bank][:, 0:W], wsl[:, slot, k, :], src[:, k, t0:t0 + W],
                                                    start=(k == 0), stop=(k == 7))),
                     reads=[k_wsl[slot], src_tk[k][tt]], writes=[bank_tk[bank]], inc=(k == 7))

        S.dma("sp", cst[:, :], cst_d[:, :], writes=[k_cst])
        S.dma("sp", pp[:, :], pp_d[:, :], writes=[k_pp])
        S.dma("sp", rpb[:, :], rp_d[:, :], writes=[k_rpb])
        S.dma("pool", wba[:, :, :].rearrange("p k n -> p (k n)"), wba_d[:, :], writes=[k_c2])
        S.op("dve", lambda e: e.tensor_copy(out=identR[:, :], in_=ident), reads=[k_cst], writes=[k_c2])
        S.op("dve", lambda e: e.tensor_copy(out=identB[:, :], in_=ident), reads=[k_cst], writes=[k_c2])
        S.op("dve", lambda e: e.tensor_copy(out=ident2[:, 0, :], in_=ident), reads=[k_cst], writes=[k_c2])
        S.op("dve", lambda e: e.tensor_copy(out=ident2[:, 1, :], in_=ident), reads=[k_cst], writes=[k_c2])
        S.op("dve", lambda e: e.tensor_copy(out=onesR[:, :], in_=cst[:, C_ONES:C_ONES + 128]), reads=[k_cst], writes=[k_c2])
        S.op("dve", lambda e: e.tensor_copy(out=onesB[:, :], in_=cst[:, C_ONES:C_ONES + 128]), reads=[k_cst], writes=[k_c2])
        S.op("dve", lambda e: e.tensor_copy(out=maskPB[:, :], in_=cst[:, C_MP:C_MP + 256]), reads=[k_cst], writes=[k_c2])
        S.op("dve", lambda e: e.tensor_copy(out=maskSB[:, :], in_=cst[:, C_MS:C_MS + 128]), reads=[k_cst], writes=[k_c2])
        S.op("dve", lambda e: e.memset(halfc[:, :], -0.5), writes=[k_c2])
        S.op("dve", lambda e: e.memset(epsc[:, :], EPS), writes=[k_c2])
        S.op("dve", lambda e: e.tensor_scalar(onwh[:, :], onw, 0.5, None, op0=ALU.mult), reads=[k_pp], writes=[k_c2])
        for tl in (t_beta, t_alpha, t_g, t_gcgl):
            S.op("dve", lambda e, tl=tl: e.memset(tl[:, :, :], 0.0), writes=[k_ba])
        S.op("dve", lambda e: e.memset(stg_cqs[:, :, :, :], 0.0), writes=[k_stg])

        with ExitStack() as ph:
            xin = [sb("xin%d" % i, [128, 8, 256], F32, ph) for i in range(2)]
            sq = [sb("sq%d" % i, [128, 8, 256], BF16, ph) for i in range(2)]
            ms = [sb("ms%d" % i, [128, 256], F32, ph) for i in range(2)]
            rs = [sb("rs%d" % i, [128, 256], F32, ph) for i in range(2)]
            k_xin = [Tk(), Tk()]
            k_sq = [Tk(), Tk()]
            k_ms = [Tk(), Tk()]
            k_rs = [Tk(), Tk()]
            subt = [(t0, 256) for t0 in range(0, T, 256)] + [(T, NS)]
            for i, (t0, W) in enumerate(subt):
                b = i % 2
                tt = min(t0 // 512, 4)
                S.dma("sp", xin[b][:, :, 0:W], xT_d[:, :, t0:t0 + W].rearrange("k p t -> p k t"), writes=[k_xin[b]])
                for k in range(8):
                    S.op("act", lambda e, b=b, k=k, W=W: e.activation(out=sq[b][:, k, 0:W], in_=xin[b][:, k, 0:W],
                                                                      func=AF.Square, scale=1.0 / 32.0),
                         reads=[k_xin[b]], writes=[k_sq[b]])
                bk = pj_next()
                for k in range(8):
                    S.op("pe", lambda e, b=b, k=k, W=W, bk=bk: e.matmul(banks[bk][:, 0:W], onesB[:, :], sq[b][:, k, 0:W],
                                                                       start=(k == 0), stop=(k == 7)),
                         reads=[k_sq[b], k_c2], writes=[bank_tk[bk]], inc=(k == 7))
                S.op("act", lambda e, b=b, W=W, bk=bk: e.activation(out=ms[b][:, 0:W], in_=banks[bk][:, 0:W],
                                                                    func=AF.Ln, bias=epsc[:, 0:1], scale=1.0),
                     reads=[bank_tk[bk], k_c2], writes=[k_ms[b]])
                S.op("act", lambda e, b=b, W=W: e.activation(out=rs[b][:, 0:W], in_=ms[b][:, 0:W], func=AF.Exp, scale=-0.5),
                     reads=[k_ms[b]], writes=[k_rs[b]])
                for k in range(8):
                    S.op("dve", lambda e, b=b, k=k, W=W, t0=t0: e.scalar_tensor_tensor(
                        out=uT[:, k, t0:t0 + W], in0=xin[b][:, k, 0:W], scalar=nwc(k), in1=rs[b][:, 0:W],
                        op0=ALU.mult, op1=ALU.mult),
                         reads=[k_xin[b], k_rs[b], k_pp], writes=[k_uT[k][tt]])
            S.barrier()
            S.emit()

        with ExitStack() as ph:
            tmp8 = sb("tmp8", [128, NTL, 8], F32, ph)
            k_t8 = Tk()
            S.op("act", lambda e: e.activation(out=negA[:, :], in_=alog_bc, func=AF.Exp), reads=[k_rpb], writes=[k_ba])
            S.op("dve", lambda e: e.tensor_scalar(negA[:, :], negA[:, :], -1.0, None, op0=ALU.mult), reads=[k_ba], writes=[k_ba])
            for n in range(NTL):
                C = 128 if n < 16 else NS
                t0 = n * 128
                tt = min(t0 // 512, 4)
                bk = pj_next()
                for k in range(8):
                    S.op("pe", lambda e, k=k, C=C, t0=t0, bk=bk: e.matmul(banks[bk][0:C, 0:16], uT[:, k, t0:t0 + C], wba[:, k, :],
                                                                         start=(k == 0), stop=(k == 7)),
                         reads=[k_uT[k][tt], k_c2], writes=[bank_tk[bk]], inc=(k == 7))
                S.op("act", lambda e, n=n, C=C, bk=bk: e.activation(out=t_beta[0:C, n, :], in_=banks[bk][0:C, 0:8], func=AF.Tanh, scale=0.5),
                     reads=[bank_tk[bk]], writes=[k_ba])
                S.op("dve", lambda e, n=n, C=C, bk=bk: e.tensor_tensor(out=t_alpha[0:C, n, :], in0=banks[bk][0:C, 8:16], in1=dtb_bc[0:C, :], op=ALU.add),
                     reads=[bank_tk[bk], k_rpb], writes=[k_ba])
            S.op("dve", lambda e: e.tensor_scalar(t_beta[:, :, :], t_beta[:, :, :], 0.5, 0.5, op0=ALU.mult, op1=ALU.add),
                 reads=[k_ba], writes=[k_ba])
            S.op("act", lambda e: e.activation(out=tmp8[:, :, :], in_=t_alpha[:, :, :], func=AF.Exp), reads=[k_ba], writes=[k_t8])
            S.op("act", lambda e: e.activation(out=tmp8[:, :, :], in_=tmp8[:, :, :], func=AF.Ln, bias=1.0, scale=1.0), reads=[k_t8], writes=[k_t8])
            S.op("dve", lambda e: e.tensor_tensor(out=t_g[:, :, :], in0=tmp8[:, :, :],
                                                  in1=negA[:, :].unsqueeze(1).to_broadcast([128, NTL, 8]), op=ALU.mult),
                 reads=[k_t8, k_ba], writes=[k_ba])
            for n in range(NTL):
                C = 128 if n < 16 else NS
                U = cst[0:C, C_UP:C_UP + 128] if n < 16 else cst[0:C, C_US:C_US + 64]
                ON = cst[0:C, C_ONES:C_ONES + 128] if n < 16 else cst[0:C, C_OS:C_OS + 64]
                bk = pj_next()
                S.op("pe", lambda e, n=n, C=C, U=U, bk=bk: e.matmul(banks[bk][0:C, 0:8], U, t_g[0:C, n, :], start=True, stop=True),
                     reads=[k_ba, k_cst], writes=[bank_tk[bk]], inc=False)
                S.op("pe", lambda e, n=n, C=C, ON=ON, bk=bk: e.matmul(banks[bk][0:C, 8:16], ON, t_g[0:C, n, :], start=True, stop=True),
                     reads=[k_ba, k_cst], writes=[bank_tk[bk]])
                S.op("act", lambda e, n=n, C=C, bk=bk: e.activation(out=t_gcgl[0:C, n, :], in_=banks[bk][0:C, 0:16], func=AF.Copy),
                     reads=[bank_tk[bk]], writes=[k_ba])
            gc_all = t_gcgl[:, :, 0:8]
            gl_all = t_gcgl[:, :, 8:16]
            S.op("dve", lambda e: e.tensor_scalar(t_negc[:, :, :], gc_all, -1.0, None, op0=ALU.mult), reads=[k_ba], writes=[k_ba])
            S.op("act", lambda e: e.activation(out=t_a[:, :, :], in_=gc_all, func=AF.Exp), reads=[k_ba], writes=[k_ba])
            S.op("dve", lambda e: e.tensor_scalar(t_nega[:, :, :], t_a[:, :, :], -1.0, None, op0=ALU.mult), reads=[k_ba], writes=[k_ba])
            S.op("dve", lambda e: e.tensor_tensor(out=t_dk[:, :, :], in0=gl_all, in1=gc_all, op=ALU.subtract), reads=[k_ba], writes=[k_ba])
            S.op("act", lambda e: e.activation(out=t_dk[:, :, :], in_=t_dk[:, :, :], func=AF.Exp), reads=[k_ba], writes=[k_ba])
            S.op("act", lambda e: e.activation(out=t_dl[:, :, :], in_=gl_all, func=AF.Exp), reads=[k_ba], writes=[k_ba])
            for h in range(8):
                bk = pj_next()
                S.op("pe", lambda e, h=h, bk=bk: e.matmul(banks[bk][:, 0:16], t_g[0:NS, 16, h:h + 1].to_broadcast([NS, 128]),
                                                          cst[0:NS, C_BSEL:C_BSEL + 16], start=True, stop=True),
                     reads=[k_ba, k_cst], writes=[bank_tk[bk]])
                S.op("act", lambda e, h=h, bk=bk: e.activation(out=dl_s[:, h, :], in_=banks[bk][:, 0:16], func=AF.Exp),
                     reads=[bank_tk[bk]], writes=[k_ba])
            S.barrier()
            S.emit()

        with ExitStack() as ph:
            Es = sb("extAs", [128, NB, 6], F32, ph)
            E = sb("extA", [128, 2 + T], F32, ph)
            k_E = [Tk() for _ in range(5)]
            tmpc = sb("tmpc", [128, 512], F32, ph)
            acc = sb("acc", [128, 512], F32, ph)
            conv = sb("conv", [128, 512], F32, ph)
            th = sb("th", [128, 512], F32, ph)
            s2 = sb("s2", [128, 512], F32, ph)
            t2 = sb("t2", [128, 512], F32, ph)
            k_tmpc, k_acc, k_conv, k_th, k_s2, k_t2 = Tk(), Tk(), Tk(), Tk(), Tk(), Tk()
            S.op("dve", lambda e: e.memset(E[:, 0:2], 0.0), writes=[k_E[0]])
            slots_next = [wload() for _ in range(4)]
            for c in range(int(os.environ.get("K_P1C", "8"))):
                sC, sH, sZ, sB = slots_next
                if c < 7:
                    slots_next = [wload() for _ in range(4)]
                if not os.environ.get("K_SKIP_SCA"):
                    S.dma("sp", Es[:, :, 0:2], sca_d[:, c * 32:(c + 1) * 32].rearrange("p (b t) -> p b t", b=NB), writes=[k_E[4]])
                for tt, (t0, W) in enumerate(TILES[:int(os.environ.get("K_P1T", "5"))]):
                    smp = tt == 4
                    bC, bH = pj_next(), pj_next()
                    proj(bC, sC, uT, k_uT, t0, W)
                    proj(bH, sH, uT, k_uT, t0, W)
                    S.op("act", lambda e, W=W, bC=bC: e.activation(out=tmpc[:, 0:W], in_=banks[bC][:, 0:W], func=AF.Copy),
                         reads=[bank_tk[bC]], writes=[k_tmpc])
                    if not smp:
                        S.op("dve", lambda e, W=W, bH=bH, t0=t0: e.tensor_tensor(out=E[:, 2 + t0:2 + t0 + W], in0=banks[bH][:, 0:W],
                                                                              in1=tmpc[:, 0:W], op=ALU.mult),
                             reads=[bank_tk[bH], k_tmpc], writes=[k_E[tt]])
                        srcs = [E[:, t0 + j:t0 + j + W] for j in range(3)]
                        o_acc, o_conv = acc[:, 0:W], conv[:, 0:W]
                        rd = [k_E[tt]] + ([k_E[tt - 1]] if tt > 0 else [])
                    else:
                        v3 = lambda ap: ap.rearrange("p (b t) -> p b t", b=NB)
                        S.op("dve", lambda e, bH=bH: e.tensor_tensor(out=Es[:, :, 2:6], in0=v3(banks[bH][:, 0:NS]),
                                                                    in1=v3(tmpc[:, 0:NS]), op=ALU.mult),
                             reads=[bank_tk[bH], k_tmpc], writes=[k_E[4]])
                        srcs = [Es[:, :, j:j + 4] for j in range(3)]
                        o_acc, o_conv = v3(acc[:, 0:NS]), v3(conv[:, 0:NS])
                        rd = [k_E[4]]
                    S.op("dve", lambda e, s=srcs[0], o=o_acc, c=c: e.tensor_scalar(o, s, cawc(c, 0), None, op0=ALU.mult),
                         reads=rd + [k_pp], writes=[k_acc])
                    S.op("dve", lambda e, s=srcs[1], o=o_acc, c=c: e.scalar_tensor_tensor(out=o, in0=s, scalar=cawc(c, 1), in1=o,
                                                                                        op0=ALU.mult, op1=ALU.add),
                         reads=rd + [k_pp, k_acc], writes=[k_acc])
                    S.op("dve", lambda e, s=srcs[2], o=o_acc, oc=o_conv, c=c: e.scalar_tensor_tensor(out=oc, in0=s, scalar=cawc(c, 2), in1=o,
                                                                                                  op0=ALU.mult, op1=ALU.add),
                         reads=rd + [k_pp, k_acc], writes=[k_conv])
                    bZ, bB = pj_next(), pj_next()
                    proj(bZ, sZ, uT, k_uT, t0, W)
                    proj(bB, sB, uT, k_uT, t0, W)
                    S.op("act", lambda e, W=W, bZ=bZ: e.activation(out=th[:, 0:W], in_=banks[bZ][:, 0:W], func=AF.Tanh, scale=0.5),
                         reads=[bank_tk[bZ]], writes=[k_th])
                    S.op("dve", lambda e, W=W, bZ=bZ: e.scalar_tensor_tensor(out=s2[:, 0:W], in0=th[:, 0:W], scalar=1.0, in1=banks[bZ][:, 0:W],
                                                                          op0=ALU.add, op1=ALU.mult),
                         reads=[k_th, bank_tk[bZ]], writes=[k_s2])
                    S.op("dve", lambda e, W=W, bB=bB: e.tensor_tensor(out=t2[:, 0:W], in0=banks[bB][:, 0:W], in1=s2[:, 0:W], op=ALU.mult),
                         reads=[k_s2, bank_tk[bB]], writes=[k_t2])
                    S.op("dve", lambda e, W=W, t0=t0, c=c: e.scalar_tensor_tensor(out=A[:, c, t0:t0 + W], in0=t2[:, 0:W], scalar=0.5,
                                                                               in1=conv[:, 0:W], op0=ALU.mult, op1=ALU.mult),
                         reads=[k_t2, k_conv], writes=[k_A[c][tt]])
                if not os.environ.get("K_NOSTG"):
                    S.op("dve", lambda e, c=c: e.tensor_copy(out=stg_cap[:, c, :], in_=E[:, T:T + 2]), reads=[k_E[3]], writes=[k_stg])
                    if os.environ.get("K_ALT52"):
                        S.op("dve", lambda e, c=c: e.memset(tmpc[:, 0:32], 0.0), writes=[k_tmpc])
                    else:
                        S.op("dve", lambda e, c=c: e.tensor_copy(out=stg_cas[:, c, :, :], in_=Es[:, :, 4:6]), reads=[k_E[4]], writes=[k_stg])

            sg = sb("sg", [128, 512], F32, ph)
            k_sg = Tk()

            def gate_phase(first):
                slots_n = [wload() for _ in range(2)]
                for j in range(8):
                    sG, sO = slots_n
                    if j < 7:
                        slots_n = [wload() for _ in range(2)]
                    for tt, (t0, W) in enumerate(TILES):
                        bG, bY = pj_next(), pj_next()
                        proj(bG, sG, uT, k_uT, t0, W)
                        proj(bY, sO, A, k_A, t0, W)
                        S.op("act", lambda e, W=W, bG=bG: e.activation(out=th[:, 0:W], in_=banks[bG][:, 0:W], func=AF.Tanh, scale=0.5),
                             reads=[bank_tk[bG]], writes=[k_th])
                        S.op("dve", lambda e, W=W: e.tensor_scalar(sg[:, 0:W], th[:, 0:W], 0.5, 0.5, op0=ALU.mult, op1=ALU.add),
                             reads=[k_th], writes=[k_sg])
                        if first:
                            S.op("dve", lambda e, W=W, bY=bY, j=j, t0=t0: e.tensor_tensor(out=Mb[:, j, t0:t0 + W], in0=banks[bY][:, 0:W],
                                                                                     in1=sg[:, 0:W], op=ALU.mult),
                                 reads=[k_sg, bank_tk[bY]], writes=[k_Mb[j][tt]])
                        else:
                            S.op("dve", lambda e, W=W, bY=bY: e.tensor_tensor(out=t2[:, 0:W], in0=banks[bY][:, 0:W], in1=sg[:, 0:W], op=ALU.mult),
                                 reads=[k_sg, bank_tk[bY]], writes=[k_t2])
                            S.op("dve", lambda e, W=W, j=j, t0=t0: e.tensor_tensor(out=Mb[:, j, t0:t0 + W], in0=Mb[:, j, t0:t0 + W],
                                                                                in1=t2[:, 0:W], op=ALU.add),
                                 reads=[k_t2, k_Mb[j][tt]], writes=[k_Mb[j][tt]])

            if not os.environ.get("K_NO1B"):
                gate_phase(True)
            S.barrier()
            S.emit()

        with ExitStack() as ph:
            Eq = [sb("Eq%d" % i, [128, 3 + 512], F32, ph) for i in range(3)]
            Eqs = [sb("Eqs%d" % i, [128, NB, 7], F32, ph) for i in range(3)]
            k_Eq = [Tk() for _ in range(3)]
            qk = [sb("qk%d" % i, [128, 2, 512], F32R, ph) for i in range(2)]
            vv = [sb("vv%d" % i, [128, 512], DLT, ph) for i in range(2)]
            sz = [sb("sz%d" % i, [128, 512], F32, ph) for i in range(2)]
            qk.append(sb("qks", [128, 2, NS], F32R, ph))
            vv.append(sb("vvs", [128, NS], DLT, ph))
            sz.append(sb("szs", [128, NS], F32, ph))
            k_qk = [[Tk(), Tk()], [Tk(), Tk()], [Tk(), Tk()]]
            k_vv = [Tk(), Tk(), Tk()]
            k_sz = [Tk(), Tk(), Tk()]
            cv = sb("cv", [128, 512], F32, ph)
            th2 = sb("th2", [128, 512], F32, ph)
            sq2 = sb("sq2", [128, 512], BF16, ph)
            rn = sb("rn", [128, 512], F32, ph)
            k_cv, k_th2, k_sl, k_sq2, k_rn = Tk(), Tk(), Tk(), Tk(), Tk()
            nrm, k_nrm = th2, k_th2
            def dbl(name, shape, dt=F32):
                return [sb("%s%d" % (name, i), shape, dt, ph) for i in range(2)]
            def tri(name, shape, dt=F32):
                return [sb("%s%d" % (name, i), shape, dt, ph) for i in range(3)]
            kdec = tri("kdec", [128, 128], DLT)
            vtok = tri("vtok", [128, 128], F32)
            Wcat = dbl("Wcat", [128, 256], F32)
            AqkT = tri("AqkT", [128, 128], DLT)
            PPa = dbl("PPa", [128, 2, 128], DBL)
            PPb = dbl("PPb", [128, 2, 128], DBL)
            Ya = tri("Ya", [128, 128], DBL)
            Yb = dbl("Yb", [128, 128], DBL)
            rpt = dbl("rpt", [128, 128], DBL)
            ut = dbl("ut", [128, 128], DLT)
            Aus = dbl("Aus", [128, 128], F32)
            osb = dbl("osb", [128, 128], F32)
            onr = dbl("onr", [128, 128], DLT)
            ssq = dbl("ssq", [128, 2], F32)
            kt = {n: [Tk(), Tk(), Tk()] for n in ("kdec", "vtok", "Wcat", "AqkT", "PPa", "PPb", "Ya", "Yb", "rpt", "ut", "Aus", "osb",
                                            "onr", "ssq")}
            junk = onr
            kt["junk"] = kt["onr"]
            Sm = sb("Sm", [128, 128], F32, ph)
            Sr = sb("Sr", [128, 128], F32R, ph)
            k_Sm, k_Sr = Tk(), Tk()
            Ss = sb("Ss", [128, NB, 128], F32, ph)
            k_Ss = [Tk() for _ in range(NB)]
            kmr = [sb("kmr%d" % i, [128, NS], F32R, ph) for i in range(2)]
            qmr = [sb("qmr%d" % i, [128, NS], F32R, ph) for i in range(2)]
            kdm = [sb("kdm%d" % i, [NS, 128], DLT, ph) for i in range(2)]
            k_kmr, k_qmr, k_kdm = [Tk(), Tk()], [Tk(), Tk()], [Tk(), Tk()]
            Sbr = [sb("Sbr%d" % i, [128, 128], F32R, ph) for i in range(2)]
            k_Sbr = [Tk(), Tk()]
            B_T, B_W, B_D, B_R = 4, 5, 6, 7
            p_Tk = p_Tv = p_G = bank_tk[4]
            p_W = p_Au = p_oT = bank_tk[5]
            p_D = p_dY = p_S = bank_tk[6]
            p_kS = p_Tr = p_qS = bank_tk[7]

            def conv_unit(which, slot, h, tt, t0, W, buf):
                chunk = (8 if which == 0 else (0 if which == 1 else 16)) + h
                smp = tt == 4
                bk = pj_next()
                proj(bk, slot, uT, k_uT, t0, W)
                yield
                Ex = Eq[which]
                if not smp:
                    S.op("act", lambda e: e.activation(out=Ex[:, 3:3 + W], in_=banks[bk][:, 0:W], func=AF.Copy),
                         reads=[bank_tk[bk]], writes=[k_Eq[which]])
                    srcs = [Ex[:, j:j + W] for j in range(4)]
                    o = cv[:, 0:W]
                else:
                    v3 = lambda ap: ap.rearrange("p (b t) -> p b t", b=NB)
                    Exs = Eqs[which]
                    S.dma("sp", Exs[:, :, 0:3], scq_d[:, chunk * 48:(chunk + 1) * 48].rearrange("p (b t) -> p b t", b=NB),
                          writes=[k_Eq[which]])
                    S.op("act", lambda e: e.activation(out=Exs[:, :, 3:7], in_=v3(banks[bk][:, 0:NS]), func=AF.Copy),
                         reads=[bank_tk[bk]], writes=[k_Eq[which]])
                    srcs = [Exs[:, :, j:j + 4] for j in range(4)]
                    o = v3(cv[:, 0:NS])
                yield
                S.op("dve", lambda e: e.tensor_scalar(o, srcs[0], cbwc(chunk, 0), None, op0=ALU.mult),
                     reads=[k_Eq[which], k_pp], writes=[k_cv])
                for j in (1, 2, 3):
                    S.op("dve", lambda e, j=j: e.scalar_tensor_tensor(out=o, in0=srcs[j], scalar=cbwc(chunk, j), in1=o,
                                                                     op0=ALU.mult, op1=ALU.add),
                         reads=[k_Eq[which], k_pp, k_cv], writes=[k_cv])
                    yield
                if not smp:
                    if tt == 3:
                        S.op("dve", lambda e: e.tensor_copy(out=stg_cqp[:, chunk, :], in_=Ex[:, 512:515]),
                             reads=[k_Eq[which]], writes=[k_stg])
                    S.op("dve", lambda e: e.tensor_copy(out=Ex[:, 0:3], in_=Ex[:, 512:515]) if tt < 3 else e.memset(Ex[:, 0:3], 0.0),
                         reads=[k_Eq[which], k_cv], writes=[k_Eq[which]])
                else:
                    S.op("dve", lambda e: e.tensor_copy(out=stg_cqs[:, chunk, :, :], in_=Eqs[which][:, :, 4:7]),
                         reads=[k_Eq[which]], writes=[k_stg])
                S.op("act", lambda e: e.activation(out=th2[:, 0:W], in_=cv[:, 0:W], func=AF.Tanh, scale=0.5), reads=[k_cv], writes=[k_th2])
                yield
                if which == 2:
                    S.op("dve", lambda e: e.scalar_tensor_tensor(out=vv[buf][:, 0:W], in0=th2[:, 0:W], scalar=1.0, in1=cv[:, 0:W],
                                                                 op0=ALU.add, op1=ALU.mult),
                         reads=[k_th2, k_cv], writes=[k_vv[buf]])
                    return
                S.op("dve", lambda e: e.scalar_tensor_tensor(out=qk[buf][:, which, 0:W], in0=th2[:, 0:W], scalar=1.0, in1=cv[:, 0:W],
                                                             op0=ALU.add, op1=ALU.mult),
                     reads=[k_th2, k_cv], writes=[k_qk[buf][which]])

            def norm_unit(which, W, buf):
                S.op("act", lambda e: e.activation(out=sq2[:, 0:W], in_=qk[buf][:, which, 0:W], func=AF.Square),
                     reads=[k_qk[buf][which]], writes=[k_sq2])
                bn = pj_next()
                S.op("pe", lambda e: e.matmul(banks[bn][:, 0:W], onesB[:, :], sq2[:, 0:W], start=True, stop=True),
                     reads=[k_sq2, k_c2], writes=[bank_tk[bn]])
                yield
                S.op("act", lambda e: e.activation(out=nrm[:, 0:W], in_=banks[bn][:, 0:W], func=AF.Ln, bias=eps4[:, 0:1], scale=1.0),
                     reads=[bank_tk[bn], k_c2], writes=[k_nrm])
                S.op("act", lambda e: e.activation(out=rn[:, 0:W], in_=nrm[:, 0:W], func=AF.Exp, scale=-0.5),
                     reads=[k_nrm], writes=[k_rn])
                yield
                if which == 0:
                    S.op("dve", lambda e: e.tensor_tensor(out=qk[buf][:, 0, 0:W], in0=qk[buf][:, 0, 0:W], in1=rn[:, 0:W], op=ALU.mult),
                         reads=[k_qk[buf][0], k_rn], writes=[k_qk[buf][0]])
                else:
                    S.op("dve", lambda e: e.scalar_tensor_tensor(out=qk[buf][:, 1, 0:W], in0=qk[buf][:, 1, 0:W], scalar=128.0 ** -0.5, in1=rn[:, 0:W],
                                                                 op0=ALU.mult, op1=ALU.mult),
                         reads=[k_qk[buf][1], k_rn], writes=[k_qk[buf][1]])

            def zb_unit(slot, tt, t0, W, buf):
                bk = pj_next()
                proj(bk, slot, uT, k_uT, t0, W)
                yield
                S.op("act", lambda e: e.activation(out=th2[:, 0:W], in_=banks[bk][:, 0:W], func=AF.Tanh, scale=0.5),
                     reads=[bank_tk[bk]], writes=[k_th2])
                yield
                S.op("dve", lambda e: e.scalar_tensor_tensor(out=sz[buf][:, 0:W], in0=th2[:, 0:W], scalar=1.0, in1=banks[bk][:, 0:W],
                                                             op0=ALU.add, op1=ALU.mult),
                     reads=[k_th2, bank_tk[bk]], writes=[k_sz[buf]])

            cstR_ones = sb("cstR_ones", [128, 128], F32R, ph)
            eps4 = sb("eps4", [128, 1], F32, ph)
            S.op("dve", lambda e: e.tensor_copy(out=cstR_ones[:, :], in_=cst[:, C_ONES:C_ONES + 128]), reads=[k_cst], writes=[k_c2])
            S.op("dve", lambda e: e.memset(eps4[:, :], 4.0 * EPS), writes=[k_c2])
            for i in range(3):
                S.op("dve", lambda e, i=i: e.memset(Eq[i][:, 0:3], 0.0), writes=[k_Eq[i]])

            pj_n[0] = 2
            bT, bW, bD, bR, bX = banks[4], banks[5], banks[6], banks[7], banks[3]
            kT, kW, kD, kR, kX = bank_tk[4], bank_tk[5], bank_tk[6], bank_tk[7], bank_tk[3]

            class CV:
                pass

            TRI = ("kdec", "vtok", "AqkT", "Ya")

            def cvars(h, n, buf, c0, C, idx, smp):
                v = CV()
                p2, p3 = idx % 2, idx % 3
                v.p2, v.p3 = p2, p3
                col = lambda t: t[0:C, n, h:h + 1]
                v.beta, v.gc, v.negc, v.a_, v.nega, v.dk = (col(t_beta), t_gcgl[0:C, n, h:h + 1], col(t_negc), col(t_a), col(t_nega), col(t_dk))
                v.knT = qk[buf][:, 0, c0:c0 + C]
                v.qnT = qk[buf][:, 1, c0:c0 + C]
                v.vT = vv[buf][:, c0:c0 + C]
                v.K = lambda name: kt[name][p3 if name in TRI else p2]
                v.rq = [k_qk[buf][0], k_qk[buf][1]]
                v.L = 6 if not smp else 1
                v.PP = [(PPa[p2], v.K("PPa")), (PPb[p2], v.K("PPb"))]
                v.YY = [(Ya[p3], v.K("Ya")), (Yb[p2], v.K("Yb"))]
                v.Yf, v.kYf = v.YY[v.L % 2]
                return v

            bY = banks[2]
            kY = bank_tk[2]

            def chunk_front(h, n, buf, c0, C, idx, smp):
                v = cvars(h, n, buf, c0, C, idx, smp)
                K, knT, vT, rq, beta, gc, negc, dk = v.K, v.knT, v.vT, v.rq, v.beta, v.gc, v.negc, v.dk
                p2, p3 = v.p2, v.p3
                S.op("pe", lambda e: e.matmul(bT[0:C, 0:128], knT, identR[:, :], start=True, stop=True), reads=[rq[0], k_c2], writes=[kT])
                S.op("pe", lambda e: e.matmul(bT[0:C, 128:256], vT, (identB if DLT == BF16 else identR)[:, :], start=True, stop=True), reads=[k_vv[buf], k_c2], writes=[kT])
                S.op("pe", lambda e: e.matmul(bT[0:C, 256:256 + 2 * C].rearrange("p (a c) -> p a c", a=2), knT, qk[buf][:, :, c0:c0 + C],
                                              start=True, stop=True), reads=rq, writes=[kT])
                mB = maskPB[0:C, :] if not smp else maskSB[0:C, :]
                S.op("pe", lambda e: e.matmul(bW[0:C, 0:2 * C].rearrange("p (a c) -> p a c", a=2), gc.to_broadcast([C, C]), ident2[0:C, :, 0:C],
                                              start=True, stop=False), reads=[k_ba, k_c2], writes=[kW], inc=False)
                S.op("pe", lambda e: e.matmul(bW[0:C, 0:2 * C], identB[0:C, 0:C], mB, start=False, stop=True), reads=[k_c2], writes=[kW])
                yield
                S.op("act", lambda e: e.activation(out=Wcat[p2][0:C, 0:2 * C], in_=bW[0:C, 0:2 * C], func=AF.Exp, bias=negc, scale=1.0),
                     reads=[kW, k_ba], writes=[K("Wcat")])
                S.op("act", lambda e: e.activation(out=kdec[p3][0:C, :], in_=bT[0:C, 0:128], func=AF.Copy, scale=dk), reads=[kT, k_ba], writes=[K("kdec")])
                S.op("act", lambda e: e.activation(out=vtok[p3][0:C, :], in_=bT[0:C, 128:256], func=AF.Copy, scale=0.5), reads=[kT], writes=[K("vtok")])
                yield
                S.op("dve", lambda e: e.scalar_tensor_tensor(out=PPa[p2][0:C, 1, 0:C], in0=bT[0:C, 256:256 + C], scalar=beta, in1=Wcat[p2][0:C, C:2 * C],
                                                             op0=ALU.mult, op1=ALU.mult), reads=[kT, K("Wcat"), k_ba], writes=[K("PPa")])
                yield
                S.op("dve", lambda e: e.tensor_tensor(out=AqkT[p3][0:C, 0:C], in0=bT[0:C, 256 + C:256 + 2 * C], in1=Wcat[p2][0:C, 0:C], op=ALU.mult),
                     reads=[kT, K("Wcat")], writes=[K("AqkT")])
                S.op("pe", lambda e: e.matmul(bT[0:C, 0:C], PPa[p2][0:C, 1, 0:C], (identB if DBL == BF16 else identR)[0:C, 0:C], start=True, stop=True),
                     reads=[K("PPa"), k_c2], writes=[kT])
                yield
                S.op("act", lambda e: e.activation(out=PPa[p2][0:C, 0, 0:C], in_=bT[0:C, 0:C], func=AF.Copy), reads=[kT], writes=[K("PPa")])
                S.op("dve", lambda e: e.tensor_tensor(out=Ya[p3][0:C, 0:C], in0=ident[0:C, 0:C], in1=PPa[p2][0:C, 1, 0:C], op=ALU.subtract),
                     reads=[K("PPa"), k_cst], writes=[K("Ya")])
                yield

            def chunk_dbl(h, n, buf, c0, C, idx, smp):
                v = cvars(h, n, buf, c0, C, idx, smp)
                L, PP, YY = v.L, v.PP, v.YY
                for k in range(1, L + 1):
                    Pp, kPp = PP[(k - 1) % 2]
                    Pn, kPn = PP[k % 2]
                    S.op("pe", lambda e, Pp=Pp: e.matmul(bD[0:C, 0:C], Pp[0:C, 1, 0:C], Pp[0:C, 0, 0:C], start=True, stop=True),
                         reads=[kPp], writes=[kD])
                    if k < L:
                        S.op("pe", lambda e, Pp=Pp: e.matmul(bD[0:C, 128:128 + C], Pp[0:C, 0, 0:C], Pp[0:C, 1, 0:C], start=True, stop=True),
                             reads=[kPp], writes=[kD])
                    if k >= 2:
                        Yp, kYp = YY[(k - 2) % 2]
                        S.op("pe", lambda e, Pp=Pp, Yp=Yp: e.matmul(bY[0:C, 0:C], Pp[0:C, 0, 0:C], Yp[0:C, 0:C], start=True, stop=True),
                             reads=[kPp, kYp], writes=[kY])
                    yield
                    if k < L:
                        S.op("act", lambda e, Pn=Pn: e.activation(out=Pn[0:C, :, 0:C], in_=bD[0:C, 0:256].rearrange("p (a c) -> p a c", a=2)[:, :, 0:C],
                                                                  func=AF.Copy), reads=[kD], writes=[kPn])
                    else:
                        S.op("act", lambda e, Pn=Pn: e.activation(out=Pn[0:C, 0, 0:C], in_=bD[0:C, 0:C], func=AF.Copy), reads=[kD], writes=[kPn])
                    if k >= 2:
                        Yn, kYn = YY[(k - 1) % 2]
                        S.op("dve", lambda e, Yp=Yp, Yn=Yn: e.tensor_tensor(out=Yn[0:C, 0:C], in0=bY[0:C, 0:C], in1=Yp[0:C, 0:C], op=ALU.add),
                             reads=[kY, kYp], writes=[kYn])
                    yield
                PL, kPL = PP[L % 2]
                Yp, kYp = YY[(L - 1) % 2]
                Yf, kYf = v.Yf, v.kYf
                S.op("pe", lambda e: e.matmul(bY[0:C, 0:C], PL[0:C, 0, 0:C], Yp[0:C, 0:C], start=True, stop=True),
                     reads=[kPL, kYp], writes=[kY])
                yield
                S.op("dve", lambda e: e.tensor_tensor(out=Yf[0:C, 0:C], in0=bY[0:C, 0:C], in1=Yp[0:C, 0:C], op=ALU.add),
                     reads=[kY, kYp], writes=[kYf])
                yield

            def chunk_rec(h, n, buf, c0, C, idx, smp):
                v = cvars(h, n, buf, c0, C, idx, smp)
                par, p3 = v.p2, v.p3
                K, knT, qnT, rq, beta, a_, nega = v.K, v.knT, v.qnT, v.rq, v.beta, v.a_, v.nega
                Yf, kYf = v.Yf, v.kYf
                if not smp:
                    S.op("pe", lambda e: e.matmul(bR[0:C, 0:128], knT, Sr[:, :], start=True, stop=True), reads=[rq[0], k_Sr], writes=[kR])
                    S.op("pe", lambda e: e.matmul(bR[0:C, 256:384], qnT, Sr[:, :], start=True, stop=True), reads=[rq[1], k_Sr], writes=[kR])
                    qS_ap, qS_tk = bR[0:C, 256:384], kR
                else:
                    for b in range(NB):
                        pb = b % 2
                        S.op("act", lambda e, b=b, pb=pb: e.activation(out=Sbr[pb][:, :], in_=Ss[:, b, :], func=AF.Copy), reads=[k_Ss[b]], writes=[k_Sbr[pb]])
                        S.op("dve", lambda e, b=b, pb=pb: e.tensor_tensor(out=kmr[pb][:, :], in0=knT, in1=cst[:, C_BM + b * 64:C_BM + (b + 1) * 64], op=ALU.mult),
                             reads=[rq[0], k_cst], writes=[k_kmr[pb]])
                        S.op("dve", lambda e, b=b, pb=pb: e.tensor_tensor(out=qmr[pb][:, :], in0=qnT, in1=cst[:, C_BM + b * 64:C_BM + (b + 1) * 64], op=ALU.mult),
                             reads=[rq[1], k_cst], writes=[k_qmr[pb]])
                        S.op("pe", lambda e, b=b, pb=pb: e.matmul(bR[0:C, 0:128], kmr[pb][:, :], Sbr[pb][:, :], start=(b == 0), stop=(b == NB - 1)),
                             reads=[k_kmr[pb], k_Sbr[pb]], writes=[kR], inc=True)
                        S.op("pe", lambda e, b=b, pb=pb: e.matmul(bX[0:C, 256:384], qmr[pb][:, :], Sbr[pb][:, :], start=(b == 0), stop=(b == NB - 1)),
                             reads=[k_qmr[pb], k_Sbr[pb]], writes=[kX], inc=True)
                        yield
                    qS_ap, qS_tk = bX[0:C, 256:384], kX
                yield
                S.op("dve", lambda e: e.scalar_tensor_tensor(out=rpt[par][0:C, :], in0=bR[0:C, 0:128], scalar=nega, in1=vtok[p3][0:C, :],
                                                             op0=ALU.mult, op1=ALU.add), reads=[kR, K("vtok"), k_ba], writes=[K("rpt")])
                yield
                S.op("pe", lambda e: e.matmul(bR[0:C, 128:256], Yf[0:C, 0:C], rpt[par][0:C, :], start=True, stop=True), reads=[kYf, K("rpt")], writes=[kR])
                yield
                S.op("act", lambda e: e.activation(out=ut[par][0:C, :], in_=bR[0:C, 128:256], func=AF.Copy, scale=beta), reads=[kR, k_ba], writes=[K("ut")])
                yield
                if not smp:
                    S.op("pe", lambda e: e.matmul(bX[:, 128:256], kdec[p3][0:C, :], ut[par][0:C, :], start=True, stop=True), reads=[K("kdec"), K("ut")], writes=[kX])
                    S.op("pe", lambda e: e.matmul(bR[0:C, 384:512], AqkT[p3][0:C, 0:C], ut[par][0:C, :], start=True, stop=True), reads=[K("AqkT"), K("ut")], writes=[kR])
                    yield
                    S.op("dve", lambda e: e.scalar_tensor_tensor(out=Sr[:, :], in0=Sr[:, :], scalar=t_dl[:, n, h:h + 1], in1=bX[:, 128:256],
                                                                 op0=ALU.mult, op1=ALU.add), reads=[kX, k_Sr, k_ba], writes=[k_Sr])
                    yield
                else:
                    S.op("pe", lambda e: e.matmul(bR[0:C, 384:512], AqkT[p3][0:C, 0:C], ut[par][0:C, :], start=True, stop=True), reads=[K("AqkT"), K("ut")], writes=[kR])
                    yield
                S.op("act", lambda e: e.activation(out=Aus[par][0:C, :], in_=bR[0:C, 384:512], func=AF.Copy), reads=[kR], writes=[K("Aus")])
                yield
                S.op("dve", lambda e: e.scalar_tensor_tensor(out=osb[par][0:C, :], in0=qS_ap, scalar=a_, in1=Aus[par][0:C, :],
                                                             op0=ALU.mult, op1=ALU.add), reads=[qS_tk, K("Aus"), k_ba], writes=[K("osb")])
                yield
                if smp:
                    for b in range(NB):
                        pb = b % 2
                        S.op("dve", lambda e, b=b, pb=pb: e.tensor_scalar(kdm[pb][0:C, :], kdec[p3][0:C, :], cst[0:C, C_BSEL + b:C_BSEL + b + 1], None, op0=ALU.mult),
                             reads=[K("kdec"), k_cst], writes=[k_kdm[pb]])
                        S.op("pe", lambda e, b=b, pb=pb: e.matmul(bX[:, 128:256], kdm[pb][0:C, :], ut[par][0:C, :], start=True, stop=True),
                             reads=[k_kdm[pb], K("ut")], writes=[kX])
                        S.op("dve", lambda e, b=b: e.scalar_tensor_tensor(out=Ss[:, b, :], in0=Ss[:, b, :], scalar=dl_s[:, h, b:b + 1], in1=bX[:, 128:256],
                                                                        op0=ALU.mult, op1=ALU.add), reads=[kX, k_Ss[b], k_ba], writes=[k_Ss[b]])
                        yield

            def chunk_out(h, n, buf, c0, C, idx, smp):
                v = cvars(h, n, buf, c0, C, idx, smp)
                par, p3 = v.p2, v.p3
                K = v.K
                yield
                yield
                S.op("act", lambda e: e.activation(out=junk[par][0:C, :], in_=osb[par][0:C, :], func=AF.Square, scale=128.0 ** -0.5,
                                                   accum_out=ssq[par][0:C, 0:1]), reads=[K("osb")], writes=[K("junk"), K("ssq")])
                yield
                yield
                S.op("dve", lambda e: e.tensor_scalar(ssq[par][0:C, 0:1], ssq[par][0:C, 0:1], EPS, None, op0=ALU.add), reads=[K("ssq")], writes=[K("ssq")])
                yield
                yield
                S.op("pool", lambda e: e.tensor_tensor(out=ssq[par][0:C, 1:2], in0=ssq[par][0:C, 0:1], in1=halfc[0:C, 0:1], op=ALU.pow),
                     reads=[K("ssq"), k_c2], writes=[K("ssq")])
                yield
                yield
                yield
                S.op("dve", lambda e: e.tensor_scalar(onr[par][0:C, :], osb[par][0:C, :], ssq[par][0:C, 1:2], None, op0=ALU.mult),
                     reads=[K("osb"), K("ssq")], writes=[K("onr")])
                yield
                S.op("pe", lambda e: e.matmul(bX[:, 0:C], onr[par][0:C, :], (identB if DLT == BF16 else identR)[0:C, 0:C], start=True, stop=True), reads=[K("onr"), k_c2], writes=[kX])
                yield
                tcol = (n * 128) if not smp else T
                tt = min(tcol // 512, 4)
                S.op("dve", lambda e: e.scalar_tensor_tensor(out=A[:, h, tcol:tcol + C], in0=bX[:, 0:C], scalar=onwh[:, 0:1], in1=sz[buf][:, c0:c0 + C],
                                                             op0=ALU.mult, op1=ALU.mult), reads=[kX, k_sz[buf], k_c2], writes=[k_A[h][tt]])
                yield

            def prep(h, tt, slots):
                sK, sQ, sV, sZ = slots
                t0, W = TILES[tt]
                buf = tt % 2 if tt < 4 else 2
                yield from conv_unit(0, sK, h, tt, t0, W, buf)
                yield
                yield from conv_unit(1, sQ, h, tt, t0, W, buf)
                yield
                yield from conv_unit(2, sV, h, tt, t0, W, buf)
                yield
                yield from zb_unit(sZ, tt, t0, W, buf)
                yield
                yield from norm_unit(0, W, buf)
                yield
                yield from norm_unit(1, W, buf)
                yield

            def drain(g):
                for _ in g:
                    pass

            def step(g):
                try:
                    next(g)
                    return True
                except StopIteration:
                    return False

            S.op("dve", lambda e: e.memset(Sm[:, :], 0.0), writes=[k_Sm])
            NH = 8
            head_slots = {0: [wload() for _ in range(4)], 1: [wload() for _ in range(4)]}
            allch = []
            for h in range(NH):
                for tt in range(5):
                    ncc = 4 if tt < 4 else 1
                    for cc in range(ncc):
                        smp = tt == 4
                        args = (h, (tt * 4 + cc) if not smp else 16, (tt % 2) if not smp else 2, cc * 128, 128 if not smp else NS, len(allch), smp)
                        allch.append(dict(h=h, tt=tt, cc=cc, args=args, smp=smp, first_tile=(cc == 0), first_head=(tt == 0 and cc == 0),
                                          last_prompt=(tt == 3 and cc == 3)))
            NCH = len(allch)
            g_first = {}
            for g, ch in enumerate(allch):
                if ch["first_tile"]:
                    g_first[(ch["h"], ch["tt"])] = g
            drain(prep(0, 0, head_slots[0]))
            drain(chunk_front(*allch[0]["args"]))
            drain(chunk_dbl(*allch[0]["args"]))
            drain(chunk_front(*allch[1]["args"]))
            bgs = []
            for g, ch in enumerate(allch):
                h, tt, args = ch["h"], ch["tt"], ch["args"]
                while bgs and bgs[0][1] <= g:
                    drain(bgs.pop(0)[0])
                if ch["first_head"]:
                    S.op("act", lambda e: e.activation(out=Sr[:, :], in_=Sm[:, :], func=AF.Copy), reads=[k_Sm], writes=[k_Sr])
                    S.dma("sp", Ss[:, :, :], sdl_d[h].rearrange("k (b v) -> k b v", b=NB), writes=k_Ss)
                active = [chunk_rec(*args)]
                if g > 0:
                    if ch["smp"] or ch["first_tile"]:
                        drain(chunk_out(*allch[g - 1]["args"]))
                    else:
                        active.append(chunk_out(*allch[g - 1]["args"]))
                if g + 1 < NCH:
                    active.append(chunk_dbl(*allch[g + 1]["args"]))
                slow = chunk_front(*allch[g + 2]["args"]) if g + 2 < NCH else None
                if ch["first_tile"]:
                    if tt < 3:
                        bgs.append([prep(h, tt + 1, head_slots[h]), g_first[(h, tt + 1)] - 2])
                    elif tt == 3:
                        bgs.append([prep(h, 4, head_slots[h]), g_first[(h, 4)] - 2])
                        if h + 1 < NH:
                            bgs.append([prep(h + 1, 0, head_slots[h + 1]), g_first[(h + 1, 0)] - 2])
                    elif h + 2 < NH:
                        head_slots[h + 2] = [wload() for _ in range(4)]
                cyc = 0
                while active or slow is not None:
                    active = [x for x in active if step(x)]
                    if slow is not None and (cyc % 2 == 0 or not active) and not step(slow):
                        slow = None
                    if bgs and not step(bgs[0][0]):
                        bgs.pop(0)
                    cyc += 1
                if ch["last_prompt"]:
                    S.dma("sp", dlp_d[h], Sr[:, :].bitcast(F32), reads=[k_Sr])
                if ch["smp"]:
                    S.dma("sp", dls_d[h].rearrange("k (b v) -> k b v", b=NB), Ss[:, :, :], reads=k_Ss)
            for b in bgs:
                drain(b[0])
            drain(chunk_out(*allch[NCH - 1]["args"]))
            pj_n[0] = 4
            S.barrier()
            S.emit()

        with ExitStack() as ph:
            th = sb("thb", [128, 512], F32, ph)
            sg = sb("sgb", [128, 512], F32, ph)
            t2 = sb("t2b", [128, 512], F32, ph)
            k_th, k_sg, k_t2 = Tk(), Tk(), Tk()
            wo = sb("wo", [128, 8, 1024], BF16, ph)
            k_wo = Tk()
            S.dma("pool", wo[:, :, :].rearrange("p k n -> p (k n)"), wo_d[:, :], writes=[k_wo])
            slots_n = [wload() for _ in range(2)]
            for j in range(8):
                sG, sO = slots_n
                if j < 7:
                    slots_n = [wload() for _ in range(2)]
                for tt, (t0, W) in enumerate(TILES):
                    bG, bY = pj_next(), pj_next()
                    proj(bG, sG, uT, k_uT, t0, W)
                    proj(bY, sO, A, k_A, t0, W)
                    S.op("act", lambda e, W=W, bG=bG: e.activation(out=th[:, 0:W], in_=banks[bG][:, 0:W], func=AF.Tanh, scale=0.5),
                         reads=[bank_tk[bG]], writes=[k_th])
                    S.op("dve", lambda e, W=W: e.tensor_scalar(sg[:, 0:W], th[:, 0:W], 0.5, 0.5, op0=ALU.mult, op1=ALU.add),
                         reads=[k_th], writes=[k_sg])
                    S.op("dve", lambda e, W=W, bY=bY: e.tensor_tensor(out=t2[:, 0:W], in0=banks[bY][:, 0:W], in1=sg[:, 0:W], op=ALU.mult),
                         reads=[k_sg, bank_tk[bY]], writes=[k_t2])
                    S.op("dve", lambda e, W=W, j=j, t0=t0: e.tensor_tensor(out=Mb[:, j, t0:t0 + W], in0=Mb[:, j, t0:t0 + W],
                                                                        in1=t2[:, 0:W], op=ALU.add),
                         reads=[k_t2, k_Mb[j][tt]], writes=[k_Mb[j][tt]])
            xt = [sb("xt%d" % i, [128, D], F32, ph) for i in range(2)]
            hb = [sb("hb%d" % i, [128, D], F32, ph) for i in range(2)]
            yb = [sb("yb%d" % i, [128, D], F32, ph) for i in range(2)]
            jk = sb("jk", [128, D], F32, ph)
            s4 = [sb("s4%d" % i, [128, 2], F32, ph) for i in range(2)]
            k_xt, k_hb, k_yb, k_s4, k_jk = [Tk(), Tk()], [Tk(), Tk()], [Tk(), Tk()], [Tk(), Tk()], Tk()
            for n in range(NTL):
                C = 128 if n < 16 else NS
                r0 = n * 128
                tt = min(r0 // 512, 4)
                b = n % 2
                S.dma("sp", xt[b][0:C, :], xtok_d[r0:r0 + C, :], writes=[k_xt[b]])
                for half in range(2):
                    bk = pj_next()
                    for k in range(8):
                        S.op("pe", lambda e, k=k, C=C, r0=r0, bk=bk, half=half: e.matmul(banks[bk][0:C, :], Mb[:, k, r0:r0 + C],
                                                                                        wo[:, k, half * 512:(half + 1) * 512],
                                                                                        start=(k == 0), stop=(k == 7)),
                             reads=[k_Mb[k][tt], k_wo], writes=[bank_tk[bk]], inc=(k == 7))
                    S.op("dve", lambda e, C=C, bk=bk, half=half, b=b: e.tensor_tensor(out=hb[b][0:C, half * 512:(half + 1) * 512], in0=banks[bk][0:C, :],
                                                                                     in1=xt[b][0:C, half * 512:(half + 1) * 512], op=ALU.add),
                         reads=[bank_tk[bk], k_xt[b]], writes=[k_hb[b]])
                S.op("act", lambda e, C=C, b=b: e.activation(out=jk[0:C, :], in_=hb[b][0:C, :], func=AF.Square, scale=1.0 / 32.0,
                                                             accum_out=s4[b][0:C, 0:1]), reads=[k_hb[b]], writes=[k_jk, k_s4[b]])
                S.op("dve", lambda e, C=C, b=b: e.tensor_scalar(s4[b][0:C, 0:1], s4[b][0:C, 0:1], EPS, None, op0=ALU.add), reads=[k_s4[b]], writes=[k_s4[b]])
                S.op("pool", lambda e, C=C, b=b: e.tensor_tensor(out=s4[b][0:C, 1:2], in0=s4[b][0:C, 0:1], in1=halfc[0:C, 0:1], op=ALU.pow),
                     reads=[k_s4[b], k_c2], writes=[k_s4[b]])
                S.op("dve", lambda e, C=C, b=b: e.scalar_tensor_tensor(out=yb[b][0:C, :], in0=hb[b][0:C, :], scalar=s4[b][0:C, 1:2], in1=fnw_bc[0:C, :],
                                                                      op0=ALU.mult, op1=ALU.mult), reads=[k_hb[b], k_s4[b], k_rpb], writes=[k_yb[b]])
                S.dma("sp", y_d[r0:r0 + C, :], yb[b][0:C, :], reads=[k_yb[b]])
            S.dma("sp", cap_d[:, :], stg_cap[:, :, :].rearrange("p a b -> p (a b)"), reads=[k_stg])
            S.dma("sp", cas_d[:, :], stg_cas[:, :, :, :].rearrange("p a b c -> p (a b c)"), reads=[k_stg])
            S.dma("sp", cqp_d[:, :], stg_cqp[:, :, :].rearrange("p a b -> p (a b)"), reads=[k_stg])
            S.dma("sp", cqs_d[:, :], stg_cqs[:, :, :, :].rearrange("p a b c -> p (a b c)"), reads=[k_stg])
            S.barrier()
            S.emit()
    return nc


def _consts():
    c = np.zeros((128, C_END), np.float32)
    i = np.arange(128)
    c[:, C_ID:C_ID + 128] = np.eye(128, dtype=np.float32)
    c[:, C_UP:C_UP + 128] = (i[:, None] <= i[None, :]).astype(np.float32)
    c[:, C_ONES:C_ONES + 128] = 1.0
    incl = i[None, :] >= i[:, None]
    strict = i[None, :] > i[:, None]
    c[:, C_MP:C_MP + 128] = np.where(incl, 0.0, NEG)
    c[:, C_MP + 128:C_MP + 256] = np.where(strict, 0.0, NEG)
    j = np.arange(64)
    same = (j[:, None] // 4) == (j[None, :] // 4)
    c[:64, C_US:C_US + 64] = (same & (j[:, None] <= j[None, :])).astype(np.float32)
    c[:64, C_OS:C_OS + 64] = same.astype(np.float32)
    c[:64, C_MS:C_MS + 64] = np.where(same & (j[None, :] >= j[:, None]), 0.0, NEG)
    c[:64, C_MS + 64:C_MS + 128] = np.where(same & (j[None, :] > j[:, None]), 0.0, NEG)
    c[64:, C_MS:C_MS + 128] = NEG
    bsel = (j[:, None] // 4 == np.arange(16)[None, :]).astype(np.float32)
    c[:64, C_BSEL:C_BSEL + 16] = bsel
    bm = np.tile(bsel.T.reshape(1, 16 * 64), (128, 1))
    c[:, C_BM:C_BM + 1024] = bm
    return c


def _blk(w, c0):
    return np.ascontiguousarray(w[:, c0:c0 + 128].reshape(8, 128, 128).transpose(1, 0, 2)).reshape(128, 1024)


_PROG = {}


def kernel(x_prompt, x_sample, state_conv_a, state_conv_qkv, state_delta, w_in, conv_a_w, conv_b_w, a_log, dt_bias,
           onorm_w, w_out_a, w_out_b, w_o, norm_w, final_norm_w):
    f = np.float32
    w_in0 = np.asarray(w_in[0], f)
    woa = np.asarray(w_out_a[0], f)
    wob = np.asarray(w_out_b[0], f)
    blocks = []
    for c in range(8):
        for off in (O_CA, O_HA, O_ZA, O_BA):
            blocks.append(_blk(w_in0, off + c * 128))
    for j in range(8):
        blocks.append(_blk(w_in0, O_GA + j * 128))
        blocks.append(_blk(woa, j * 128))
    for h in range(8):
        for off in (O_K, O_Q, O_V, O_ZB):
            blocks.append(_blk(w_in0, off + h * 128))
    for j in range(8):
        blocks.append(_blk(w_in0, O_GB + j * 128))
        blocks.append(_blk(wob, j * 128))
    wstream = np.stack(blocks)
    wba = np.ascontiguousarray(w_in0[:, O_BETA:O_BETA + 16].reshape(8, 128, 16).transpose(1, 0, 2)).reshape(128, 128)
    wo = np.ascontiguousarray(np.asarray(w_o[0], f).reshape(8, 128, 1024).transpose(1, 0, 2)).reshape(128, 8192)
    pp = np.zeros((128, 136), f)
    pp[:, 0:8] = np.asarray(norm_w[0], f).reshape(8, 128).T
    pp[:, 8:32] = np.asarray(conv_a_w[0], f).reshape(3, 8, 128).transpose(2, 1, 0).reshape(128, 24)
    pp[:, 32:128] = np.asarray(conv_b_w[0], f).reshape(4, 24, 128).transpose(2, 1, 0).reshape(128, 96)
    pp[:, 128] = np.asarray(onorm_w[0], f)
    rp = np.zeros((128, 1040), f)
    rp[:, 0:1024] = np.asarray(final_norm_w, f)[None, :]
    rp[:, 1024:1032] = np.asarray(a_log[0], f)[None, :]
    rp[:, 1032:1040] = np.asarray(dt_bias[0], f)[None, :]
    cst = _consts()
    in_maps = []
    for i in range(NCORES):
        xs = np.asarray(x_sample[16 * i:16 * i + 16], f).reshape(NS, D)
        x_all = np.concatenate([np.asarray(x_prompt[i], f), xs], axis=0)
        xT = np.ascontiguousarray(x_all.T).reshape(8, 128, NT)
        sca = np.ascontiguousarray(np.asarray(state_conv_a[0, 16 * i:16 * i + 16], f).reshape(16, 2, 8, 128).transpose(3, 2, 0, 1)).reshape(128, 256)
        scq = np.ascontiguousarray(np.asarray(state_conv_qkv[0, 16 * i:16 * i + 16], f).reshape(16, 3, 24, 128).transpose(3, 2, 0, 1)).reshape(128, 1152)
        sdl = np.ascontiguousarray(np.asarray(state_delta[0, 16 * i:16 * i + 16], f).transpose(1, 2, 0, 3)).reshape(8, 128, 2048)
        in_maps.append({"xT": xT, "xtok": x_all, "wstream": wstream, "wba": wba, "wo": wo, "pp": pp, "rp": rp, "cst": cst,
                        "sca": sca, "scq": scq, "sdl": sdl})
    if "nc" not in _PROG:
        _PROG["nc"] = build_program()
    res = run_bass_kernel_spmd(_PROG["nc"], in_maps, core_ids=list(range(NCORES)))
    R = res.results
    y_prompt = np.stack([R[i]["y"][:T] for i in range(NCORES)])
    y_sample = np.concatenate([R[i]["y"][T:].reshape(16, 4, D) for i in range(NCORES)], axis=0)
    ncap = np.stack([R[i]["ca_p"].reshape(128, 8, 2).transpose(2, 1, 0).reshape(2, 1024) for i in range(NCORES)])[None]
    ncqp = np.stack([R[i]["cq_p"].reshape(128, 24, 3).transpose(2, 1, 0).reshape(3, 3072) for i in range(NCORES)])[None]
    ndp = np.stack([R[i]["dl_p"] for i in range(NCORES)])[None]
    ncas = np.concatenate([R[i]["ca_s"].reshape(128, 8, 16, 2).transpose(2, 3, 1, 0).reshape(16, 2, 1024) for i in range(NCORES)], axis=0)[None]
    ncqs = np.concatenate([R[i]["cq_s"].reshape(128, 24, 16, 3).transpose(2, 3, 1, 0).reshape(16, 3, 3072) for i in range(NCORES)], axis=0)[None]
    nds = np.concatenate([R[i]["dl_s"].reshape(8, 128, 16, 128).transpose(2, 0, 1, 3) for i in range(NCORES)], axis=0)[None]
    return (y_prompt.astype(f), y_sample.astype(f), ncap.astype(f), ncqp.astype(f), ndp.astype(f),
            ncas.astype(f), ncqs.astype(f), nds.astype(f))
```

```python
import os
import numpy as np
from contextlib import ExitStack
import concourse.bass as bass
import concourse.mybir as mybir
from concourse.bass_utils import run_bass_kernel_spmd

F32 = mybir.dt.float32
F32R = mybir.dt.float32r
BF16 = mybir.dt.bfloat16
ALU = mybir.AluOpType
AF = mybir.ActivationFunctionType

NCORES = 8
D = 1024
T = 2048
NS = 64
NT = T + NS
NB = 16
EPS = 1e-6
TILES = [(0, 512), (512, 512), (1024, 512), (1536, 512), (2048, 64)]
NEG = -1.0e9
PSUM_EXCL = not os.environ.get("K_NOEXCL")
DLT = BF16 if os.environ.get("K_DLT", "bf16") == "bf16" else F32R
DBL = BF16 if os.environ.get("K_DBL", "f32r") == "bf16" else F32R
ATTACH_WAIT = not os.environ.get("K_NOATTACH")

O_BA, O_CA, O_HA, O_ZA, O_Q, O_K, O_V, O_ZB, O_BETA, O_GA, O_GB = (
    0, 1024, 2048, 3072, 4096, 5120, 6144, 7168, 8192, 8208, 9232)

C_ID, C_UP, C_ONES, C_MP, C_US, C_OS, C_MS, C_BSEL, C_BM = 0, 128, 256, 384, 640, 704, 768, 896, 912
C_END = C_BM + 1024


class Tk:
    __slots__ = ("w", "r", "px")

    def __init__(self, px=False):
        self.w = None
        self.r = {}
        self.px = px


class Sched:
    ENG = ("pe", "act", "dve", "pool", "sp")

    def __init__(self, nc, sems, dma_sems):
        self.nc = nc
        self.sems = sems
        self.streams = {e: [] for e in self.ENG}
        self.cnt = {e: 0 for e in self.ENG}
        self.known = {e: {} for e in self.ENG}
        self.dma_keys = dma_sems
        self.dma_i = {q: 0 for q in dma_sems}
        self.latest = {}
        self.mute = False
        self.n_emit = 0
        self.stop_after = int(os.environ.get("K_STOP", "99"))
        self.n_ops = 0
        self.max_ph = int(os.environ.get("K_MAXPH", "-1"))
        self.max_ops = int(os.environ.get("K_MAXOPS", "0"))

    def _collect(self, eng, reads, writes):
        waits = {}
        kn = self.known[eng]

        def need(tok, is_raw):
            if tok is None:
                return
            key, val, teng = tok
            if teng == eng and eng == "pe":
                return
            if kn.get(key, 0) >= val:
                return
            if waits.get(key, 0) < val:
                waits[key] = val

        for t in reads:
            need(t.w, True)
        for t in writes:
            need(t.w, False)
            for tok in t.r.values():
                need(tok, False)
        for k, v in waits.items():
            kn[k] = v
        return list(waits.items())

    def _limit(self):
        self.n_ops += 1
        if self.n_emit == self.max_ph and self.n_ops > self.max_ops:
            self.mute = True

    def op(self, eng, fn, reads=(), writes=(), inc=True):
        self._limit()
        if self.mute:
            return None
        if PSUM_EXCL and eng != "pe":
            ex = [t for t in reads if t.px]
            if ex:
                reads = [t for t in reads if not t.px]
                writes = list(writes) + ex
        waits = self._collect(eng, reads, writes)
        seq = self.cnt[eng] + 1
        if inc:
            self.cnt[eng] = seq
            self.latest[eng] = seq
        tok = (eng, seq, eng)
        self.streams[eng].append((fn, waits, (eng, 1) if inc else None))
        for t in reads:
            t.r[eng] = tok
        for t in writes:
            t.w = tok
            t.r = {}
        return tok

    def dma(self, q, out_ap, in_ap, reads=(), writes=()):
        self._limit()
        if self.mute:
            return None
        keys = self.dma_keys[q]
        i = self.dma_i[q]
        self.dma_i[q] = i + 1
        R = len(keys)
        key = keys[i % R]
        val = 16 * (i // R + 1)
        waits = dict(self._collect(q, reads, writes))
        if i >= R and self.known[q].get(key, 0) < val - 16:
            waits[key] = max(waits.get(key, 0), val - 16)
            self.known[q][key] = val - 16
        tok = (key, val, "dma")
        self.latest[key] = val
        self.streams[q].append((lambda e: e.dma_start(out=out_ap, in_=in_ap), list(waits.items()), (key, 16)))
        for t in reads:
            t.r[key] = tok
        for t in writes:
            t.w = tok
            t.r = {}
        return tok

    def barrier(self):
        for eng in self.ENG:
            waits = []
            kn = self.known[eng]
            for key, val in self.latest.items():
                if key == eng:
                    continue
                if kn.get(key, 0) < val:
                    kn[key] = val
                    waits.append((key, val))
            if waits:
                self.streams[eng].append((None, waits, None))

    def emit(self):
        nc = self.nc
        streams = self.streams
        sems = self.sems

        def run(e, stream):
            for fn, waits, inc in stream:
                attach = None
                if fn is not None and waits and inc is not None and inc[1] == 1 and ATTACH_WAIT:
                    attach = waits[-1]
                    waits = waits[:-1]
                for key, val in waits:
                    e.wait_ge(sems[key], val)
                if fn is not None:
                    ins = fn(e)
                    if attach is not None:
                        ins._wait_ge(sems[attach[0]], attach[1])
                    if inc is not None:
                        ins.then_inc(sems[inc[0]], inc[1])

        if os.environ.get("K_DUMP"):
            for en in self.ENG:
                print("COUNT", self.n_emit, en, sum(len(w) + (1 if f is not None else 0) for f, w, i in streams[en]))
                print("STREAM", self.n_emit, en, [(w, i, f is not None) for f, w, i in streams[en]][-12:])
        with nc.Block() as block:
            @block.tensor
            def _(e):
                run(e, streams["pe"])

            @block.scalar
            def _(e):
                run(e, streams["act"])

            @block.vector
            def _(e):
                run(e, streams["dve"])

            @block.gpsimd
            def _(e):
                run(e, streams["pool"])

            @block.sync
            def _(e):
                run(e, streams["sp"])
        self.streams = {e: [] for e in self.ENG}
        self.n_emit += 1
        self.n_ops = 0
        if self.n_emit > self.stop_after:
            self.mute = True


def build_program(debug=False):
    nc = bass.Bass("TRN2", target_bir_lowering=False)
    dram = {}

    def din(name, shape):
        dram[name] = nc.dram_tensor(name, list(shape), F32, kind="ExternalInput").ap()
        return dram[name]

    def dout(name, shape):
        dram[name] = nc.dram_tensor(name, list(shape), F32, kind="ExternalOutput").ap()
        return dram[name]

    xT_d = din("xT", [8, 128, NT])
    xtok_d = din("xtok", [NT, D])
    wst_d = din("wstream", [96, 128, 1024])
    wba_d = din("wba", [128, 128])
    wo_d = din("wo", [128, 8192])
    pp_d = din("pp", [128, 136])
    rp_d = din("rp", [128, 1040])
    cst_d = din("cst", [128, C_END])
    sca_d = din("sca", [128, 256])
    scq_d = din("scq", [128, 1152])
    sdl_d = din("sdl", [8, 128, 2048])
    y_d = dout("y", [NT, D])
    cap_d = dout("ca_p", [128, 16])
    cas_d = dout("ca_s", [128, 256])
    cqp_d = dout("cq_p", [128, 72])
    cqs_d = dout("cq_s", [128, 1152])
    dlp_d = dout("dl_p", [8, 128, 128])
    dls_d = dout("dl_s", [8, 128, 2048])

    with ExitStack() as top:
        def sb(name, shape, dt=F32, stack=top):
            return stack.enter_context(nc.sbuf_tensor("s_" + name, list(shape), dt))

        sems = {}
        for e in ("pe", "act", "dve", "pool"):
            sems[e] = top.enter_context(nc.semaphore("s_" + e))
        dma_sems = {"sp": [], "pool": []}
        for i in range(8):
            k = "dsp%d" % i
            sems[k] = top.enter_context(nc.semaphore(k))
            dma_sems["sp"].append(k)
        for i in range(4):
            k = "dpl%d" % i
            sems[k] = top.enter_context(nc.semaphore(k))
            dma_sems["pool"].append(k)
        S = Sched(nc, sems, dma_sems)

        banks = [top.enter_context(nc.psum_tensor("bank%d" % i, [128, 512], F32)) for i in range(8)]
        bank_tk = [Tk(px=True) for _ in range(8)]

        uT = sb("uT", [128, 8, NT], BF16)
        A = sb("A", [128, 8, NT], BF16)
        Mb = sb("Mb", [128, 8, NT], BF16)
        NW = 8
        wsl = sb("wsl", [128, NW, 8, 128], BF16)
        cst = sb("cst", [128, C_END], F32)
        pp = sb("pp", [128, 136], F32)
        rpb = sb("rpb", [128, 1040], F32)
        identR = sb("identR", [128, 128], F32R)
        ident2 = sb("ident2", [128, 2, 128], F32)
        identB = sb("identB", [128, 128], BF16)
        onesR = sb("onesR", [128, 128], F32R)
        onesB = sb("onesB", [128, 128], BF16)
        maskPB = sb("maskPB", [128, 256], BF16)
        maskSB = sb("maskSB", [128, 128], BF16)
        halfc = sb("halfc", [128, 4], F32)
        epsc = sb("epsc", [128, 1], F32)
        onwh = sb("onwh", [128, 1], F32)
        wba = sb("wba", [128, 8, 16], BF16)
        stg_cap = sb("stg_cap", [128, 8, 2], F32)
        stg_cas = sb("stg_cas", [128, 8, 16, 2], F32)
        stg_cqp = sb("stg_cqp", [128, 24, 3], F32)
        stg_cqs = sb("stg_cqs", [128, 24, 16, 3], F32)
        NTL = 17
        t_beta = sb("t_beta", [128, NTL, 8], F32)
        t_alpha = sb("t_alpha", [128, NTL, 8], F32)
        t_g = sb("t_g", [128, NTL, 8], F32)
        t_gcgl = sb("t_gcgl", [128, NTL, 16], F32)
        t_negc = sb("t_negc", [128, NTL, 8], F32)
        t_a = sb("t_a", [128, NTL, 8], F32)
        t_nega = sb("t_nega", [128, NTL, 8], F32)
        t_dk = sb("t_dk", [128, NTL, 8], F32)
        t_dl = sb("t_dl", [128, NTL, 8], F32)
        dl_s = sb("dl_s", [128, 8, 16], F32)
        negA = sb("negA", [128, 8], F32)
        k_uT = [[Tk() for _ in range(5)] for _ in range(8)]
        k_A = [[Tk() for _ in range(5)] for _ in range(8)]
        k_Mb = [[Tk() for _ in range(5)] for _ in range(8)]
        k_wsl = [Tk() for _ in range(NW)]
        k_cst, k_pp, k_rpb, k_c2, k_ba, k_stg = Tk(), Tk(), Tk(), Tk(), Tk(), Tk()

        ident = cst[:, C_ID:C_ID + 128]
        nwc = lambda k: pp[:, k:k + 1]
        cawc = lambda c, j: pp[:, 8 + c * 3 + j: 8 + c * 3 + j + 1]
        cbwc = lambda c, j: pp[:, 32 + c * 4 + j: 32 + c * 4 + j + 1]
        onw = pp[:, 128:129]
        fnw_bc = rpb[:, 0:1024]
        alog_bc = rpb[:, 1024:1032]
        dtb_bc = rpb[:, 1032:1040]

        wcount = [0]

        def wload():
            i = wcount[0]
            wcount[0] += 1
            s = i % NW
            S.dma("pool", wsl[:, s, :, :].rearrange("p k n -> p (k n)"), wst_d[i], writes=[k_wsl[s]])
            return s

        pj_rr = [0]

        pj_n = [4]

        def pj_next():
            b = pj_rr[0] % pj_n[0]
            pj_rr[0] += 1
            return b

        def proj(bank, slot, src, src_tk, t0, W, extra_reads=()):
            tt = min(t0 // 512, 4)
            for k in range(8):
                S.op("pe", (lambda e, k=k: e.matmul(banks[bank][:, 0:W], wsl[:, slot, k, :], src[:, k, t0:t0 + W],
                                                    start=(k == 0), stop=(k == 7))),
                     reads=[k_wsl[slot], src_tk[k][tt]], writes=[bank_tk[bank]], inc=(k == 7))

        S.dma("sp", cst[:, :], cst_d[:, :], writes=[k_cst])
        S.dma("sp", pp[:, :], pp_d[:, :], writes=[k_pp])
        S.dma("sp", rpb[:, :], rp_d[:, :], writes=[k_rpb])
        S.dma("pool", wba[:, :, :].rearrange("p k n -> p (k n)"), wba_d[:, :], writes=[k_c2])
        S.op("dve", lambda e: e.tensor_copy(out=identR[:, :], in_=ident), reads=[k_cst], writes=[k_c2])
        S.op("dve", lambda e: e.tensor_copy(out=identB[:, :], in_=ident), reads=[k_cst], writes=[k_c2])
        S.op("dve", lambda e: e.tensor_copy(out=ident2[:, 0, :], in_=ident), reads=[k_cst], writes=[k_c2])
        S.op("dve", lambda e: e.tensor_copy(out=ident2[:, 1, :], in_=ident), reads=[k_cst], writes=[k_c2])
        S.op("dve", lambda e: e.tensor_copy(out=onesR[:, :], in_=cst[:, C_ONES:C_ONES + 128]), reads=[k_cst], writes=[k_c2])
        S.op("dve", lambda e: e.tensor_copy(out=onesB[:, :], in_=cst[:, C_ONES:C_ONES + 128]), reads=[k_cst], writes=[k_c2])
        S.op("dve", lambda e: e.tensor_copy(out=maskPB[:, :], in_=cst[:, C_MP:C_MP + 256]), reads=[k_cst], writes=[k_c2])
        S.op("dve", lambda e: e.tensor_copy(out=maskSB[:, :], in_=cst[:, C_MS:C_MS + 128]), reads=[k_cst], writes=[k_c2])
        S.op("dve", lambda e: e.memset(halfc[:, :], -0.5), writes=[k_c2])
        S.op("dve", lambda e: e.memset(epsc[:, :], EPS), writes=[k_c2])
        S.op("dve", lambda e: e.tensor_scalar(onwh[:, :], onw, 0.5, None, op0=ALU.mult), reads=[k_pp], writes=[k_c2])
        for tl in (t_beta, t_alpha, t_g, t_gcgl):
            S.op("dve", lambda e, tl=tl: e.memset(tl[:, :, :], 0.0), writes=[k_ba])
        S.op("dve", lambda e: e.memset(stg_cqs[:, :, :, :], 0.0), writes=[k_stg])

        with ExitStack() as ph:
            xin = [sb("xin%d" % i, [128, 8, 256], F32, ph) for i in range(2)]
            sq = [sb("sq%d" % i, [128, 8, 256], BF16, ph) for i in range(2)]
            ms = [sb("ms%d" % i, [128, 256], F32, ph) for i in range(2)]
            rs = [sb("rs%d" % i, [128, 256], F32, ph) for i in range(2)]
            k_xin = [Tk(), Tk()]
            k_sq = [Tk(), Tk()]
            k_ms = [Tk(), Tk()]
            k_rs = [Tk(), Tk()]
            subt = [(t0, 256) for t0 in range(0, T, 256)] + [(T, NS)]
            for i, (t0, W) in enumerate(subt):
                b = i % 2
                tt = min(t0 // 512, 4)
                S.dma("sp", xin[b][:, :, 0:W], xT_d[:, :, t0:t0 + W].rearrange("k p t -> p k t"), writes=[k_xin[b]])
                for k in range(8):
                    S.op("act", lambda e, b=b, k=k, W=W: e.activation(out=sq[b][:, k, 0:W], in_=xin[b][:, k, 0:W],
                                                                      func=AF.Square, scale=1.0 / 32.0),
                         reads=[k_xin[b]], writes=[k_sq[b]])
                bk = pj_next()
                for k in range(8):
                    S.op("pe", lambda e, b=b, k=k, W=W, bk=bk: e.matmul(banks[bk][:, 0:W], onesB[:, :], sq[b][:, k, 0:W],
                                                                       start=(k == 0), stop=(k == 7)),
                         reads=[k_sq[b], k_c2], writes=[bank_tk[bk]], inc=(k == 7))
                S.op("act", lambda e, b=b, W=W, bk=bk: e.activation(out=ms[b][:, 0:W], in_=banks[bk][:, 0:W],
                                                                    func=AF.Ln, bias=epsc[:, 0:1], scale=1.0),
                     reads=[bank_tk[bk], k_c2], writes=[k_ms[b]])
                S.op("act", lambda e, b=b, W=W: e.activation(out=rs[b][:, 0:W], in_=ms[b][:, 0:W], func=AF.Exp, scale=-0.5),
                     reads=[k_ms[b]], writes=[k_rs[b]])
                for k in range(8):
                    S.op("dve", lambda e, b=b, k=k, W=W, t0=t0: e.scalar_tensor_tensor(
                        out=uT[:, k, t0:t0 + W], in0=xin[b][:, k, 0:W], scalar=nwc(k), in1=rs[b][:, 0:W],
                        op0=ALU.mult, op1=ALU.mult),
                         reads=[k_xin[b], k_rs[b], k_pp], writes=[k_uT[k][tt]])
            S.barrier()
            S.emit()

        with ExitStack() as ph:
            tmp8 = sb("tmp8", [128, NTL, 8], F32, ph)
            k_t8 = Tk()
            S.op("act", lambda e: e.activation(out=negA[:, :], in_=alog_bc, func=AF.Exp), reads=[k_rpb], writes=[k_ba])
            S.op("dve", lambda e: e.tensor_scalar(negA[:, :], negA[:, :], -1.0, None, op0=ALU.mult), reads=[k_ba], writes=[k_ba])
            for n in range(NTL):
                C = 128 if n < 16 else NS
                t0 = n * 128
                tt = min(t0 // 512, 4)
                bk = pj_next()
                for k in range(8):
                    S.op("pe", lambda e, k=k, C=C, t0=t0, bk=bk: e.matmul(banks[bk][0:C, 0:16], uT[:, k, t0:t0 + C], wba[:, k, :],
                                                                         start=(k == 0), stop=(k == 7)),
                         reads=[k_uT[k][tt], k_c2], writes=[bank_tk[bk]], inc=(k == 7))
                S.op("act", lambda e, n=n, C=C, bk=bk: e.activation(out=t_beta[0:C, n, :], in_=banks[bk][0:C, 0:8], func=AF.Tanh, scale=0.5),
                     reads=[bank_tk[bk]], writes=[k_ba])
                S.op("dve", lambda e, n=n, C=C, bk=bk: e.tensor_tensor(out=t_alpha[0:C, n, :], in0=banks[bk][0:C, 8:16], in1=dtb_bc[0:C, :], op=ALU.add),
                     reads=[bank_tk[bk], k_rpb], writes=[k_ba])
            S.op("dve", lambda e: e.tensor_scalar(t_beta[:, :, :], t_beta[:, :, :], 0.5, 0.5, op0=ALU.mult, op1=ALU.add),
                 reads=[k_ba], writes=[k_ba])
            S.op("act", lambda e: e.activation(out=tmp8[:, :, :], in_=t_alpha[:, :, :], func=AF.Exp), reads=[k_ba], writes=[k_t8])
            S.op("act", lambda e: e.activation(out=tmp8[:, :, :], in_=tmp8[:, :, :], func=AF.Ln, bias=1.0, scale=1.0), reads=[k_t8], writes=[k_t8])
            S.op("dve", lambda e: e.tensor_tensor(out=t_g[:, :, :], in0=tmp8[:, :, :],
                                                  in1=negA[:, :].unsqueeze(1).to_broadcast([128, NTL, 8]), op=ALU.mult),
                 reads=[k_t8, k_ba], writes=[k_ba])
            for n in range(NTL):
                C = 128 if n < 16 else NS
                U = cst[0:C, C_UP:C_UP + 128] if n < 16 else cst[0:C, C_US:C_US + 64]
                ON = cst[0:C, C_ONES:C_ONES + 128] if n < 16 else cst[0:C, C_OS:C_OS + 64]
                bk = pj_next()
                S.op("pe", lambda e, n=n, C=C, U=U, bk=bk: e.matmul(banks[bk][0:C, 0:8], U, t_g[0:C, n, :], start=True, stop=True),
                     reads=[k_ba, k_cst], writes=[bank_tk[bk]], inc=False)
                S.op("pe", lambda e, n=n, C=C, ON=ON, bk=bk: e.matmul(banks[bk][0:C, 8:16], ON, t_g[0:C, n, :], start=True, stop=True),
                     reads=[k_ba, k_cst], writes=[bank_tk[bk]])
                S.op("act", lambda e, n=n, C=C, bk=bk: e.activation(out=t_gcgl[0:C, n, :], in_=banks[bk][0:C, 0:16], func=AF.Copy),
                     reads=[bank_tk[bk]], writes=[k_ba])
            gc_all = t_gcgl[:, :, 0:8]
            gl_all = t_gcgl[:, :, 8:16]
            S.op("dve", lambda e: e.tensor_scalar(t_negc[:, :, :], gc_all, -1.0, None, op0=ALU.mult), reads=[k_ba], writes=[k_ba])
            S.op("act", lambda e: e.activation(out=t_a[:, :, :], in_=gc_all, func=AF.Exp), reads=[k_ba], writes=[k_ba])
            S.op("dve", lambda e: e.tensor_scalar(t_nega[:, :, :], t_a[:, :, :], -1.0, None, op0=ALU.mult), reads=[k_ba], writes=[k_ba])
            S.op("dve", lambda e: e.tensor_tensor(out=t_dk[:, :, :], in0=gl_all, in1=gc_all, op=ALU.subtract), reads=[k_ba], writes=[k_ba])
            S.op("act", lambda e: e.activation(out=t_dk[:, :, :], in_=t_dk[:, :, :], func=AF.Exp), reads=[k_ba], writes=[k_ba])
            S.op("act", lambda e: e.activation(out=t_dl[:, :, :], in_=gl_all, func=AF.Exp), reads=[k_ba], writes=[k_ba])
            for h in range(8):
                bk = pj_next()
                S.op("pe", lambda e, h=h, bk=bk: e.matmul(banks[bk][:, 0:16], t_g[0:NS, 16, h:h + 1].to_broadcast([NS, 128]),
                                                          cst[0:NS, C_BSEL:C_BSEL + 16], start=True, stop=True),
                     reads=[k_ba, k_cst], writes=[bank_tk[bk]])
                S.op("act", lambda e, h=h, bk=bk: e.activation(out=dl_s[:, h, :], in_=banks[bk][:, 0:16], func=AF.Exp),
                     reads=[bank_tk[bk]], writes=[k_ba])
            S.barrier()
            S.emit()

        with ExitStack() as ph:
            Es = sb("extAs", [128, NB, 6], F32, ph)
            E = sb("extA", [128, 2 + T], F32, ph)
            k_E = [Tk() for _ in range(5)]
            tmpc = sb("tmpc", [128, 512], F32, ph)
            acc = sb("acc", [128, 512], F32, ph)
            conv = sb("conv", [128, 512], F32, ph)
            th = sb("th", [128, 512], F32, ph)
            s2 = sb("s2", [128, 512], F32, ph)
            t2 = sb("t2", [128, 512], F32, ph)
            k_tmpc, k_acc, k_conv, k_th, k_s2, k_t2 = Tk(), Tk(), Tk(), Tk(), Tk(), Tk()
            S.op("dve", lambda e: e.memset(E[:, 0:2], 0.0), writes=[k_E[0]])
            slots_next = [wload() for _ in range(4)]
            for c in range(int(os.environ.get("K_P1C", "8"))):
                sC, sH, sZ, sB = slots_next
                if c < 7:
                    slots_next = [wload() for _ in range(4)]
                if not os.environ.get("K_SKIP_SCA"):
                    S.dma("sp", Es[:, :, 0:2], sca_d[:, c * 32:(c + 1) * 32].rearrange("p (b t) -> p b t", b=NB), writes=[k_E[4]])
                for tt, (t0, W) in enumerate(TILES[:int(os.environ.get("K_P1T", "5"))]):
                    smp = tt == 4
                    bC, bH = pj_next(), pj_next()
                    proj(bC, sC, uT, k_uT, t0, W)
                    proj(bH, sH, uT, k_uT, t0, W)
                    S.op("act", lambda e, W=W, bC=bC: e.activation(out=tmpc[:, 0:W], in_=banks[bC][:, 0:W], func=AF.Copy),
                         reads=[bank_tk[bC]], writes=[k_tmpc])
                    if not smp:
                        S.op("dve", lambda e, W=W, bH=bH, t0=t0: e.tensor_tensor(out=E[:, 2 + t0:2 + t0 + W], in0=banks[bH][:, 0:W],
                                                                              in1=tmpc[:, 0:W], op=ALU.mult),
                             reads=[bank_tk[bH], k_tmpc], writes=[k_E[tt]])
                        srcs = [E[:, t0 + j:t0 + j + W] for j in range(3)]
                        o_acc, o_conv = acc[:, 0:W], conv[:, 0:W]
                        rd = [k_E[tt]] + ([k_E[tt - 1]] if tt > 0 else [])
                    else:
                        v3 = lambda ap: ap.rearrange("p (b t) -> p b t", b=NB)
                        S.op("dve", lambda e, bH=bH: e.tensor_tensor(out=Es[:, :, 2:6], in0=v3(banks[bH][:, 0:NS]),
                                                                    in1=v3(tmpc[:, 0:NS]), op=ALU.mult),
                             reads=[bank_tk[bH], k_tmpc], writes=[k_E[4]])
                        srcs = [Es[:, :, j:j + 4] for j in range(3)]
                        o_acc, o_conv = v3(acc[:, 0:NS]), v3(conv[:, 0:NS])
                        rd = [k_E[4]]
                    S.op("dve", lambda e, s=srcs[0], o=o_acc, c=c: e.tensor_scalar(o, s, cawc(c, 0), None, op0=ALU.mult),
                         reads=rd + [k_pp], writes=[k_acc])
                    S.op("dve", lambda e, s=srcs[1], o=o_acc, c=c: e.scalar_tensor_tensor(out=o, in0=s, scalar=cawc(c, 1), in1=o,
                                                                                        op0=ALU.mult, op1=ALU.add),
                         reads=rd + [k_pp, k_acc], writes=[k_acc])
                    S.op("dve", lambda e, s=srcs[2], o=o_acc, oc=o_conv, c=c: e.scalar_tensor_tensor(out=oc, in0=s, scalar=cawc(c, 2), in1=o,
                                                                                                  op0=ALU.mult, op1=ALU.add),
                         reads=rd + [k_pp, k_acc], writes=[k_conv])
                    bZ, bB = pj_next(), pj_next()
                    proj(bZ, sZ, uT, k_uT, t0, W)
                    proj(bB, sB, uT, k_uT, t0, W)
                    S.op("act", lambda e, W=W, bZ=bZ: e.activation(out=th[:, 0:W], in_=banks[bZ][:, 0:W], func=AF.Tanh, scale=0.5),
                         reads=[bank_tk[bZ]], writes=[k_th])
                    S.op("dve", lambda e, W=W, bZ=bZ: e.scalar_tensor_tensor(out=s2[:, 0:W], in0=th[:, 0:W], scalar=1.0, in1=banks[bZ][:, 0:W],
                                                                          op0=ALU.add, op1=ALU.mult),
                         reads=[k_th, bank_tk[bZ]], writes=[k_s2])
                    S.op("dve", lambda e, W=W, bB=bB: e.tensor_tensor(out=t2[:, 0:W], in0=banks[bB][:, 0:W], in1=s2[:, 0:W], op=ALU.mult),
                         reads=[k_s2, bank_tk[bB]], writes=[k_t2])
                    S.op("dve", lambda e, W=W, t0=t0, c=c: e.scalar_tensor_tensor(out=A[:, c, t0:t0 + W], in0=t2[:, 0:W], scalar=0.5,
                                                                               in1=conv[:, 0:W], op0=ALU.mult, op1=ALU.mult),
                         reads=[k_t2, k_conv], writes=[k_A[c][tt]])
                if not os.environ.get("K_NOSTG"):
                    S.op("dve", lambda e, c=c: e.tensor_copy(out=stg_cap[:, c, :], in_=E[:, T:T + 2]), reads=[k_E[3]], writes=[k_stg])
                    if os.environ.get("K_ALT52"):
                        S.op("dve", lambda e, c=c: e.memset(tmpc[:, 0:32], 0.0), writes=[k_tmpc])
                    else:
                        S.op("dve", lambda e, c=c: e.tensor_copy(out=stg_cas[:, c, :, :], in_=Es[:, :, 4:6]), reads=[k_E[4]], writes=[k_stg])

            sg = sb("sg", [128, 512], F32, ph)
            k_sg = Tk()

            def gate_phase(first):
                slots_n = [wload() for _ in range(2)]
                for j in range(8):
                    sG, sO = slots_n
                    if j < 7:
                        slots_n = [wload() for _ in range(2)]
                    for tt, (t0, W) in enumerate(TILES):
                        bG, bY = pj_next(), pj_next()
                        proj(bG, sG, uT, k_uT, t0, W)
                        proj(bY, sO, A, k_A, t0, W)
                        S.op("act", lambda e, W=W, bG=bG: e.activation(out=th[:, 0:W], in_=banks[bG][:, 0:W], func=AF.Tanh, scale=0.5),
                             reads=[bank_tk[bG]], writes=[k_th])
                        S.op("dve", lambda e, W=W: e.tensor_scalar(sg[:, 0:W], th[:, 0:W], 0.5, 0.5, op0=ALU.mult, op1=ALU.add),
                             reads=[k_th], writes=[k_sg])
                        if first:
                            S.op("dve", lambda e, W=W, bY=bY, j=j, t0=t0: e.tensor_tensor(out=Mb[:, j, t0:t0 + W], in0=banks[bY][:, 0:W],
                                                                                     in1=sg[:, 0:W], op=ALU.mult),
                                 reads=[k_sg, bank_tk[bY]], writes=[k_Mb[j][tt]])
                        else:
                            S.op("dve", lambda e, W=W, bY=bY: e.tensor_tensor(out=t2[:, 0:W], in0=banks[bY][:, 0:W], in1=sg[:, 0:W], op=ALU.mult),
                                 reads=[k_sg, bank_tk[bY]], writes=[k_t2])
                            S.op("dve", lambda e, W=W, j=j, t0=t0: e.tensor_tensor(out=Mb[:, j, t0:t0 + W], in0=Mb[:, j, t0:t0 + W],
                                                                                in1=t2[:, 0:W], op=ALU.add),
                                 reads=[k_t2, k_Mb[j][tt]], writes=[k_Mb[j][tt]])

            if not os.environ.get("K_NO1B"):
                gate_phase(True)
            S.barrier()
            S.emit()

        with ExitStack() as ph:
            Eq = [sb("Eq%d" % i, [128, 3 + 512], F32, ph) for i in range(3)]
            Eqs = [sb("Eqs%d" % i, [128, NB, 7], F32, ph) for i in range(3)]
            k_Eq = [Tk() for _ in range(3)]
            qk = [sb("qk%d" % i, [128, 2, 512], F32R, ph) for i in range(2)]
            vv = [sb("vv%d" % i, [128, 512], DLT, ph) for i in range(2)]
            sz = [sb("sz%d" % i, [128, 512], F32, ph) for i in range(2)]
            qk.append(sb("qks", [128, 2, NS], F32R, ph))
            vv.append(sb("vvs", [128, NS], DLT, ph))
            sz.append(sb("szs", [128, NS], F32, ph))
            k_qk = [[Tk(), Tk()], [Tk(), Tk()], [Tk(), Tk()]]
            k_vv = [Tk(), Tk(), Tk()]
            k_sz = [Tk(), Tk(), Tk()]
            cv = sb("cv", [128, 512], F32, ph)
            th2 = sb("th2", [128, 512], F32, ph)
            sq2 = sb("sq2", [128, 512], BF16, ph)
            rn = sb("rn", [128, 512], F32, ph)
            k_cv, k_th2, k_sl, k_sq2, k_rn = Tk(), Tk(), Tk(), Tk(), Tk()
            nrm, k_nrm = th2, k_th2
            def dbl(name, shape, dt=F32):
                return [sb("%s%d" % (name, i), shape, dt, ph) for i in range(2)]
            def tri(name, shape, dt=F32):
                return [sb("%s%d" % (name, i), shape, dt, ph) for i in range(3)]
            kdec = tri("kdec", [128, 128], DLT)
            vtok = tri("vtok", [128, 128], F32)
            Wcat = dbl("Wcat", [128, 256], F32)
            AqkT = tri("AqkT", [128, 128], DLT)
            PPa = dbl("PPa", [128, 2, 128], DBL)
            PPb = dbl("PPb", [128, 2, 128], DBL)
            Ya = tri("Ya", [128, 128], DBL)
            Yb = dbl("Yb", [128, 128], DBL)
            rpt = dbl("rpt", [128, 128], DBL)
            ut = dbl("ut", [128, 128], DLT)
            Aus = dbl("Aus", [128, 128], F32)
            osb = dbl("osb", [128, 128], F32)
            onr = dbl("onr", [128, 128], DLT)
            ssq = dbl("ssq", [128, 2], F32)
            kt = {n: [Tk(), Tk(), Tk()] for n in ("kdec", "vtok", "Wcat", "AqkT", "PPa", "PPb", "Ya", "Yb", "rpt", "ut", "Aus", "osb",
                                            "onr", "ssq")}
            junk = onr
            kt["junk"] = kt["onr"]
            Sm = sb("Sm", [128, 128], F32, ph)
            Sr = sb("Sr", [128, 128], F32R, ph)
            k_Sm, k_Sr = Tk(), Tk()
            Ss = sb("Ss", [128, NB, 128], F32, ph)
            k_Ss = [Tk() for _ in range(NB)]
            kmr = [sb("kmr%d" % i, [128, NS], F32R, ph) for i in range(2)]
            qmr = [sb("qmr%d" % i, [128, NS], F32R, ph) for i in range(2)]
            kdm = [sb("kdm%d" % i, [NS, 128], DLT, ph) for i in range(2)]
            k_kmr, k_qmr, k_kdm = [Tk(), Tk()], [Tk(), Tk()], [Tk(), Tk()]
            Sbr = [sb("Sbr%d" % i, [128, 128], F32R, ph) for i in range(2)]
            k_Sbr = [Tk(), Tk()]
            B_T, B_W, B_D, B_R = 4, 5, 6, 7
            p_Tk = p_Tv = p_G = bank_tk[4]
            p_W = p_Au = p_oT = bank_tk[5]
            p_D = p_dY = p_S = bank_tk[6]
            p_kS = p_Tr = p_qS = bank_tk[7]

            def conv_unit(which, slot, h, tt, t0, W, buf):
                chunk = (8 if which == 0 else (0 if which == 1 else 16)) + h
                smp = tt == 4
                bk = pj_next()
                proj(bk, slot, uT, k_uT, t0, W)
                yield
                Ex = Eq[which]
                if not smp:
                    S.op("act", lambda e: e.activation(out=Ex[:, 3:3 + W], in_=banks[bk][:, 0:W], func=AF.Copy),
                         reads=[bank_tk[bk]], writes=[k_Eq[which]])
                    srcs = [Ex[:, j:j + W] for j in range(4)]
                    o = cv[:, 0:W]
                else:
                    v3 = lambda ap: ap.rearrange("p (b t) -> p b t", b=NB)
                    Exs = Eqs[which]
                    S.dma("sp", Exs[:, :, 0:3], scq_d[:, chunk * 48:(chunk + 1) * 48].rearrange("p (b t) -> p b t", b=NB),
                          writes=[k_Eq[which]])
                    S.op("act", lambda e: e.activation(out=Exs[:, :, 3:7], in_=v3(banks[bk][:, 0:NS]), func=AF.Copy),
                         reads=[bank_tk[bk]], writes=[k_Eq[which]])
                    srcs = [Exs[:, :, j:j + 4] for j in range(4)]
                    o = v3(cv[:, 0:NS])
                yield
                S.op("dve", lambda e: e.tensor_scalar(o, srcs[0], cbwc(chunk, 0), None, op0=ALU.mult),
                     reads=[k_Eq[which], k_pp], writes=[k_cv])
                for j in (1, 2, 3):
                    S.op("dve", lambda e, j=j: e.scalar_tensor_tensor(out=o, in0=srcs[j], scalar=cbwc(chunk, j), in1=o,
                                                                     op0=ALU.mult, op1=ALU.add),
                         reads=[k_Eq[which], k_pp, k_cv], writes=[k_cv])
                    yield
                if not smp:
                    if tt == 3:
                        S.op("dve", lambda e: e.tensor_copy(out=stg_cqp[:, chunk, :], in_=Ex[:, 512:515]),
                             reads=[k_Eq[which]], writes=[k_stg])
                    S.op("dve", lambda e: e.tensor_copy(out=Ex[:, 0:3], in_=Ex[:, 512:515]) if tt < 3 else e.memset(Ex[:, 0:3], 0.0),
                         reads=[k_Eq[which], k_cv], writes=[k_Eq[which]])
                else:
                    S.op("dve", lambda e: e.tensor_copy(out=stg_cqs[:, chunk, :, :], in_=Eqs[which][:, :, 4:7]),
                         reads=[k_Eq[which]], writes=[k_stg])
                S.op("act", lambda e: e.activation(out=th2[:, 0:W], in_=cv[:, 0:W], func=AF.Tanh, scale=0.5), reads=[k_cv], writes=[k_th2])
                yield
                if which == 2:
                    S.op("dve", lambda e: e.scalar_tensor_tensor(out=vv[buf][:, 0:W], in0=th2[:, 0:W], scalar=1.0, in1=cv[:, 0:W],
                                                                 op0=ALU.add, op1=ALU.mult),
                         reads=[k_th2, k_cv], writes=[k_vv[buf]])
                    return
                S.op("dve", lambda e: e.scalar_tensor_tensor(out=qk[buf][:, which, 0:W], in0=th2[:, 0:W], scalar=1.0, in1=cv[:, 0:W],
                                                             op0=ALU.add, op1=ALU.mult),
                     reads=[k_th2, k_cv], writes=[k_qk[buf][which]])

            def norm_unit(which, W, buf):
                S.op("act", lambda e: e.activation(out=sq2[:, 0:W], in_=qk[buf][:, which, 0:W], func=AF.Square),
                     reads=[k_qk[buf][which]], writes=[k_sq2])
                bn = pj_next()
                S.op("pe", lambda e: e.matmul(banks[bn][:, 0:W], onesB[:, :], sq2[:, 0:W], start=True, stop=True),
                     reads=[k_sq2, k_c2], writes=[bank_tk[bn]])
                yield
                S.op("act", lambda e: e.activation(out=nrm[:, 0:W], in_=banks[bn][:, 0:W], func=AF.Ln, bias=eps4[:, 0:1], scale=1.0),
                     reads=[bank_tk[bn], k_c2], writes=[k_nrm])
                S.op("act", lambda e: e.activation(out=rn[:, 0:W], in_=nrm[:, 0:W], func=AF.Exp, scale=-0.5),
                     reads=[k_nrm], writes=[k_rn])
                yield
                if which == 0:
                    S.op("dve", lambda e: e.tensor_tensor(out=qk[buf][:, 0, 0:W], in0=qk[buf][:, 0, 0:W], in1=rn[:, 0:W], op=ALU.mult),
                         reads=[k_qk[buf][0], k_rn], writes=[k_qk[buf][0]])
                else:
                    S.op("dve", lambda e: e.scalar_tensor_tensor(out=qk[buf][:, 1, 0:W], in0=qk[buf][:, 1, 0:W], scalar=128.0 ** -0.5, in1=rn[:, 0:W],
                                                                 op0=ALU.mult, op1=ALU.mult),
                         reads=[k_qk[buf][1], k_rn], writes=[k_qk[buf][1]])

            def zb_unit(slot, tt, t0, W, buf):
                bk = pj_next()
                proj(bk, slot, uT, k_uT, t0, W)
                yield
                S.op("act", lambda e: e.activation(out=th2[:, 0:W], in_=banks[bk][:, 0:W], func=AF.Tanh, scale=0.5),
                     reads=[bank_tk[bk]], writes=[k_th2])
                yield
                S.op("dve", lambda e: e.scalar_tensor_tensor(out=sz[buf][:, 0:W], in0=th2[:, 0:W], scalar=1.0, in1=banks[bk][:, 0:W],
                                                             op0=ALU.add, op1=ALU.mult),
                     reads=[k_th2, bank_tk[bk]], writes=[k_sz[buf]])

            cstR_ones = sb("cstR_ones", [128, 128], F32R, ph)
            eps4 = sb("eps4", [128, 1], F32, ph)
            S.op("dve", lambda e: e.tensor_copy(out=cstR_ones[:, :], in_=cst[:, C_ONES:C_ONES + 128]), reads=[k_cst], writes=[k_c2])
            S.op("dve", lambda e: e.memset(eps4[:, :], 4.0 * EPS), writes=[k_c2])
            for i in range(3):
                S.op("dve", lambda e, i=i: e.memset(Eq[i][:, 0:3], 0.0), writes=[k_Eq[i]])

            pj_n[0] = 2
            bT, bW, bD, bR, bX = banks[4], banks[5], banks[6], banks[7], banks[3]
            kT, kW, kD, kR, kX = bank_tk[4], bank_tk[5], bank_tk[6], bank_tk[7], bank_tk[3]

            class CV:
                pass

            TRI = ("kdec", "vtok", "AqkT", "Ya")

            def cvars(h, n, buf, c0, C, idx, smp):
                v = CV()
                p2, p3 = idx % 2, idx % 3
                v.p2, v.p3 = p2, p3
                col = lambda t: t[0:C, n, h:h + 1]
                v.beta, v.gc, v.negc, v.a_, v.nega, v.dk = (col(t_beta), t_gcgl[0:C, n, h:h + 1], col(t_negc), col(t_a), col(t_nega), col(t_dk))
                v.knT = qk[buf][:, 0, c0:c0 + C]
                v.qnT = qk[buf][:, 1, c0:c0 + C]
                v.vT = vv[buf][:, c0:c0 + C]
                v.K = lambda name: kt[name][p3 if name in TRI else p2]
                v.rq = [k_qk[buf][0], k_qk[buf][1]]
                v.L = 6 if not smp else 1
                v.PP = [(PPa[p2], v.K("PPa")), (PPb[p2], v.K("PPb"))]
                v.YY = [(Ya[p3], v.K("Ya")), (Yb[p2], v.K("Yb"))]
                v.Yf, v.kYf = v.YY[v.L % 2]
                return v

            bY = banks[2]
            kY = bank_tk[2]

            def chunk_front(h, n, buf, c0, C, idx, smp):
                v = cvars(h, n, buf, c0, C, idx, smp)
                K, knT, vT, rq, beta, gc, negc, dk = v.K, v.knT, v.vT, v.rq, v.beta, v.gc, v.negc, v.dk
                p2, p3 = v.p2, v.p3
                S.op("pe", lambda e: e.matmul(bT[0:C, 0:128], knT, identR[:, :], start=True, stop=True), reads=[rq[0], k_c2], writes=[kT])
                S.op("pe", lambda e: e.matmul(bT[0:C, 128:256], vT, (identB if DLT == BF16 else identR)[:, :], start=True, stop=True), reads=[k_vv[buf], k_c2], writes=[kT])
                S.op("pe", lambda e: e.matmul(bT[0:C, 256:256 + 2 * C].rearrange("p (a c) -> p a c", a=2), knT, qk[buf][:, :, c0:c0 + C],
                                              start=True, stop=True), reads=rq, writes=[kT])
                mB = maskPB[0:C, :] if not smp else maskSB[0:C, :]
                S.op("pe", lambda e: e.matmul(bW[0:C, 0:2 * C].rearrange("p (a c) -> p a c", a=2), gc.to_broadcast([C, C]), ident2[0:C, :, 0:C],
                                              start=True, stop=False), reads=[k_ba, k_c2], writes=[kW], inc=False)
                S.op("pe", lambda e: e.matmul(bW[0:C, 0:2 * C], identB[0:C, 0:C], mB, start=False, stop=True), reads=[k_c2], writes=[kW])
                yield
                S.op("act", lambda e: e.activation(out=Wcat[p2][0:C, 0:2 * C], in_=bW[0:C, 0:2 * C], func=AF.Exp, bias=negc, scale=1.0),
                     reads=[kW, k_ba], writes=[K("Wcat")])
                S.op("act", lambda e: e.activation(out=kdec[p3][0:C, :], in_=bT[0:C, 0:128], func=AF.Copy, scale=dk), reads=[kT, k_ba], writes=[K("kdec")])
                S.op("act", lambda e: e.activation(out=vtok[p3][0:C, :], in_=bT[0:C, 128:256], func=AF.Copy, scale=0.5), reads=[kT], writes=[K("vtok")])
                yield
                S.op("dve", lambda e: e.scalar_tensor_tensor(out=PPa[p2][0:C, 1, 0:C], in0=bT[0:C, 256:256 + C], scalar=beta, in1=Wcat[p2][0:C, C:2 * C],
                                                             op0=ALU.mult, op1=ALU.mult), reads=[kT, K("Wcat"), k_ba], writes=[K("PPa")])
                yield
                S.op("dve", lambda e: e.tensor_tensor(out=AqkT[p3][0:C, 0:C], in0=bT[0:C, 256 + C:256 + 2 * C], in1=Wcat[p2][0:C, 0:C], op=ALU.mult),
                     reads=[kT, K("Wcat")], writes=[K("AqkT")])
                S.op("pe", lambda e: e.matmul(bT[0:C, 0:C], PPa[p2][0:C, 1, 0:C], (identB if DBL == BF16 else identR)[0:C, 0:C], start=True, stop=True),
                     reads=[K("PPa"), k_c2], writes=[kT])
                yield
                S.op("act", lambda e: e.activation(out=PPa[p2][0:C, 0, 0:C], in_=bT[0:C, 0:C], func=AF.Copy), reads=[kT], writes=[K("PPa")])
                S.op("dve", lambda e: e.tensor_tensor(out=Ya[p3][0:C, 0:C], in0=ident[0:C, 0:C], in1=PPa[p2][0:C, 1, 0:C], op=ALU.subtract),
                     reads=[K("PPa"), k_cst], writes=[K("Ya")])
                yield

            def chunk_dbl(h, n, buf, c0, C, idx, smp):
                v = cvars(h, n, buf, c0, C, idx, smp)
                L, PP, YY = v.L, v.PP, v.YY
                for k in range(1, L + 1):
                    Pp, kPp = PP[(k - 1) % 2]
                    Pn, kPn = PP[k % 2]
                    S.op("pe", lambda e, Pp=Pp: e.matmul(bD[0:C, 0:C], Pp[0:C, 1, 0:C], Pp[0:C, 0, 0:C], start=True, stop=True),
                         reads=[kPp], writes=[kD])
                    if k < L:
                        S.op("pe", lambda e, Pp=Pp: e.matmul(bD[0:C, 128:128 + C], Pp[0:C, 0, 0:C], Pp[0:C, 1, 0:C], start=True, stop=True),
                             reads=[kPp], writes=[kD])
                    if k >= 2:
                        Yp, kYp = YY[(k - 2) % 2]
                        S.op("pe", lambda e, Pp=Pp, Yp=Yp: e.matmul(bY[0:C, 0:C], Pp[0:C, 0, 0:C], Yp[0:C, 0:C], start=True, stop=True),
                             reads=[kPp, kYp], writes=[kY])
                    yield
                    if k < L:
                        S.op("act", lambda e, Pn=Pn: e.activation(out=Pn[0:C, :, 0:C], in_=bD[0:C, 0:256].rearrange("p (a c) -> p a c", a=2)[:, :, 0:C],
                                                                  func=AF.Copy), reads=[kD], writes=[kPn])
                    else:
                        S.op("act", lambda e, Pn=Pn: e.activation(out=Pn[0:C, 0, 0:C], in_=bD[0:C, 0:C], func=AF.Copy), reads=[kD], writes=[kPn])
                    if k >= 2:
                        Yn, kYn = YY[(k - 1) % 2]
                        S.op("dve", lambda e, Yp=Yp, Yn=Yn: e.tensor_tensor(out=Yn[0:C, 0:C], in0=bY[0:C, 0:C], in1=Yp[0:C, 0:C], op=ALU.add),
                             reads=[kY, kYp], writes=[kYn])
                    yield
                PL, kPL = PP[L % 2]
                Yp, kYp = YY[(L - 1) % 2]
                Yf, kYf = v.Yf, v.kYf
                S.op("pe", lambda e: e.matmul(bY[0:C, 0:C], PL[0:C, 0, 0:C], Yp[0:C, 0:C], start=True, stop=True),
                     reads=[kPL, kYp], writes=[kY])
                yield
                S.op("dve", lambda e: e.tensor_tensor(out=Yf[0:C, 0:C], in0=bY[0:C, 0:C], in1=Yp[0:C, 0:C], op=ALU.add),
                     reads=[kY, kYp], writes=[kYf])
                yield

            def chunk_rec(h, n, buf, c0, C, idx, smp):
                v = cvars(h, n, buf, c0, C, idx, smp)
                par, p3 = v.p2, v.p3
                K, knT, qnT, rq, beta, a_, nega = v.K, v.knT, v.qnT, v.rq, v.beta, v.a_, v.nega
                Yf, kYf = v.Yf, v.kYf
                if not smp:
                    S.op("pe", lambda e: e.matmul(bR[0:C, 0:128], knT, Sr[:, :], start=True, stop=True), reads=[rq[0], k_Sr], writes=[kR])
                    S.op("pe", lambda e: e.matmul(bR[0:C, 256:384], qnT, Sr[:, :], start=True, stop=True), reads=[rq[1], k_Sr], writes=[kR])
                    qS_ap, qS_tk = bR[0:C, 256:384], kR
                else:
                    for b in range(NB):
                        pb = b % 2
                        S.op("act", lambda e, b=b, pb=pb: e.activation(out=Sbr[pb][:, :], in_=Ss[:, b, :], func=AF.Copy), reads=[k_Ss[b]], writes=[k_Sbr[pb]])
                        S.op("dve", lambda e, b=b, pb=pb: e.tensor_tensor(out=kmr[pb][:, :], in0=knT, in1=cst[:, C_BM + b * 64:C_BM + (b + 1) * 64], op=ALU.mult),
                             reads=[rq[0], k_cst], writes=[k_kmr[pb]])
                        S.op("dve", lambda e, b=b, pb=pb: e.tensor_tensor(out=qmr[pb][:, :], in0=qnT, in1=cst[:, C_BM + b * 64:C_BM + (b + 1) * 64], op=ALU.mult),
                             reads=[rq[1], k_cst], writes=[k_qmr[pb]])
                        S.op("pe", lambda e, b=b, pb=pb: e.matmul(bR[0:C, 0:128], kmr[pb][:, :], Sbr[pb][:, :], start=(b == 0), stop=(b == NB - 1)),
                             reads=[k_kmr[pb], k_Sbr[pb]], writes=[kR], inc=True)
                        S.op("pe", lambda e, b=b, pb=pb: e.matmul(bX[0:C, 256:384], qmr[pb][:, :], Sbr[pb][:, :], start=(b == 0), stop=(b == NB - 1)),
                             reads=[k_qmr[pb], k_Sbr[pb]], writes=[kX], inc=True)
                        yield
                    qS_ap, qS_tk = bX[0:C, 256:384], kX
                yield
                S.op("dve", lambda e: e.scalar_tensor_tensor(out=rpt[par][0:C, :], in0=bR[0:C, 0:128], scalar=nega, in1=vtok[p3][0:C, :],
                                                             op0=ALU.mult, op1=ALU.add), reads=[kR, K("vtok"), k_ba], writes=[K("rpt")])
                yield
                S.op("pe", lambda e: e.matmul(bR[0:C, 128:256], Yf[0:C, 0:C], rpt[par][0:C, :], start=True, stop=True), reads=[kYf, K("rpt")], writes=[kR])
                yield
                S.op("act", lambda e: e.activation(out=ut[par][0:C, :], in_=bR[0:C, 128:256], func=AF.Copy, scale=beta), reads=[kR, k_ba], writes=[K("ut")])
                yield
                if not smp:
                    S.op("pe", lambda e: e.matmul(bX[:, 128:256], kdec[p3][0:C, :], ut[par][0:C, :], start=True, stop=True), reads=[K("kdec"), K("ut")], writes=[kX])
                    S.op("pe", lambda e: e.matmul(bR[0:C, 384:512], AqkT[p3][0:C, 0:C], ut[par][0:C, :], start=True, stop=True), reads=[K("AqkT"), K("ut")], writes=[kR])
                    yield
                    S.op("dve", lambda e: e.scalar_tensor_tensor(out=Sr[:, :], in0=Sr[:, :], scalar=t_dl[:, n, h:h + 1], in1=bX[:, 128:256],
                                                                 op0=ALU.mult, op1=ALU.add), reads=[kX, k_Sr, k_ba], writes=[k_Sr])
                    yield
                else:
                    S.op("pe", lambda e: e.matmul(bR[0:C, 384:512], AqkT[p3][0:C, 0:C], ut[par][0:C, :], start=True, stop=True), reads=[K("AqkT"), K("ut")], writes=[kR])
                    yield
                S.op("act", lambda e: e.activation(out=Aus[par][0:C, :], in_=bR[0:C, 384:512], func=AF.Copy), reads=[kR], writes=[K("Aus")])
                yield
                S.op("dve", lambda e: e.scalar_tensor_tensor(out=osb[par][0:C, :], in0=qS_ap, scalar=a_, in1=Aus[par][0:C, :],
                                                             op0=ALU.mult, op1=ALU.add), reads=[qS_tk, K("Aus"), k_ba], writes=[K("osb")])
                yield
                if smp:
                    for b in range(NB):
                        pb = b % 2
                        S.op("dve", lambda e, b=b, pb=pb: e.tensor_scalar(kdm[pb][0:C, :], kdec[p3][0:C, :], cst[0:C, C_BSEL + b:C_BSEL + b + 1], None, op0=ALU.mult),
                             reads=[K("kdec"), k_cst], writes=[k_kdm[pb]])
                        S.op("pe", lambda e, b=b, pb=pb: e.matmul(bX[:, 128:256], kdm[pb][0:C, :], ut[par][0:C, :], start=True, stop=True),
                             reads=[k_kdm[pb], K("ut")], writes=[kX])
                        S.op("dve", lambda e, b=b: e.scalar_tensor_tensor(out=Ss[:, b, :], in0=Ss[:, b, :], scalar=dl_s[:, h, b:b + 1], in1=bX[:, 128:256],
                                                                        op0=ALU.mult, op1=ALU.add), reads=[kX, k_Ss[b], k_ba], writes=[k_Ss[b]])
                        yield

            def chunk_out(h, n, buf, c0, C, idx, smp):
                v = cvars(h, n, buf, c0, C, idx, smp)
                par, p3 = v.p2, v.p3
                K = v.K
                yield
                yield
                S.op("act", lambda e: e.activation(out=junk[par][0:C, :], in_=osb[par][0:C, :], func=AF.Square, scale=128.0 ** -0.5,
                                                   accum_out=ssq[par][0:C, 0:1]), reads=[K("osb")], writes=[K("junk"), K("ssq")])
                yield
                yield
                S.op("pool", lambda e: e.tensor_tensor(out=ssq[par][0:C, 0:1], in0=ssq[par][0:C, 0:1], in1=epsc[0:C, 0:1], op=ALU.add),
                     reads=[K("ssq"), k_c2], writes=[K("ssq")])
                yield
                yield
                S.op("pool", lambda e: e.tensor_tensor(out=ssq[par][0:C, 1:2], in0=ssq[par][0:C, 0:1], in1=halfc[0:C, 0:1], op=ALU.pow),
                     reads=[K("ssq"), k_c2], writes=[K("ssq")])
                yield
                yield
                yield
                S.op("dve", lambda e: e.tensor_scalar(onr[par][0:C, :], osb[par][0:C, :], ssq[par][0:C, 1:2], None, op0=ALU.mult),
                     reads=[K("osb"), K("ssq")], writes=[K("onr")])
                yield
                S.op("pe", lambda e: e.matmul(bX[:, 0:C], onr[par][0:C, :], (identB if DLT == BF16 else identR)[0:C, 0:C], start=True, stop=True), reads=[K("onr"), k_c2], writes=[kX])
                yield
                tcol = (n * 128) if not smp else T
                tt = min(tcol // 512, 4)
                S.op("dve", lambda e: e.scalar_tensor_tensor(out=A[:, h, tcol:tcol + C], in0=bX[:, 0:C], scalar=onwh[:, 0:1], in1=sz[buf][:, c0:c0 + C],
                                                             op0=ALU.mult, op1=ALU.mult), reads=[kX, k_sz[buf], k_c2], writes=[k_A[h][tt]])
                yield

            def prep(h, tt, slots):
                sK, sQ, sV, sZ = slots
                t0, W = TILES[tt]
                buf = tt % 2 if tt < 4 else 2
                yield from conv_unit(0, sK, h, tt, t0, W, buf)
                yield
                yield from conv_unit(1, sQ, h, tt, t0, W, buf)
                yield
                yield from conv_unit(2, sV, h, tt, t0, W, buf)
                yield
                yield from zb_unit(sZ, tt, t0, W, buf)
                yield
                yield from norm_unit(0, W, buf)
                yield
                yield from norm_unit(1, W, buf)
                yield

            def drain(g):
                for _ in g:
                    pass

            def step(g):
                try:
                    next(g)
                    return True
                except StopIteration:
                    return False

            S.op("dve", lambda e: e.memset(Sm[:, :], 0.0), writes=[k_Sm])
            NH = 8
            head_slots = {0: [wload() for _ in range(4)], 1: [wload() for _ in range(4)]}
            allch = []
            for h in range(NH):
                for tt in range(5):
                    ncc = 4 if tt < 4 else 1
                    for cc in range(ncc):
                        smp = tt == 4
                        args = (h, (tt * 4 + cc) if not smp else 16, (tt % 2) if not smp else 2, cc * 128, 128 if not smp else NS, len(allch), smp)
                        allch.append(dict(h=h, tt=tt, cc=cc, args=args, smp=smp, first_tile=(cc == 0), first_head=(tt == 0 and cc == 0),
                                          last_prompt=(tt == 3 and cc == 3)))
            NCH = len(allch)
            g_first = {}
            for g, ch in enumerate(allch):
                if ch["first_tile"]:
                    g_first[(ch["h"], ch["tt"])] = g
            drain(prep(0, 0, head_slots[0]))
            drain(chunk_front(*allch[0]["args"]))
            drain(chunk_dbl(*allch[0]["args"]))
            drain(chunk_front(*allch[1]["args"]))
            bgs = []
            for g, ch in enumerate(allch):
                h, tt, args = ch["h"], ch["tt"], ch["args"]
                while bgs and bgs[0][1] <= g:
                    drain(bgs.pop(0)[0])
                if ch["first_head"]:
                    S.op("act", lambda e: e.activation(out=Sr[:, :], in_=Sm[:, :], func=AF.Copy), reads=[k_Sm], writes=[k_Sr])
                    S.dma("sp", Ss[:, :, :], sdl_d[h].rearrange("k (b v) -> k b v", b=NB), writes=k_Ss)
                active = [chunk_rec(*args)]
                if g > 0:
                    if ch["smp"] or ch["first_tile"]:
                        drain(chunk_out(*allch[g - 1]["args"]))
                    else:
                        active.append(chunk_out(*allch[g - 1]["args"]))
                if g + 1 < NCH:
                    active.append(chunk_dbl(*allch[g + 1]["args"]))
                slow = chunk_front(*allch[g + 2]["args"]) if g + 2 < NCH else None
                if ch["first_tile"]:
                    if tt < 3:
                        bgs.append([prep(h, tt + 1, head_slots[h]), g_first[(h, tt + 1)] - 2])
                    elif tt == 3:
                        bgs.append([prep(h, 4, head_slots[h]), g_first[(h, 4)] - 2])
                        if h + 1 < NH:
                            bgs.append([prep(h + 1, 0, head_slots[h + 1]), g_first[(h + 1, 0)] - 2])
                    elif h + 2 < NH:
                        head_slots[h + 2] = [wload() for _ in range(4)]
                cyc = 0
                while active or slow is not None:
                    active = [x for x in active if step(x)]
                    if slow is not None and (cyc % 3 == 0 or not active) and not step(slow):
                        slow = None
                    if bgs and not step(bgs[0][0]):
                        bgs.pop(0)
                    cyc += 1
                if ch["last_prompt"]:
                    S.dma("sp", dlp_d[h], Sr[:, :].bitcast(F32), reads=[k_Sr])
                if ch["smp"]:
                    S.dma("sp", dls_d[h].rearrange("k (b v) -> k b v", b=NB), Ss[:, :, :], reads=k_Ss)
            for b in bgs:
                drain(b[0])
            drain(chunk_out(*allch[NCH - 1]["args"]))
            pj_n[0] = 4
            S.barrier()
            S.emit()

        with ExitStack() as ph:
            th = sb("thb", [128, 512], F32, ph)
            sg = sb("sgb", [128, 512], F32, ph)
            t2 = sb("t2b", [128, 512], F32, ph)
            k_th, k_sg, k_t2 = Tk(), Tk(), Tk()
            wo = sb("wo", [128, 8, 1024], BF16, ph)
            k_wo = Tk()
            S.dma("pool", wo[:, :, :].rearrange("p k n -> p (k n)"), wo_d[:, :], writes=[k_wo])
            slots_n = [wload() for _ in range(2)]
            for j in range(8):
                sG, sO = slots_n
                if j < 7:
                    slots_n = [wload() for _ in range(2)]
                for tt, (t0, W) in enumerate(TILES):
                    bG, bY = pj_next(), pj_next()
                    proj(bG, sG, uT, k_uT, t0, W)
                    proj(bY, sO, A, k_A, t0, W)
                    S.op("act", lambda e, W=W, bG=bG: e.activation(out=th[:, 0:W], in_=banks[bG][:, 0:W], func=AF.Tanh, scale=0.5),
                         reads=[bank_tk[bG]], writes=[k_th])
                    S.op("dve", lambda e, W=W: e.tensor_scalar(sg[:, 0:W], th[:, 0:W], 0.5, 0.5, op0=ALU.mult, op1=ALU.add),
                         reads=[k_th], writes=[k_sg])
                    S.op("dve", lambda e, W=W, bY=bY: e.tensor_tensor(out=t2[:, 0:W], in0=banks[bY][:, 0:W], in1=sg[:, 0:W], op=ALU.mult),
                         reads=[k_sg, bank_tk[bY]], writes=[k_t2])
                    S.op("dve", lambda e, W=W, j=j, t0=t0: e.tensor_tensor(out=Mb[:, j, t0:t0 + W], in0=Mb[:, j, t0:t0 + W],
                                                                        in1=t2[:, 0:W], op=ALU.add),
                         reads=[k_t2, k_Mb[j][tt]], writes=[k_Mb[j][tt]])
            xt = [sb("xt%d" % i, [128, D], F32, ph) for i in range(2)]
            hb = [sb("hb%d" % i, [128, D], F32, ph) for i in range(2)]
            yb = [sb("yb%d" % i, [128, D], F32, ph) for i in range(2)]
            jk = sb("jk", [128, D], F32, ph)
            s4 = [sb("s4%d" % i, [128, 2], F32, ph) for i in range(2)]
            k_xt, k_hb, k_yb, k_s4, k_jk = [Tk(), Tk()], [Tk(), Tk()], [Tk(), Tk()], [Tk(), Tk()], Tk()
            for n in range(NTL):
                C = 128 if n < 16 else NS
                r0 = n * 128
                tt = min(r0 // 512, 4)
                b = n % 2
                S.dma("sp", xt[b][0:C, :], xtok_d[r0:r0 + C, :], writes=[k_xt[b]])
                for half in range(2):
                    bk = pj_next()
                    for k in range(8):
                        S.op("pe", lambda e, k=k, C=C, r0=r0, bk=bk, half=half: e.matmul(banks[bk][0:C, :], Mb[:, k, r0:r0 + C],
                                                                                        wo[:, k, half * 512:(half + 1) * 512],
                                                                                        start=(k == 0), stop=(k == 7)),
                             reads=[k_Mb[k][tt], k_wo], writes=[bank_tk[bk]], inc=(k == 7))
                    S.op("dve", lambda e, C=C, bk=bk, half=half, b=b: e.tensor_tensor(out=hb[b][0:C, half * 512:(half + 1) * 512], in0=banks[bk][0:C, :],
                                                                                     in1=xt[b][0:C, half * 512:(half + 1) * 512], op=ALU.add),
                         reads=[bank_tk[bk], k_xt[b]], writes=[k_hb[b]])
                S.op("act", lambda e, C=C, b=b: e.activation(out=jk[0:C, :], in_=hb[b][0:C, :], func=AF.Square, scale=1.0 / 32.0,
                                                             accum_out=s4[b][0:C, 0:1]), reads=[k_hb[b]], writes=[k_jk, k_s4[b]])
                S.op("dve", lambda e, C=C, b=b: e.tensor_scalar(s4[b][0:C, 0:1], s4[b][0:C, 0:1], EPS, None, op0=ALU.add), reads=[k_s4[b]], writes=[k_s4[b]])
                S.op("pool", lambda e, C=C, b=b: e.tensor_tensor(out=s4[b][0:C, 1:2], in0=s4[b][0:C, 0:1], in1=halfc[0:C, 0:1], op=ALU.pow),
                     reads=[k_s4[b], k_c2], writes=[k_s4[b]])
                S.op("dve", lambda e, C=C, b=b: e.scalar_tensor_tensor(out=yb[b][0:C, :], in0=hb[b][0:C, :], scalar=s4[b][0:C, 1:2], in1=fnw_bc[0:C, :],
                                                                      op0=ALU.mult, op1=ALU.mult), reads=[k_hb[b], k_s4[b], k_rpb], writes=[k_yb[b]])
                S.dma("sp", y_d[r0:r0 + C, :], yb[b][0:C, :], reads=[k_yb[b]])
            S.dma("sp", cap_d[:, :], stg_cap[:, :, :].rearrange("p a b -> p (a b)"), reads=[k_stg])
            S.dma("sp", cas_d[:, :], stg_cas[:, :, :, :].rearrange("p a b c -> p (a b c)"), reads=[k_stg])
            S.dma("sp", cqp_d[:, :], stg_cqp[:, :, :].rearrange("p a b -> p (a b)"), reads=[k_stg])
            S.dma("sp", cqs_d[:, :], stg_cqs[:, :, :, :].rearrange("p a b c -> p (a b c)"), reads=[k_stg])
            S.barrier()
            S.emit()
    return nc


def _consts():
    c = np.zeros((128, C_END), np.float32)
    i = np.arange(128)
    c[:, C_ID:C_ID + 128] = np.eye(128, dtype=np.float32)
    c[:, C_UP:C_UP + 128] = (i[:, None] <= i[None, :]).astype(np.float32)
    c[:, C_ONES:C_ONES + 128] = 1.0
    incl = i[None, :] >= i[:, None]
    strict = i[None, :] > i[:, None]
    c[:, C_MP:C_MP + 128] = np.where(incl, 0.0, NEG)
    c[:, C_MP + 128:C_MP + 256] = np.where(strict, 0.0, NEG)
    j = np.arange(64)
    same = (j[:, None] // 4) == (j[None, :] // 4)
    c[:64, C_US:C_US + 64] = (same & (j[:, None] <= j[None, :])).astype(np.float32)
    c[:64, C_OS:C_OS + 64] = same.astype(np.float32)
    c[:64, C_MS:C_MS + 64] = np.where(same & (j[None, :] >= j[:, None]), 0.0, NEG)
    c[:64, C_MS + 64:C_MS + 128] = np.where(same & (j[None, :] > j[:, None]), 0.0, NEG)
    c[64:, C_MS:C_MS + 128] = NEG
    bsel = (j[:, None] // 4 == np.arange(16)[None, :]).astype(np.float32)
    c[:64, C_BSEL:C_BSEL + 16] = bsel
    bm = np.tile(bsel.T.reshape(1, 16 * 64), (128, 1))
    c[:, C_BM:C_BM + 1024] = bm
    return c


def _blk(w, c0):
    return np.ascontiguousarray(w[:, c0:c0 + 128].reshape(8, 128, 128).transpose(1, 0, 2)).reshape(128, 1024)


_PROG = {}


def kernel(x_prompt, x_sample, state_conv_a, state_conv_qkv, state_delta, w_in, conv_a_w, conv_b_w, a_log, dt_bias,
           onorm_w, w_out_a, w_out_b, w_o, norm_w, final_norm_w):
    f = np.float32
    w_in0 = np.asarray(w_in[0], f)
    woa = np.asarray(w_out_a[0], f)
    wob = np.asarray(w_out_b[0], f)
    blocks = []
    for c in range(8):
        for off in (O_CA, O_HA, O_ZA, O_BA):
            blocks.append(_blk(w_in0, off + c * 128))
    for j in range(8):
        blocks.append(_blk(w_in0, O_GA + j * 128))
        blocks.append(_blk(woa, j * 128))
    for h in range(8):
        for off in (O_K, O_Q, O_V, O_ZB):
            blocks.append(_blk(w_in0, off + h * 128))
    for j in range(8):
        blocks.append(_blk(w_in0, O_GB + j * 128))
        blocks.append(_blk(wob, j * 128))
    wstream = np.stack(blocks)
    wba = np.ascontiguousarray(w_in0[:, O_BETA:O_BETA + 16].reshape(8, 128, 16).transpose(1, 0, 2)).reshape(128, 128)
    wo = np.ascontiguousarray(np.asarray(w_o[0], f).reshape(8, 128, 1024).transpose(1, 0, 2)).reshape(128, 8192)
    pp = np.zeros((128, 136), f)
    pp[:, 0:8] = np.asarray(norm_w[0], f).reshape(8, 128).T
    pp[:, 8:32] = np.asarray(conv_a_w[0], f).reshape(3, 8, 128).transpose(2, 1, 0).reshape(128, 24)
    pp[:, 32:128] = np.asarray(conv_b_w[0], f).reshape(4, 24, 128).transpose(2, 1, 0).reshape(128, 96)
    pp[:, 128] = np.asarray(onorm_w[0], f)
    rp = np.zeros((128, 1040), f)
    rp[:, 0:1024] = np.asarray(final_norm_w, f)[None, :]
    rp[:, 1024:1032] = np.asarray(a_log[0], f)[None, :]
    rp[:, 1032:1040] = np.asarray(dt_bias[0], f)[None, :]
    cst = _consts()
    in_maps = []
    for i in range(NCORES):
        xs = np.asarray(x_sample[16 * i:16 * i + 16], f).reshape(NS, D)
        x_all = np.concatenate([np.asarray(x_prompt[i], f), xs], axis=0)
        xT = np.ascontiguousarray(x_all.T).reshape(8, 128, NT)
        sca = np.ascontiguousarray(np.asarray(state_conv_a[0, 16 * i:16 * i + 16], f).reshape(16, 2, 8, 128).transpose(3, 2, 0, 1)).reshape(128, 256)
        scq = np.ascontiguousarray(np.asarray(state_conv_qkv[0, 16 * i:16 * i + 16], f).reshape(16, 3, 24, 128).transpose(3, 2, 0, 1)).reshape(128, 1152)
        sdl = np.ascontiguousarray(np.asarray(state_delta[0, 16 * i:16 * i + 16], f).transpose(1, 2, 0, 3)).reshape(8, 128, 2048)
        in_maps.append({"xT": xT, "xtok": x_all, "wstream": wstream, "wba": wba, "wo": wo, "pp": pp, "rp": rp, "cst": cst,
                        "sca": sca, "scq": scq, "sdl": sdl})
    if "nc" not in _PROG:
        _PROG["nc"] = build_program()
    res = run_bass_kernel_spmd(_PROG["nc"], in_maps, core_ids=list(range(NCORES)))
    R = res.results
    y_prompt = np.stack([R[i]["y"][:T] for i in range(NCORES)])
    y_sample = np.concatenate([R[i]["y"][T:].reshape(16, 4, D) for i in range(NCORES)], axis=0)
    ncap = np.stack([R[i]["ca_p"].reshape(128, 8, 2).transpose(2, 1, 0).reshape(2, 1024) for i in range(NCORES)])[None]
    ncqp = np.stack([R[i]["cq_p"].reshape(128, 24, 3).transpose(2, 1, 0).reshape(3, 3072) for i in range(NCORES)])[None]
    ndp = np.stack([R[i]["dl_p"] for i in range(NCORES)])[None]
    ncas = np.concatenate([R[i]["ca_s"].reshape(128, 8, 16, 2).transpose(2, 3, 1, 0).reshape(16, 2, 1024) for i in range(NCORES)], axis=0)[None]
    ncqs = np.concatenate([R[i]["cq_s"].reshape(128, 24, 16, 3).transpose(2, 3, 1, 0).reshape(16, 3, 3072) for i in range(NCORES)], axis=0)[None]
    nds = np.concatenate([R[i]["dl_s"].reshape(8, 128, 16, 128).transpose(2, 0, 1, 3) for i in range(NCORES)], axis=0)[None]
    return (y_prompt.astype(f), y_sample.astype(f), ncap.astype(f), ncqp.astype(f), ndp.astype(f),
            ncas.astype(f), ncqs.astype(f), nds.astype(f))
```

```python
import os
import numpy as np
from contextlib import ExitStack
import concourse.bass as bass
import concourse.mybir as mybir
from concourse.bass_utils import run_bass_kernel_spmd

F32 = mybir.dt.float32
F32R = mybir.dt.float32r
BF16 = mybir.dt.bfloat16
ALU = mybir.AluOpType
AF = mybir.ActivationFunctionType

NCORES = 8
D = 1024
T = 2048
NS = 64
NT = T + NS
NB = 16
EPS = 1e-6
TILES = [(0, 512), (512, 512), (1024, 512), (1536, 512), (2048, 64)]
NEG = -1.0e9
PSUM_EXCL = not os.environ.get("K_NOEXCL")
DLT = BF16 if os.environ.get("K_DLT", "bf16") == "bf16" else F32R
DBL = BF16 if os.environ.get("K_DBL", "f32r") == "bf16" else F32R
ATTACH_WAIT = not os.environ.get("K_NOATTACH")

O_BA, O_CA, O_HA, O_ZA, O_Q, O_K, O_V, O_ZB, O_BETA, O_GA, O_GB = (
    0, 1024, 2048, 3072, 4096, 5120, 6144, 7168, 8192, 8208, 9232)

C_ID, C_UP, C_ONES, C_MP, C_US, C_OS, C_MS, C_BSEL, C_BM = 0, 128, 256, 384, 640, 704, 768, 896, 912
C_END = C_BM + 1024


class Tk:
    __slots__ = ("w", "r", "px")

    def __init__(self, px=False):
        self.w = None
        self.r = {}
        self.px = px


class Sched:
    ENG = ("pe", "act", "dve", "pool", "sp")

    def __init__(self, nc, sems, dma_sems):
        self.nc = nc
        self.sems = sems
        self.streams = {e: [] for e in self.ENG}
        self.cnt = {e: 0 for e in self.ENG}
        self.known = {e: {} for e in self.ENG}
        self.dma_keys = dma_sems
        self.dma_i = {q: 0 for q in dma_sems}
        self.latest = {}
        self.mute = False
        self.n_emit = 0
        self.stop_after = int(os.environ.get("K_STOP", "99"))
        self.n_ops = 0
        self.max_ph = int(os.environ.get("K_MAXPH", "-1"))
        self.max_ops = int(os.environ.get("K_MAXOPS", "0"))

    def _collect(self, eng, reads, writes):
        waits = {}
        kn = self.known[eng]

        def need(tok, is_raw):
            if tok is None:
                return
            key, val, teng = tok
            if teng == eng and eng == "pe":
                return
            if kn.get(key, 0) >= val:
                return
            if waits.get(key, 0) < val:
                waits[key] = val

        for t in reads:
            need(t.w, True)
        for t in writes:
            need(t.w, False)
            for tok in t.r.values():
                need(tok, False)
        for k, v in waits.items():
            kn[k] = v
        return list(waits.items())

    def _limit(self):
        self.n_ops += 1
        if self.n_emit == self.max_ph and self.n_ops > self.max_ops:
            self.mute = True

    def op(self, eng, fn, reads=(), writes=(), inc=True):
        self._limit()
        if self.mute:
            return None
        if PSUM_EXCL and eng != "pe":
            ex = [t for t in reads if t.px]
            if ex:
                reads = [t for t in reads if not t.px]
                writes = list(writes) + ex
        waits = self._collect(eng, reads, writes)
        seq = self.cnt[eng] + 1
        if inc:
            self.cnt[eng] = seq
            self.latest[eng] = seq
        tok = (eng, seq, eng)
        self.streams[eng].append((fn, waits, (eng, 1) if inc else None))
        for t in reads:
            t.r[eng] = tok
        for t in writes:
            t.w = tok
            t.r = {}
        return tok

    def dma(self, q, out_ap, in_ap, reads=(), writes=()):
        self._limit()
        if self.mute:
            return None
        keys = self.dma_keys[q]
        i = self.dma_i[q]
        self.dma_i[q] = i + 1
        R = len(keys)
        key = keys[i % R]
        val = 16 * (i // R + 1)
        waits = dict(self._collect(q, reads, writes))
        if i >= R and self.known[q].get(key, 0) < val - 16:
            waits[key] = max(waits.get(key, 0), val - 16)
            self.known[q][key] = val - 16
        tok = (key, val, "dma")
        self.latest[key] = val
        self.streams[q].append((lambda e: e.dma_start(out=out_ap, in_=in_ap), list(waits.items()), (key, 16)))
        for t in reads:
            t.r[key] = tok
        for t in writes:
            t.w = tok
            t.r = {}
        return tok

    def barrier(self):
        for eng in self.ENG:
            waits = []
            kn = self.known[eng]
            for key, val in self.latest.items():
                if key == eng:
                    continue
                if kn.get(key, 0) < val:
                    kn[key] = val
                    waits.append((key, val))
            if waits:
                self.streams[eng].append((None, waits, None))

    def emit(self):
        nc = self.nc
        streams = self.streams
        sems = self.sems

        def run(e, stream):
            for fn, waits, inc in stream:
                attach = None
                if fn is not None and waits and inc is not None and inc[1] == 1 and ATTACH_WAIT:
                    attach = waits[-1]
                    waits = waits[:-1]
                for key, val in waits:
                    e.wait_ge(sems[key], val)
                if fn is not None:
                    ins = fn(e)
                    if attach is not None:
                        ins._wait_ge(sems[attach[0]], attach[1])
                    if inc is not None:
                        ins.then_inc(sems[inc[0]], inc[1])

        if os.environ.get("K_DUMP"):
            for en in self.ENG:
                print("COUNT", self.n_emit, en, sum(len(w) + (1 if f is not None else 0) for f, w, i in streams[en]))
                print("STREAM", self.n_emit, en, [(w, i, f is not None) for f, w, i in streams[en]][-12:])
        with nc.Block() as block:
            @block.tensor
            def _(e):
                run(e, streams["pe"])

            @block.scalar
            def _(e):
                run(e, streams["act"])

            @block.vector
            def _(e):
                run(e, streams["dve"])

            @block.gpsimd
            def _(e):
                run(e, streams["pool"])

            @block.sync
            def _(e):
                run(e, streams["sp"])
        self.streams = {e: [] for e in self.ENG}
        self.n_emit += 1
        self.n_ops = 0
        if self.n_emit > self.stop_after:
            self.mute = True


def build_program(debug=False):
    nc = bass.Bass("TRN2", target_bir_lowering=False)
    dram = {}

    def din(name, shape):
        dram[name] = nc.dram_tensor(name, list(shape), F32, kind="ExternalInput").ap()
        return dram[name]

    def dout(name, shape):
        dram[name] = nc.dram_tensor(name, list(shape), F32, kind="ExternalOutput").ap()
        return dram[name]

    xT_d = din("xT", [8, 128, NT])
    xtok_d = din("xtok", [NT, D])
    wst_d = din("wstream", [96, 128, 1024])
    wba_d = din("wba", [128, 128])
    wo_d = din("wo", [128, 8192])
    pp_d = din("pp", [128, 136])
    rp_d = din("rp", [128, 1040])
    cst_d = din("cst", [128, C_END])
    sca_d = din("sca", [128, 256])
    scq_d = din("scq", [128, 1152])
    sdl_d = din("sdl", [8, 128, 2048])
    y_d = dout("y", [NT, D])
    cap_d = dout("ca_p", [128, 16])
    cas_d = dout("ca_s", [128, 256])
    cqp_d = dout("cq_p", [128, 72])
    cqs_d = dout("cq_s", [128, 1152])
    dlp_d = dout("dl_p", [8, 128, 128])
    dls_d = dout("dl_s", [8, 128, 2048])

    with ExitStack() as top:
        def sb(name, shape, dt=F32, stack=top):
            return stack.enter_context(nc.sbuf_tensor("s_" + name, list(shape), dt))

        sems = {}
        for e in ("pe", "act", "dve", "pool"):
            sems[e] = top.enter_context(nc.semaphore("s_" + e))
        dma_sems = {"sp": [], "pool": []}
        for i in range(8):
            k = "dsp%d" % i
            sems[k] = top.enter_context(nc.semaphore(k))
            dma_sems["sp"].append(k)
        for i in range(4):
            k = "dpl%d" % i
            sems[k] = top.enter_context(nc.semaphore(k))
            dma_sems["pool"].append(k)
        S = Sched(nc, sems, dma_sems)

        banks = [top.enter_context(nc.psum_tensor("bank%d" % i, [128, 512], F32)) for i in range(8)]
        bank_tk = [Tk(px=True) for _ in range(8)]

        uT = sb("uT", [128, 8, NT], BF16)
        A = sb("A", [128, 8, NT], BF16)
        Mb = sb("Mb", [128, 8, NT], BF16)
        NW = 8
        wsl = sb("wsl", [128, NW, 8, 128], BF16)
        cst = sb("cst", [128, C_END], F32)
        pp = sb("pp", [128, 136], F32)
        rpb = sb("rpb", [128, 1040], F32)
        identR = sb("identR", [128, 128], F32R)
        ident2 = sb("ident2", [128, 2, 128], F32)
        identB = sb("identB", [128, 128], BF16)
        onesR = sb("onesR", [128, 128], F32R)
        onesB = sb("onesB", [128, 128], BF16)
        maskPB = sb("maskPB", [128, 256], BF16)
        maskSB = sb("maskSB", [128, 128], BF16)
        halfc = sb("halfc", [128, 4], F32)
        epsc = sb("epsc", [128, 1], F32)
        onwh = sb("onwh", [128, 1], F32)
        wba = sb("wba", [128, 8, 16], BF16)
        stg_cap = sb("stg_cap", [128, 8, 2], F32)
        stg_cas = sb("stg_cas", [128, 8, 16, 2], F32)
        stg_cqp = sb("stg_cqp", [128, 24, 3], F32)
        stg_cqs = sb("stg_cqs", [128, 24, 16, 3], F32)
        NTL = 17
        t_beta = sb("t_beta", [128, NTL, 8], F32)
        t_alpha = sb("t_alpha", [128, NTL, 8], F32)
        t_g = sb("t_g", [128, NTL, 8], F32)
        t_gcgl = sb("t_gcgl", [128, NTL, 16], F32)
        t_negc = sb("t_negc", [128, NTL, 8], F32)
        t_a = sb("t_a", [128, NTL, 8], F32)
        t_nega = sb("t_nega", [128, NTL, 8], F32)
        t_dk = sb("t_dk", [128, NTL, 8], F32)
        t_dl = sb("t_dl", [128, NTL, 8], F32)
        dl_s = sb("dl_s", [128, 8, 16], F32)
        negA = sb("negA", [128, 8], F32)
        k_uT = [[Tk() for _ in range(5)] for _ in range(8)]
        k_A = [[Tk() for _ in range(5)] for _ in range(8)]
        k_Mb = [[Tk() for _ in range(5)] for _ in range(8)]
        k_wsl = [Tk() for _ in range(NW)]
        k_cst, k_pp, k_rpb, k_c2, k_ba, k_stg = Tk(), Tk(), Tk(), Tk(), Tk(), Tk()

        ident = cst[:, C_ID:C_ID + 128]
        nwc = lambda k: pp[:, k:k + 1]
        cawc = lambda c, j: pp[:, 8 + c * 3 + j: 8 + c * 3 + j + 1]
        cbwc = lambda c, j: pp[:, 32 + c * 4 + j: 32 + c * 4 + j + 1]
        onw = pp[:, 128:129]
        fnw_bc = rpb[:, 0:1024]
        alog_bc = rpb[:, 1024:1032]
        dtb_bc = rpb[:, 1032:1040]

        wcount = [0]

        def wload():
            i = wcount[0]
            wcount[0] += 1
            s = i % NW
            S.dma("pool", wsl[:, s, :, :].rearrange("p k n -> p (k n)"), wst_d[i], writes=[k_wsl[s]])
            return s

        pj_rr = [0]

        pj_n = [4]

        def pj_next():
            b = pj_rr[0] % pj_n[0]
            pj_rr[0] += 1
            return b

        def proj(bank, slot, src, src_tk, t0, W, extra_reads=()):
            tt = min(t0 // 512, 4)
            for k in range(8):
                S.op("pe", (lambda e, k=k: e.matmul(banks[bank][:, 0:W], wsl[:, slot, k, :], src[:, k, t0:t0 + W],
                                                    start=(k == 0), stop=(k == 7))),
                     reads=[k_wsl[slot], src_tk[k][tt]], writes=[bank_tk[bank]], inc=(k == 7))

        S.dma("sp", cst[:, :], cst_d[:, :], writes=[k_cst])
        S.dma("sp", pp[:, :], pp_d[:, :], writes=[k_pp])
        S.dma("sp", rpb[:, :], rp_d[:, :], writes=[k_rpb])
        S.dma("pool", wba[:, :, :].rearrange("p k n -> p (k n)"), wba_d[:, :], writes=[k_c2])
        S.op("dve", lambda e: e.tensor_copy(out=identR[:, :], in_=ident), reads=[k_cst], writes=[k_c2])
        S.op("dve", lambda e: e.tensor_copy(out=identB[:, :], in_=ident), reads=[k_cst], writes=[k_c2])
        S.op("dve", lambda e: e.tensor_copy(out=ident2[:, 0, :], in_=ident), reads=[k_cst], writes=[k_c2])
        S.op("dve", lambda e: e.tensor_copy(out=ident2[:, 1, :], in_=ident), reads=[k_cst], writes=[k_c2])
        S.op("dve", lambda e: e.tensor_copy(out=onesR[:, :], in_=cst[:, C_ONES:C_ONES + 128]), reads=[k_cst], writes=[k_c2])
        S.op("dve", lambda e: e.tensor_copy(out=onesB[:, :], in_=cst[:, C_ONES:C_ONES + 128]), reads=[k_cst], writes=[k_c2])
        S.op("dve", lambda e: e.tensor_copy(out=maskPB[:, :], in_=cst[:, C_MP:C_MP + 256]), reads=[k_cst], writes=[k_c2])
        S.op("dve", lambda e: e.tensor_copy(out=maskSB[:, :], in_=cst[:, C_MS:C_MS + 128]), reads=[k_cst], writes=[k_c2])
        S.op("dve", lambda e: e.memset(halfc[:, :], -0.5), writes=[k_c2])
        S.op("dve", lambda e: e.memset(epsc[:, :], EPS), writes=[k_c2])
        S.op("dve", lambda e: e.tensor_scalar(onwh[:, :], onw, 0.5, None, op0=ALU.mult), reads=[k_pp], writes=[k_c2])
        for tl in (t_beta, t_alpha, t_g, t_gcgl):
            S.op("dve", lambda e, tl=tl: e.memset(tl[:, :, :], 0.0), writes=[k_ba])
        S.op("dve", lambda e: e.memset(stg_cqs[:, :, :, :], 0.0), writes=[k_stg])

        with ExitStack() as ph:
            xin = [sb("xin%d" % i, [128, 8, 256], F32, ph) for i in range(2)]
            sq = [sb("sq%d" % i, [128, 8, 256], BF16, ph) for i in range(2)]
            ms = [sb("ms%d" % i, [128, 256], F32, ph) for i in range(2)]
            rs = [sb("rs%d" % i, [128, 256], F32, ph) for i in range(2)]
            k_xin = [Tk(), Tk()]
            k_sq = [Tk(), Tk()]
            k_ms = [Tk(), Tk()]
            k_rs = [Tk(), Tk()]
            subt = [(t0, 256) for t0 in range(0, T, 256)] + [(T, NS)]
            for i, (t0, W) in enumerate(subt):
                b = i % 2
                tt = min(t0 // 512, 4)
                S.dma("sp", xin[b][:, :, 0:W], xT_d[:, :, t0:t0 + W].rearrange("k p t -> p k t"), writes=[k_xin[b]])
                for k in range(8):
                    S.op("act", lambda e, b=b, k=k, W=W: e.activation(out=sq[b][:, k, 0:W], in_=xin[b][:, k, 0:W],
                                                                      func=AF.Square, scale=1.0 / 32.0),
                         reads=[k_xin[b]], writes=[k_sq[b]])
                bk = pj_next()
                for k in range(8):
                    S.op("pe", lambda e, b=b, k=k, W=W, bk=bk: e.matmul(banks[bk][:, 0:W], onesB[:, :], sq[b][:, k, 0:W],
                                                                       start=(k == 0), stop=(k == 7)),
                         reads=[k_sq[b], k_c2], writes=[bank_tk[bk]], inc=(k == 7))
                S.op("act", lambda e, b=b, W=W, bk=bk: e.activation(out=ms[b][:, 0:W], in_=banks[bk][:, 0:W],
                                                                    func=AF.Ln, bias=epsc[:, 0:1], scale=1.0),
                     reads=[bank_tk[bk], k_c2], writes=[k_ms[b]])
                S.op("act", lambda e, b=b, W=W: e.activation(out=rs[b][:, 0:W], in_=ms[b][:, 0:W], func=AF.Exp, scale=-0.5),
                     reads=[k_ms[b]], writes=[k_rs[b]])
                for k in range(8):
                    S.op("dve", lambda e, b=b, k=k, W=W, t0=t0: e.scalar_tensor_tensor(
                        out=uT[:, k, t0:t0 + W], in0=xin[b][:, k, 0:W], scalar=nwc(k), in1=rs[b][:, 0:W],
                        op0=ALU.mult, op1=ALU.mult),
                         reads=[k_xin[b], k_rs[b], k_pp], writes=[k_uT[k][tt]])
            S.barrier()
            S.emit()

        with ExitStack() as ph:
            tmp8 = sb("tmp8", [128, NTL, 8], F32, ph)
            k_t8 = Tk()
            S.op("act", lambda e: e.activation(out=negA[:, :], in_=alog_bc, func=AF.Exp), reads=[k_rpb], writes=[k_ba])
            S.op("dve", lambda e: e.tensor_scalar(negA[:, :], negA[:, :], -1.0, None, op0=ALU.mult), reads=[k_ba], writes=[k_ba])
            for n in range(NTL):
                C = 128 if n < 16 else NS
                t0 = n * 128
                tt = min(t0 // 512, 4)
                bk = pj_next()
                for k in range(8):
                    S.op("pe", lambda e, k=k, C=C, t0=t0, bk=bk: e.matmul(banks[bk][0:C, 0:16], uT[:, k, t0:t0 + C], wba[:, k, :],
                                                                         start=(k == 0), stop=(k == 7)),
                         reads=[k_uT[k][tt], k_c2], writes=[bank_tk[bk]], inc=(k == 7))
                S.op("act", lambda e, n=n, C=C, bk=bk: e.activation(out=t_beta[0:C, n, :], in_=banks[bk][0:C, 0:8], func=AF.Tanh, scale=0.5),
                     reads=[bank_tk[bk]], writes=[k_ba])
                S.op("dve", lambda e, n=n, C=C, bk=bk: e.tensor_tensor(out=t_alpha[0:C, n, :], in0=banks[bk][0:C, 8:16], in1=dtb_bc[0:C, :], op=ALU.add),
                     reads=[bank_tk[bk], k_rpb], writes=[k_ba])
            S.op("dve", lambda e: e.tensor_scalar(t_beta[:, :, :], t_beta[:, :, :], 0.5, 0.5, op0=ALU.mult, op1=ALU.add),
                 reads=[k_ba], writes=[k_ba])
            S.op("act", lambda e: e.activation(out=tmp8[:, :, :], in_=t_alpha[:, :, :], func=AF.Exp), reads=[k_ba], writes=[k_t8])
            S.op("act", lambda e: e.activation(out=tmp8[:, :, :], in_=tmp8[:, :, :], func=AF.Ln, bias=1.0, scale=1.0), reads=[k_t8], writes=[k_t8])
            S.op("dve", lambda e: e.tensor_tensor(out=t_g[:, :, :], in0=tmp8[:, :, :],
                                                  in1=negA[:, :].unsqueeze(1).to_broadcast([128, NTL, 8]), op=ALU.mult),
                 reads=[k_t8, k_ba], writes=[k_ba])
            for n in range(NTL):
                C = 128 if n < 16 else NS
                U = cst[0:C, C_UP:C_UP + 128] if n < 16 else cst[0:C, C_US:C_US + 64]
                ON = cst[0:C, C_ONES:C_ONES + 128] if n < 16 else cst[0:C, C_OS:C_OS + 64]
                bk = pj_next()
                S.op("pe", lambda e, n=n, C=C, U=U, bk=bk: e.matmul(banks[bk][0:C, 0:8], U, t_g[0:C, n, :], start=True, stop=True),
                     reads=[k_ba, k_cst], writes=[bank_tk[bk]], inc=False)
                S.op("pe", lambda e, n=n, C=C, ON=ON, bk=bk: e.matmul(banks[bk][0:C, 8:16], ON, t_g[0:C, n, :], start=True, stop=True),
                     reads=[k_ba, k_cst], writes=[bank_tk[bk]])
                S.op("act", lambda e, n=n, C=C, bk=bk: e.activation(out=t_gcgl[0:C, n, :], in_=banks[bk][0:C, 0:16], func=AF.Copy),
                     reads=[bank_tk[bk]], writes=[k_ba])
            gc_all = t_gcgl[:, :, 0:8]
            gl_all = t_gcgl[:, :, 8:16]
            S.op("dve", lambda e: e.tensor_scalar(t_negc[:, :, :], gc_all, -1.0, None, op0=ALU.mult), reads=[k_ba], writes=[k_ba])
            S.op("act", lambda e: e.activation(out=t_a[:, :, :], in_=gc_all, func=AF.Exp), reads=[k_ba], writes=[k_ba])
            S.op("dve", lambda e: e.tensor_scalar(t_nega[:, :, :], t_a[:, :, :], -1.0, None, op0=ALU.mult), reads=[k_ba], writes=[k_ba])
            S.op("dve", lambda e: e.tensor_tensor(out=t_dk[:, :, :], in0=gl_all, in1=gc_all, op=ALU.subtract), reads=[k_ba], writes=[k_ba])
            S.op("act", lambda e: e.activation(out=t_dk[:, :, :], in_=t_dk[:, :, :], func=AF.Exp), reads=[k_ba], writes=[k_ba])
            S.op("act", lambda e: e.activation(out=t_dl[:, :, :], in_=gl_all, func=AF.Exp), reads=[k_ba], writes=[k_ba])
            for h in range(8):
                bk = pj_next()
                S.op("pe", lambda e, h=h, bk=bk: e.matmul(banks[bk][:, 0:16], t_g[0:NS, 16, h:h + 1].to_broadcast([NS, 128]),
                                                          cst[0:NS, C_BSEL:C_BSEL + 16], start=True, stop=True),
                     reads=[k_ba, k_cst], writes=[bank_tk[bk]])
                S.op("act", lambda e, h=h, bk=bk: e.activation(out=dl_s[:, h, :], in_=banks[bk][:, 0:16], func=AF.Exp),
                     reads=[bank_tk[bk]], writes=[k_ba])
            S.barrier()
            S.emit()

        with ExitStack() as ph:
            Es = sb("extAs", [128, NB, 6], F32, ph)
            E = sb("extA", [128, 2 + T], F32, ph)
            k_E = [Tk() for _ in range(5)]
            tmpc = sb("tmpc", [128, 512], F32, ph)
            acc = sb("acc", [128, 512], F32, ph)
            conv = sb("conv", [128, 512], F32, ph)
            th = sb("th", [128, 512], F32, ph)
            s2 = sb("s2", [128, 512], F32, ph)
            t2 = sb("t2", [128, 512], F32, ph)
            k_tmpc, k_acc, k_conv, k_th, k_s2, k_t2 = Tk(), Tk(), Tk(), Tk(), Tk(), Tk()
            S.op("dve", lambda e: e.memset(E[:, 0:2], 0.0), writes=[k_E[0]])
            slots_next = [wload() for _ in range(4)]
            for c in range(int(os.environ.get("K_P1C", "8"))):
                sC, sH, sZ, sB = slots_next
                if c < 7:
                    slots_next = [wload() for _ in range(4)]
                if not os.environ.get("K_SKIP_SCA"):
                    S.dma("sp", Es[:, :, 0:2], sca_d[:, c * 32:(c + 1) * 32].rearrange("p (b t) -> p b t", b=NB), writes=[k_E[4]])
                for tt, (t0, W) in enumerate(TILES[:int(os.environ.get("K_P1T", "5"))]):
                    smp = tt == 4
                    bC, bH = pj_next(), pj_next()
                    proj(bC, sC, uT, k_uT, t0, W)
                    proj(bH, sH, uT, k_uT, t0, W)
                    S.op("act", lambda e, W=W, bC=bC: e.activation(out=tmpc[:, 0:W], in_=banks[bC][:, 0:W], func=AF.Copy),
                         reads=[bank_tk[bC]], writes=[k_tmpc])
                    if not smp:
                        S.op("dve", lambda e, W=W, bH=bH, t0=t0: e.tensor_tensor(out=E[:, 2 + t0:2 + t0 + W], in0=banks[bH][:, 0:W],
                                                                              in1=tmpc[:, 0:W], op=ALU.mult),
                             reads=[bank_tk[bH], k_tmpc], writes=[k_E[tt]])
                        srcs = [E[:, t0 + j:t0 + j + W] for j in range(3)]
                        o_acc, o_conv = acc[:, 0:W], conv[:, 0:W]
                        rd = [k_E[tt]] + ([k_E[tt - 1]] if tt > 0 else [])
                    else:
                        v3 = lambda ap: ap.rearrange("p (b t) -> p b t", b=NB)
                        S.op("dve", lambda e, bH=bH: e.tensor_tensor(out=Es[:, :, 2:6], in0=v3(banks[bH][:, 0:NS]),
                                                                    in1=v3(tmpc[:, 0:NS]), op=ALU.mult),
                             reads=[bank_tk[bH], k_tmpc], writes=[k_E[4]])
                        srcs = [Es[:, :, j:j + 4] for j in range(3)]
                        o_acc, o_conv = v3(acc[:, 0:NS]), v3(conv[:, 0:NS])
                        rd = [k_E[4]]
                    S.op("dve", lambda e, s=srcs[0], o=o_acc, c=c: e.tensor_scalar(o, s, cawc(c, 0), None, op0=ALU.mult),
                         reads=rd + [k_pp], writes=[k_acc])
                    S.op("dve", lambda e, s=srcs[1], o=o_acc, c=c: e.scalar_tensor_tensor(out=o, in0=s, scalar=cawc(c, 1), in1=o,
                                                                                        op0=ALU.mult, op1=ALU.add),
                         reads=rd + [k_pp, k_acc], writes=[k_acc])
                    S.op("dve", lambda e, s=srcs[2], o=o_acc, oc=o_conv, c=c: e.scalar_tensor_tensor(out=oc, in0=s, scalar=cawc(c, 2), in1=o,
                                                                                                  op0=ALU.mult, op1=ALU.add),
                         reads=rd + [k_pp, k_acc], writes=[k_conv])
                    bZ, bB = pj_next(), pj_next()
                    proj(bZ, sZ, uT, k_uT, t0, W)
                    proj(bB, sB, uT, k_uT, t0, W)
                    S.op("act", lambda e, W=W, bZ=bZ: e.activation(out=th[:, 0:W], in_=banks[bZ][:, 0:W], func=AF.Tanh, scale=0.5),
                         reads=[bank_tk[bZ]], writes=[k_th])
                    S.op("dve", lambda e, W=W, bZ=bZ: e.scalar_tensor_tensor(out=s2[:, 0:W], in0=th[:, 0:W], scalar=1.0, in1=banks[bZ][:, 0:W],
                                                                          op0=ALU.add, op1=ALU.mult),
                         reads=[k_th, bank_tk[bZ]], writes=[k_s2])
                    S.op("dve", lambda e, W=W, bB=bB: e.tensor_tensor(out=t2[:, 0:W], in0=banks[bB][:, 0:W], in1=s2[:, 0:W], op=ALU.mult),
                         reads=[k_s2, bank_tk[bB]], writes=[k_t2])
                    S.op("dve", lambda e, W=W, t0=t0, c=c: e.scalar_tensor_tensor(out=A[:, c, t0:t0 + W], in0=t2[:, 0:W], scalar=0.5,
                                                                               in1=conv[:, 0:W], op0=ALU.mult, op1=ALU.mult),
                         reads=[k_t2, k_conv], writes=[k_A[c][tt]])
                if not os.environ.get("K_NOSTG"):
                    S.op("dve", lambda e, c=c: e.tensor_copy(out=stg_cap[:, c, :], in_=E[:, T:T + 2]), reads=[k_E[3]], writes=[k_stg])
                    if os.environ.get("K_ALT52"):
                        S.op("dve", lambda e, c=c: e.memset(tmpc[:, 0:32], 0.0), writes=[k_tmpc])
                    else:
                        S.op("dve", lambda e, c=c: e.tensor_copy(out=stg_cas[:, c, :, :], in_=Es[:, :, 4:6]), reads=[k_E[4]], writes=[k_stg])

            sg = sb("sg", [128, 512], F32, ph)
            k_sg = Tk()

            def gate_phase(first):
                slots_n = [wload() for _ in range(2)]
                for j in range(8):
                    sG, sO = slots_n
                    if j < 7:
                        slots_n = [wload() for _ in range(2)]
                    for tt, (t0, W) in enumerate(TILES):
                        bG, bY = pj_next(), pj_next()
                        proj(bG, sG, uT, k_uT, t0, W)
                        proj(bY, sO, A, k_A, t0, W)
                        S.op("act", lambda e, W=W, bG=bG: e.activation(out=th[:, 0:W], in_=banks[bG][:, 0:W], func=AF.Tanh, scale=0.5),
                             reads=[bank_tk[bG]], writes=[k_th])
                        S.op("dve", lambda e, W=W: e.tensor_scalar(sg[:, 0:W], th[:, 0:W], 0.5, 0.5, op0=ALU.mult, op1=ALU.add),
                             reads=[k_th], writes=[k_sg])
                        if first:
                            S.op("dve", lambda e, W=W, bY=bY, j=j, t0=t0: e.tensor_tensor(out=Mb[:, j, t0:t0 + W], in0=banks[bY][:, 0:W],
                                                                                     in1=sg[:, 0:W], op=ALU.mult),
                                 reads=[k_sg, bank_tk[bY]], writes=[k_Mb[j][tt]])
                        else:
                            S.op("dve", lambda e, W=W, bY=bY: e.tensor_tensor(out=t2[:, 0:W], in0=banks[bY][:, 0:W], in1=sg[:, 0:W], op=ALU.mult),
                                 reads=[k_sg, bank_tk[bY]], writes=[k_t2])
                            S.op("dve", lambda e, W=W, j=j, t0=t0: e.tensor_tensor(out=Mb[:, j, t0:t0 + W], in0=Mb[:, j, t0:t0 + W],
                                                                                in1=t2[:, 0:W], op=ALU.add),
                                 reads=[k_t2, k_Mb[j][tt]], writes=[k_Mb[j][tt]])

            if not os.environ.get("K_NO1B"):
                gate_phase(True)
            S.barrier()
            S.emit()

        with ExitStack() as ph:
            Eq = [sb("Eq%d" % i, [128, 3 + 512], F32, ph) for i in range(3)]
            Eqs = [sb("Eqs%d" % i, [128, NB, 7], F32, ph) for i in range(3)]
            k_Eq = [Tk() for _ in range(3)]
            qk = [sb("qk%d" % i, [128, 2, 512], F32R, ph) for i in range(2)]
            vv = [sb("vv%d" % i, [128, 512], DLT, ph) for i in range(2)]
            sz = [sb("sz%d" % i, [128, 512], F32, ph) for i in range(2)]
            qk.append(sb("qks", [128, 2, NS], F32R, ph))
            vv.append(sb("vvs", [128, NS], DLT, ph))
            sz.append(sb("szs", [128, NS], F32, ph))
            k_qk = [[Tk(), Tk()], [Tk(), Tk()], [Tk(), Tk()]]
            k_vv = [Tk(), Tk(), Tk()]
            k_sz = [Tk(), Tk(), Tk()]
            cv = sb("cv", [128, 512], F32, ph)
            th2 = sb("th2", [128, 512], F32, ph)
            sq2 = sb("sq2", [128, 512], BF16, ph)
            rn = sb("rn", [128, 512], F32, ph)
            k_cv, k_th2, k_sl, k_sq2, k_rn = Tk(), Tk(), Tk(), Tk(), Tk()
            nrm, k_nrm = th2, k_th2
            def dbl(name, shape, dt=F32):
                return [sb("%s%d" % (name, i), shape, dt, ph) for i in range(2)]
            def tri(name, shape, dt=F32):
                return [sb("%s%d" % (name, i), shape, dt, ph) for i in range(3)]
            kdec = tri("kdec", [128, 128], DLT)
            vtok = tri("vtok", [128, 128], F32)
            Wcat = dbl("Wcat", [128, 256], F32)
            AqkT = tri("AqkT", [128, 128], DLT)
            PPa = dbl("PPa", [128, 2, 128], DBL)
            PPb = dbl("PPb", [128, 2, 128], DBL)
            Ya = tri("Ya", [128, 128], DBL)
            Yb = dbl("Yb", [128, 128], DBL)
            rpt = dbl("rpt", [128, 128], DBL)
            ut = dbl("ut", [128, 128], DLT)
            Aus = dbl("Aus", [128, 128], F32)
            osb = dbl("osb", [128, 128], F32)
            onr = dbl("onr", [128, 128], DLT)
            ssq = dbl("ssq", [128, 2], F32)
            kt = {n: [Tk(), Tk(), Tk()] for n in ("kdec", "vtok", "Wcat", "AqkT", "PPa", "PPb", "Ya", "Yb", "rpt", "ut", "Aus", "osb",
                                            "onr", "ssq")}
            junk = onr
            kt["junk"] = kt["onr"]
            Sm = sb("Sm", [128, 128], F32, ph)
            Sr = sb("Sr", [128, 128], F32R, ph)
            k_Sm, k_Sr = Tk(), Tk()
            Ss = sb("Ss", [128, NB, 128], F32, ph)
            k_Ss = [Tk() for _ in range(NB)]
            kmr = [sb("kmr%d" % i, [128, NS], F32R, ph) for i in range(2)]
            qmr = [sb("qmr%d" % i, [128, NS], F32R, ph) for i in range(2)]
            kdm = [sb("kdm%d" % i, [NS, 128], DLT, ph) for i in range(2)]
            k_kmr, k_qmr, k_kdm = [Tk(), Tk()], [Tk(), Tk()], [Tk(), Tk()]
            Sbr = [sb("Sbr%d" % i, [128, 128], F32R, ph) for i in range(2)]
            k_Sbr = [Tk(), Tk()]
            B_T, B_W, B_D, B_R = 4, 5, 6, 7
            p_Tk = p_Tv = p_G = bank_tk[4]
            p_W = p_Au = p_oT = bank_tk[5]
            p_D = p_dY = p_S = bank_tk[6]
            p_kS = p_Tr = p_qS = bank_tk[7]

            def conv_unit(which, slot, h, tt, t0, W, buf):
                chunk = (8 if which == 0 else (0 if which == 1 else 16)) + h
                smp = tt == 4
                bk = pj_next()
                proj(bk, slot, uT, k_uT, t0, W)
                yield
                Ex = Eq[which]
                if not smp:
                    S.op("act", lambda e: e.activation(out=Ex[:, 3:3 + W], in_=banks[bk][:, 0:W], func=AF.Copy),
                         reads=[bank_tk[bk]], writes=[k_Eq[which]])
                    srcs = [Ex[:, j:j + W] for j in range(4)]
                    o = cv[:, 0:W]
                else:
                    v3 = lambda ap: ap.rearrange("p (b t) -> p b t", b=NB)
                    Exs = Eqs[which]
                    S.dma("sp", Exs[:, :, 0:3], scq_d[:, chunk * 48:(chunk + 1) * 48].rearrange("p (b t) -> p b t", b=NB),
                          writes=[k_Eq[which]])
                    S.op("act", lambda e: e.activation(out=Exs[:, :, 3:7], in_=v3(banks[bk][:, 0:NS]), func=AF.Copy),
                         reads=[bank_tk[bk]], writes=[k_Eq[which]])
                    srcs = [Exs[:, :, j:j + 4] for j in range(4)]
                    o = v3(cv[:, 0:NS])
                yield
                S.op("dve", lambda e: e.tensor_scalar(o, srcs[0], cbwc(chunk, 0), None, op0=ALU.mult),
                     reads=[k_Eq[which], k_pp], writes=[k_cv])
                for j in (1, 2, 3):
                    S.op("dve", lambda e, j=j: e.scalar_tensor_tensor(out=o, in0=srcs[j], scalar=cbwc(chunk, j), in1=o,
                                                                     op0=ALU.mult, op1=ALU.add),
                         reads=[k_Eq[which], k_pp, k_cv], writes=[k_cv])
                    yield
                if not smp:
                    if tt == 3:
                        S.op("dve", lambda e: e.tensor_copy(out=stg_cqp[:, chunk, :], in_=Ex[:, 512:515]),
                             reads=[k_Eq[which]], writes=[k_stg])
                    S.op("dve", lambda e: e.tensor_copy(out=Ex[:, 0:3], in_=Ex[:, 512:515]) if tt < 3 else e.memset(Ex[:, 0:3], 0.0),
                         reads=[k_Eq[which], k_cv], writes=[k_Eq[which]])
                else:
                    S.op("dve", lambda e: e.tensor_copy(out=stg_cqs[:, chunk, :, :], in_=Eqs[which][:, :, 4:7]),
                         reads=[k_Eq[which]], writes=[k_stg])
                S.op("act", lambda e: e.activation(out=th2[:, 0:W], in_=cv[:, 0:W], func=AF.Tanh, scale=0.5), reads=[k_cv], writes=[k_th2])
                yield
                if which == 2:
                    S.op("dve", lambda e: e.scalar_tensor_tensor(out=vv[buf][:, 0:W], in0=th2[:, 0:W], scalar=1.0, in1=cv[:, 0:W],
                                                                 op0=ALU.add, op1=ALU.mult),
                         reads=[k_th2, k_cv], writes=[k_vv[buf]])
                    return
                S.op("dve", lambda e: e.scalar_tensor_tensor(out=qk[buf][:, which, 0:W], in0=th2[:, 0:W], scalar=1.0, in1=cv[:, 0:W],
                                                             op0=ALU.add, op1=ALU.mult),
                     reads=[k_th2, k_cv], writes=[k_qk[buf][which]])

            def norm_unit(which, W, buf):
                S.op("act", lambda e: e.activation(out=sq2[:, 0:W], in_=qk[buf][:, which, 0:W], func=AF.Square),
                     reads=[k_qk[buf][which]], writes=[k_sq2])
                bn = pj_next()
                S.op("pe", lambda e: e.matmul(banks[bn][:, 0:W], onesB[:, :], sq2[:, 0:W], start=True, stop=True),
                     reads=[k_sq2, k_c2], writes=[bank_tk[bn]])
                yield
                S.op("act", lambda e: e.activation(out=nrm[:, 0:W], in_=banks[bn][:, 0:W], func=AF.Ln, bias=eps4[:, 0:1], scale=1.0),
                     reads=[bank_tk[bn], k_c2], writes=[k_nrm])
                S.op("act", lambda e: e.activation(out=rn[:, 0:W], in_=nrm[:, 0:W], func=AF.Exp, scale=-0.5),
                     reads=[k_nrm], writes=[k_rn])
                yield
                if which == 0:
                    S.op("dve", lambda e: e.tensor_tensor(out=qk[buf][:, 0, 0:W], in0=qk[buf][:, 0, 0:W], in1=rn[:, 0:W], op=ALU.mult),
                         reads=[k_qk[buf][0], k_rn], writes=[k_qk[buf][0]])
                else:
                    S.op("dve", lambda e: e.scalar_tensor_tensor(out=qk[buf][:, 1, 0:W], in0=qk[buf][:, 1, 0:W], scalar=128.0 ** -0.5, in1=rn[:, 0:W],
                                                                 op0=ALU.mult, op1=ALU.mult),
                         reads=[k_qk[buf][1], k_rn], writes=[k_qk[buf][1]])

            def zb_unit(slot, tt, t0, W, buf):
                bk = pj_next()
                proj(bk, slot, uT, k_uT, t0, W)
                yield
                S.op("act", lambda e: e.activation(out=th2[:, 0:W], in_=banks[bk][:, 0:W], func=AF.Tanh, scale=0.5),
                     reads=[bank_tk[bk]], writes=[k_th2])
                yield
                S.op("dve", lambda e: e.scalar_tensor_tensor(out=sz[buf][:, 0:W], in0=th2[:, 0:W], scalar=1.0, in1=banks[bk][:, 0:W],
                                                             op0=ALU.add, op1=ALU.mult),
                     reads=[k_th2, bank_tk[bk]], writes=[k_sz[buf]])

            cstR_ones = sb("cstR_ones", [128, 128], F32R, ph)
            eps4 = sb("eps4", [128, 1], F32, ph)
            S.op("dve", lambda e: e.tensor_copy(out=cstR_ones[:, :], in_=cst[:, C_ONES:C_ONES + 128]), reads=[k_cst], writes=[k_c2])
            S.op("dve", lambda e: e.memset(eps4[:, :], 4.0 * EPS), writes=[k_c2])
            for i in range(3):
                S.op("dve", lambda e, i=i: e.memset(Eq[i][:, 0:3], 0.0), writes=[k_Eq[i]])

            pj_n[0] = 2
            bT, bW, bD, bR, bX = banks[4], banks[5], banks[6], banks[7], banks[3]
            kT, kW, kD, kR, kX = bank_tk[4], bank_tk[5], bank_tk[6], bank_tk[7], bank_tk[3]

            class CV:
                pass

            TRI = ("kdec", "vtok", "AqkT", "Ya")

            def cvars(h, n, buf, c0, C, idx, smp):
                v = CV()
                p2, p3 = idx % 2, idx % 3
                v.p2, v.p3 = p2, p3
                col = lambda t: t[0:C, n, h:h + 1]
                v.beta, v.gc, v.negc, v.a_, v.nega, v.dk = (col(t_beta), t_gcgl[0:C, n, h:h + 1], col(t_negc), col(t_a), col(t_nega), col(t_dk))
                v.knT = qk[buf][:, 0, c0:c0 + C]
                v.qnT = qk[buf][:, 1, c0:c0 + C]
                v.vT = vv[buf][:, c0:c0 + C]
                v.K = lambda name: kt[name][p3 if name in TRI else p2]
                v.rq = [k_qk[buf][0], k_qk[buf][1]]
                v.L = 6 if not smp else 1
                v.PP = [(PPa[p2], v.K("PPa")), (PPb[p2], v.K("PPb"))]
                v.YY = [(Ya[p3], v.K("Ya")), (Yb[p2], v.K("Yb"))]
                v.Yf, v.kYf = v.YY[v.L % 2]
                return v

            bY = banks[2]
            kY = bank_tk[2]

            def chunk_front(h, n, buf, c0, C, idx, smp):
                v = cvars(h, n, buf, c0, C, idx, smp)
                K, knT, vT, rq, beta, gc, negc, dk = v.K, v.knT, v.vT, v.rq, v.beta, v.gc, v.negc, v.dk
                p2, p3 = v.p2, v.p3
                S.op("pe", lambda e: e.matmul(bT[0:C, 0:128], knT, identR[:, :], start=True, stop=True), reads=[rq[0], k_c2], writes=[kT])
                S.op("pe", lambda e: e.matmul(bT[0:C, 128:256], vT, (identB if DLT == BF16 else identR)[:, :], start=True, stop=True), reads=[k_vv[buf], k_c2], writes=[kT])
                S.op("pe", lambda e: e.matmul(bT[0:C, 256:256 + 2 * C].rearrange("p (a c) -> p a c", a=2), knT, qk[buf][:, :, c0:c0 + C],
                                              start=True, stop=True), reads=rq, writes=[kT])
                mB = maskPB[0:C, :] if not smp else maskSB[0:C, :]
                S.op("pe", lambda e: e.matmul(bW[0:C, 0:2 * C].rearrange("p (a c) -> p a c", a=2), gc.to_broadcast([C, C]), ident2[0:C, :, 0:C],
                                              start=True, stop=False), reads=[k_ba, k_c2], writes=[kW], inc=False)
                S.op("pe", lambda e: e.matmul(bW[0:C, 0:2 * C], identB[0:C, 0:C], mB, start=False, stop=True), reads=[k_c2], writes=[kW])
                yield
                S.op("act", lambda e: e.activation(out=Wcat[p2][0:C, 0:2 * C], in_=bW[0:C, 0:2 * C], func=AF.Exp, bias=negc, scale=1.0),
                     reads=[kW, k_ba], writes=[K("Wcat")])
                S.op("act", lambda e: e.activation(out=kdec[p3][0:C, :], in_=bT[0:C, 0:128], func=AF.Copy, scale=dk), reads=[kT, k_ba], writes=[K("kdec")])
                S.op("act", lambda e: e.activation(out=vtok[p3][0:C, :], in_=bT[0:C, 128:256], func=AF.Copy, scale=0.5), reads=[kT], writes=[K("vtok")])
                yield
                S.op("dve", lambda e: e.scalar_tensor_tensor(out=PPa[p2][0:C, 1, 0:C], in0=bT[0:C, 256:256 + C], scalar=beta, in1=Wcat[p2][0:C, C:2 * C],
                                                             op0=ALU.mult, op1=ALU.mult), reads=[kT, K("Wcat"), k_ba], writes=[K("PPa")])
                yield
                S.op("dve", lambda e: e.tensor_tensor(out=AqkT[p3][0:C, 0:C], in0=bT[0:C, 256 + C:256 + 2 * C], in1=Wcat[p2][0:C, 0:C], op=ALU.mult),
                     reads=[kT, K("Wcat")], writes=[K("AqkT")])
                S.op("pe", lambda e: e.matmul(bT[0:C, 0:C], PPa[p2][0:C, 1, 0:C], (identB if DBL == BF16 else identR)[0:C, 0:C], start=True, stop=True),
                     reads=[K("PPa"), k_c2], writes=[kT])
                yield
                S.op("act", lambda e: e.activation(out=PPa[p2][0:C, 0, 0:C], in_=bT[0:C, 0:C], func=AF.Copy), reads=[kT], writes=[K("PPa")])
                S.op("dve", lambda e: e.tensor_tensor(out=Ya[p3][0:C, 0:C], in0=ident[0:C, 0:C], in1=PPa[p2][0:C, 1, 0:C], op=ALU.subtract),
                     reads=[K("PPa"), k_cst], writes=[K("Ya")])
                yield

            def chunk_dbl(h, n, buf, c0, C, idx, smp):
                v = cvars(h, n, buf, c0, C, idx, smp)
                L, PP, YY = v.L, v.PP, v.YY
                for k in range(1, L + 1):
                    Pp, kPp = PP[(k - 1) % 2]
                    Pn, kPn = PP[k % 2]
                    S.op("pe", lambda e, Pp=Pp: e.matmul(bD[0:C, 0:C], Pp[0:C, 1, 0:C], Pp[0:C, 0, 0:C], start=True, stop=True),
                         reads=[kPp], writes=[kD])
                    if k < L:
                        S.op("pe", lambda e, Pp=Pp: e.matmul(bD[0:C, 128:128 + C], Pp[0:C, 0, 0:C], Pp[0:C, 1, 0:C], start=True, stop=True),
                             reads=[kPp], writes=[kD])
                    if k >= 2:
                        Yp, kYp = YY[(k - 2) % 2]
                        S.op("pe", lambda e, Pp=Pp, Yp=Yp: e.matmul(bY[0:C, 0:C], Pp[0:C, 0, 0:C], Yp[0:C, 0:C], start=True, stop=True),
                             reads=[kPp, kYp], writes=[kY])
                    yield
                    if k < L:
                        S.op("act", lambda e, Pn=Pn: e.activation(out=Pn[0:C, :, 0:C], in_=bD[0:C, 0:256].rearrange("p (a c) -> p a c", a=2)[:, :, 0:C],
                                                                  func=AF.Copy), reads=[kD], writes=[kPn])
                    else:
                        S.op("act", lambda e, Pn=Pn: e.activation(out=Pn[0:C, 0, 0:C], in_=bD[0:C, 0:C], func=AF.Copy), reads=[kD], writes=[kPn])
                    if k >= 2:
                        Yn, kYn = YY[(k - 1) % 2]
                        S.op("dve", lambda e, Yp=Yp, Yn=Yn: e.tensor_tensor(out=Yn[0:C, 0:C], in0=bY[0:C, 0:C], in1=Yp[0:C, 0:C], op=ALU.add),
                             reads=[kY, kYp], writes=[kYn])
                    yield
                PL, kPL = PP[L % 2]
                Yp, kYp = YY[(L - 1) % 2]
                Yf, kYf = v.Yf, v.kYf
                S.op("pe", lambda e: e.matmul(bY[0:C, 0:C], PL[0:C, 0, 0:C], Yp[0:C, 0:C], start=True, stop=True),
                     reads=[kPL, kYp], writes=[kY])
                yield
                S.op("dve", lambda e: e.tensor_tensor(out=Yf[0:C, 0:C], in0=bY[0:C, 0:C], in1=Yp[0:C, 0:C], op=ALU.add),
                     reads=[kY, kYp], writes=[kYf])
                yield

            def chunk_rec(h, n, buf, c0, C, idx, smp):
                v = cvars(h, n, buf, c0, C, idx, smp)
                par, p3 = v.p2, v.p3
                K, knT, qnT, rq, beta, a_, nega = v.K, v.knT, v.qnT, v.rq, v.beta, v.a_, v.nega
                Yf, kYf = v.Yf, v.kYf
                if not smp:
                    S.op("pe", lambda e: e.matmul(bR[0:C, 0:128], knT, Sr[:, :], start=True, stop=True), reads=[rq[0], k_Sr], writes=[kR])
                    S.op("pe", lambda e: e.matmul(bR[0:C, 256:384], qnT, Sr[:, :], start=True, stop=True), reads=[rq[1], k_Sr], writes=[kR])
                    qS_ap, qS_tk = bR[0:C, 256:384], kR
                else:
                    for b in range(NB):
                        pb = b % 2
                        S.op("act", lambda e, b=b, pb=pb: e.activation(out=Sbr[pb][:, :], in_=Ss[:, b, :], func=AF.Copy), reads=[k_Ss[b]], writes=[k_Sbr[pb]])
                        S.op("dve", lambda e, b=b, pb=pb: e.tensor_tensor(out=kmr[pb][:, :], in0=knT, in1=cst[:, C_BM + b * 64:C_BM + (b + 1) * 64], op=ALU.mult),
                             reads=[rq[0], k_cst], writes=[k_kmr[pb]])
                        S.op("dve", lambda e, b=b, pb=pb: e.tensor_tensor(out=qmr[pb][:, :], in0=qnT, in1=cst[:, C_BM + b * 64:C_BM + (b + 1) * 64], op=ALU.mult),
                             reads=[rq[1], k_cst], writes=[k_qmr[pb]])
                        S.op("pe", lambda e, b=b, pb=pb: e.matmul(bR[0:C, 0:128], kmr[pb][:, :], Sbr[pb][:, :], start=(b == 0), stop=(b == NB - 1)),
                             reads=[k_kmr[pb], k_Sbr[pb]], writes=[kR], inc=True)
                        S.op("pe", lambda e, b=b, pb=pb: e.matmul(bX[0:C, 256:384], qmr[pb][:, :], Sbr[pb][:, :], start=(b == 0), stop=(b == NB - 1)),
                             reads=[k_qmr[pb], k_Sbr[pb]], writes=[kX], inc=True)
                        yield
                    qS_ap, qS_tk = bX[0:C, 256:384], kX
                yield
                S.op("dve", lambda e: e.scalar_tensor_tensor(out=rpt[par][0:C, :], in0=bR[0:C, 0:128], scalar=nega, in1=vtok[p3][0:C, :],
                                                             op0=ALU.mult, op1=ALU.add), reads=[kR, K("vtok"), k_ba], writes=[K("rpt")])
                yield
                S.op("pe", lambda e: e.matmul(bR[0:C, 128:256], Yf[0:C, 0:C], rpt[par][0:C, :], start=True, stop=True), reads=[kYf, K("rpt")], writes=[kR])
                yield
                S.op("act", lambda e: e.activation(out=ut[par][0:C, :], in_=bR[0:C, 128:256], func=AF.Copy, scale=beta), reads=[kR, k_ba], writes=[K("ut")])
                yield
                if not smp:
                    S.op("pe", lambda e: e.matmul(bX[:, 128:256], kdec[p3][0:C, :], ut[par][0:C, :], start=True, stop=True), reads=[K("kdec"), K("ut")], writes=[kX])
                    S.op("pe", lambda e: e.matmul(bR[0:C, 384:512], AqkT[p3][0:C, 0:C], ut[par][0:C, :], start=True, stop=True), reads=[K("AqkT"), K("ut")], writes=[kR])
                    yield
                    S.op("dve", lambda e: e.scalar_tensor_tensor(out=Sr[:, :], in0=Sr[:, :], scalar=t_dl[:, n, h:h + 1], in1=bX[:, 128:256],
                                                                 op0=ALU.mult, op1=ALU.add), reads=[kX, k_Sr, k_ba], writes=[k_Sr])
                    yield
                else:
                    S.op("pe", lambda e: e.matmul(bR[0:C, 384:512], AqkT[p3][0:C, 0:C], ut[par][0:C, :], start=True, stop=True), reads=[K("AqkT"), K("ut")], writes=[kR])
                    yield
                S.op("act", lambda e: e.activation(out=Aus[par][0:C, :], in_=bR[0:C, 384:512], func=AF.Copy), reads=[kR], writes=[K("Aus")])
                yield
                S.op("dve", lambda e: e.scalar_tensor_tensor(out=osb[par][0:C, :], in0=qS_ap, scalar=a_, in1=Aus[par][0:C, :],
                                                             op0=ALU.mult, op1=ALU.add), reads=[qS_tk, K("Aus"), k_ba], writes=[K("osb")])
                yield
                if smp:
                    for b in range(NB):
                        pb = b % 2
                        S.op("dve", lambda e, b=b, pb=pb: e.tensor_scalar(kdm[pb][0:C, :], kdec[p3][0:C, :], cst[0:C, C_BSEL + b:C_BSEL + b + 1], None, op0=ALU.mult),
                             reads=[K("kdec"), k_cst], writes=[k_kdm[pb]])
                        S.op("pe", lambda e, b=b, pb=pb: e.matmul(bX[:, 128:256], kdm[pb][0:C, :], ut[par][0:C, :], start=True, stop=True),
                             reads=[k_kdm[pb], K("ut")], writes=[kX])
                        S.op("dve", lambda e, b=b: e.scalar_tensor_tensor(out=Ss[:, b, :], in0=Ss[:, b, :], scalar=dl_s[:, h, b:b + 1], in1=bX[:, 128:256],
                                                                        op0=ALU.mult, op1=ALU.add), reads=[kX, k_Ss[b], k_ba], writes=[k_Ss[b]])
                        yield

            def chunk_out(h, n, buf, c0, C, idx, smp):
                v = cvars(h, n, buf, c0, C, idx, smp)
                par, p3 = v.p2, v.p3
                K = v.K
                yield
                yield
                S.op("act", lambda e: e.activation(out=junk[par][0:C, :], in_=osb[par][0:C, :], func=AF.Square, scale=128.0 ** -0.5,
                                                   accum_out=ssq[par][0:C, 0:1]), reads=[K("osb")], writes=[K("junk"), K("ssq")])
                yield
                yield
                S.op("dve", lambda e: e.tensor_scalar(ssq[par][0:C, 0:1], ssq[par][0:C, 0:1], EPS, None, op0=ALU.add), reads=[K("ssq")], writes=[K("ssq")])
                yield
                yield
                S.op("pool", lambda e: e.tensor_tensor(out=ssq[par][0:C, 1:2], in0=ssq[par][0:C, 0:1], in1=halfc[0:C, 0:1], op=ALU.pow),
                     reads=[K("ssq"), k_c2], writes=[K("ssq")])
                yield
                yield
                yield
                S.op("dve", lambda e: e.tensor_scalar(onr[par][0:C, :], osb[par][0:C, :], ssq[par][0:C, 1:2], None, op0=ALU.mult),
                     reads=[K("osb"), K("ssq")], writes=[K("onr")])
                yield
                S.op("pe", lambda e: e.matmul(bX[:, 0:C], onr[par][0:C, :], (identB if DLT == BF16 else identR)[0:C, 0:C], start=True, stop=True), reads=[K("onr"), k_c2], writes=[kX])
                yield
                tcol = (n * 128) if not smp else T
                tt = min(tcol // 512, 4)
                S.op("dve", lambda e: e.scalar_tensor_tensor(out=A[:, h, tcol:tcol + C], in0=bX[:, 0:C], scalar=onwh[:, 0:1], in1=sz[buf][:, c0:c0 + C],
                                                             op0=ALU.mult, op1=ALU.mult), reads=[kX, k_sz[buf], k_c2], writes=[k_A[h][tt]])
                yield

            def prep(h, tt, slots):
                sK, sQ, sV, sZ = slots
                t0, W = TILES[tt]
                buf = tt % 2 if tt < 4 else 2
                yield from conv_unit(0, sK, h, tt, t0, W, buf)
                yield
                yield from conv_unit(1, sQ, h, tt, t0, W, buf)
                yield
                yield from conv_unit(2, sV, h, tt, t0, W, buf)
                yield
                yield from zb_unit(sZ, tt, t0, W, buf)
                yield
                yield from norm_unit(0, W, buf)
                yield
                yield from norm_unit(1, W, buf)
                yield

            def drain(g):
                for _ in g:
                    pass

            def step(g):
                try:
                    next(g)
                    return True
                except StopIteration:
                    return False

            S.op("dve", lambda e: e.memset(Sm[:, :], 0.0), writes=[k_Sm])
            NH = 8
            head_slots = {0: [wload() for _ in range(4)], 1: [wload() for _ in range(4)]}
            allch = []
            for h in range(NH):
                for tt in range(5):
                    ncc = 4 if tt < 4 else 1
                    for cc in range(ncc):
                        smp = tt == 4
                        args = (h, (tt * 4 + cc) if not smp else 16, (tt % 2) if not smp else 2, cc * 128, 128 if not smp else NS, len(allch), smp)
                        allch.append(dict(h=h, tt=tt, cc=cc, args=args, smp=smp, first_tile=(cc == 0), first_head=(tt == 0 and cc == 0),
                                          last_prompt=(tt == 3 and cc == 3)))
            NCH = len(allch)
            g_first = {}
            for g, ch in enumerate(allch):
                if ch["first_tile"]:
                    g_first[(ch["h"], ch["tt"])] = g
            drain(prep(0, 0, head_slots[0]))
            drain(chunk_front(*allch[0]["args"]))
            drain(chunk_dbl(*allch[0]["args"]))
            drain(chunk_front(*allch[1]["args"]))
            bgs = []
            for g, ch in enumerate(allch):
                h, tt, args = ch["h"], ch["tt"], ch["args"]
                while bgs and bgs[0][1] <= g:
                    drain(bgs.pop(0)[0])
                if ch["first_head"]:
                    S.op("act", lambda e: e.activation(out=Sr[:, :], in_=Sm[:, :], func=AF.Copy), reads=[k_Sm], writes=[k_Sr])
                    S.dma("sp", Ss[:, :, :], sdl_d[h].rearrange("k (b v) -> k b v", b=NB), writes=k_Ss)
                active = [chunk_rec(*args)]
                if g > 0:
                    if ch["smp"] or ch["first_tile"]:
                        drain(chunk_out(*allch[g - 1]["args"]))
                    else:
                        active.append(chunk_out(*allch[g - 1]["args"]))
                if g + 1 < NCH:
                    active.append(chunk_dbl(*allch[g + 1]["args"]))
                slow = chunk_front(*allch[g + 2]["args"]) if g + 2 < NCH else None
                if ch["first_tile"]:
                    if tt < 3:
                        bgs.append([prep(h, tt + 1, head_slots[h]), g_first[(h, tt + 1)] - 2])
                    elif tt == 3:
                        bgs.append([prep(h, 4, head_slots[h]), g_first[(h, 4)] - 2])
                        if h + 1 < NH:
                            bgs.append([prep(h + 1, 0, head_slots[h + 1]), g_first[(h + 1, 0)] - 2])
                    elif h + 2 < NH:
                        head_slots[h + 2] = [wload() for _ in range(4)]
                cyc = 0
                while active or slow is not None:
                    active = [x for x in active if step(x)]
                    if slow is not None and (cyc % 3 == 0 or not active) and not step(slow):
                        slow = None
                    if bgs and not step(bgs[0][0]):
                        bgs.pop(0)
                    cyc += 1
                if ch["last_prompt"]:
                    S.dma("sp", dlp_d[h], Sr[:, :].bitcast(F32), reads=[k_Sr])
                if ch["smp"]:
                    S.dma("sp", dls_d[h].rearrange("k (b v) -> k b v", b=NB), Ss[:, :, :], reads=k_Ss)
            for b in bgs:
                drain(b[0])
            drain(chunk_out(*allch[NCH - 1]["args"]))
            pj_n[0] = 4
            S.barrier()
            S.emit()

        with ExitStack() as ph:
            th = sb("thb", [128, 512], F32, ph)
            sg = sb("sgb", [128, 512], F32, ph)
            t2 = sb("t2b", [128, 512], F32, ph)
            k_th, k_sg, k_t2 = Tk(), Tk(), Tk()
            wo = sb("wo", [128, 8, 1024], BF16, ph)
            k_wo = Tk()
            S.dma("pool", wo[:, :, :].rearrange("p k n -> p (k n)"), wo_d[:, :], writes=[k_wo])
            slots_n = [wload() for _ in range(2)]
            for j in range(8):
                sG, sO = slots_n
                if j < 7:
                    slots_n = [wload() for _ in range(2)]
                for tt, (t0, W) in enumerate(TILES):
                    bG, bY = pj_next(), pj_next()
                    proj(bG, sG, uT, k_uT, t0, W)
                    proj(bY, sO, A, k_A, t0, W)
                    S.op("act", lambda e, W=W, bG=bG: e.activation(out=th[:, 0:W], in_=banks[bG][:, 0:W], func=AF.Tanh, scale=0.5),
                         reads=[bank_tk[bG]], writes=[k_th])
                    S.op("dve", lambda e, W=W: e.tensor_scalar(sg[:, 0:W], th[:, 0:W], 0.5, 0.5, op0=ALU.mult, op1=ALU.add),
                         reads=[k_th], writes=[k_sg])
                    S.op("dve", lambda e, W=W, bY=bY: e.tensor_tensor(out=t2[:, 0:W], in0=banks[bY][:, 0:W], in1=sg[:, 0:W], op=ALU.mult),
                         reads=[k_sg, bank_tk[bY]], writes=[k_t2])
                    S.op("dve", lambda e, W=W, j=j, t0=t0: e.tensor_tensor(out=Mb[:, j, t0:t0 + W], in0=Mb[:, j, t0:t0 + W],
                                                                        in1=t2[:, 0:W], op=ALU.add),
                         reads=[k_t2, k_Mb[j][tt]], writes=[k_Mb[j][tt]])
            xt = [sb("xt%d" % i, [128, D], F32, ph) for i in range(2)]
            hb = [sb("hb%d" % i, [128, D], F32, ph) for i in range(2)]
            yb = [sb("yb%d" % i, [128, D], F32, ph) for i in range(2)]
            jk = sb("jk", [128, D], F32, ph)
            s4 = [sb("s4%d" % i, [128, 2], F32, ph) for i in range(2)]
            k_xt, k_hb, k_yb, k_s4, k_jk = [Tk(), Tk()], [Tk(), Tk()], [Tk(), Tk()], [Tk(), Tk()], Tk()
            S.dma("sp", xt[0][0:128, :], xtok_d[0:128, :], writes=[k_xt[0]])
            for n in range(NTL):
                C = 128 if n < 16 else NS
                r0 = n * 128
                tt = min(r0 // 512, 4)
                b = n % 2
                if n + 1 < NTL:
                    Cn = 128 if n + 1 < 16 else NS
                    S.dma("sp", xt[1 - b][0:Cn, :], xtok_d[r0 + 128:r0 + 128 + Cn, :], writes=[k_xt[1 - b]])
                for half in range(2):
                    bk = pj_next()
                    for k in range(8):
                        S.op("pe", lambda e, k=k, C=C, r0=r0, bk=bk, half=half: e.matmul(banks[bk][0:C, :], Mb[:, k, r0:r0 + C],
                                                                                        wo[:, k, half * 512:(half + 1) * 512],
                                                                                        start=(k == 0), stop=(k == 7)),
                             reads=[k_Mb[k][tt], k_wo], writes=[bank_tk[bk]], inc=(k == 7))
                    S.op("dve", lambda e, C=C, bk=bk, half=half, b=b: e.tensor_tensor(out=hb[b][0:C, half * 512:(half + 1) * 512], in0=banks[bk][0:C, :],
                                                                                     in1=xt[b][0:C, half * 512:(half + 1) * 512], op=ALU.add),
                         reads=[bank_tk[bk], k_xt[b]], writes=[k_hb[b]])
                S.op("act", lambda e, C=C, b=b: e.activation(out=jk[0:C, :], in_=hb[b][0:C, :], func=AF.Square, scale=1.0 / 32.0,
                                                             accum_out=s4[b][0:C, 0:1]), reads=[k_hb[b]], writes=[k_jk, k_s4[b]])
                S.op("dve", lambda e, C=C, b=b: e.tensor_scalar(s4[b][0:C, 0:1], s4[b][0:C, 0:1], EPS, None, op0=ALU.add), reads=[k_s4[b]], writes=[k_s4[b]])
                S.op("pool", lambda e, C=C, b=b: e.tensor_tensor(out=s4[b][0:C, 1:2], in0=s4[b][0:C, 0:1], in1=halfc[0:C, 0:1], op=ALU.pow),
                     reads=[k_s4[b], k_c2], writes=[k_s4[b]])
                S.op("dve", lambda e, C=C, b=b: e.scalar_tensor_tensor(out=yb[b][0:C, :], in0=hb[b][0:C, :], scalar=s4[b][0:C, 1:2], in1=fnw_bc[0:C, :],
                                                                      op0=ALU.mult, op1=ALU.mult), reads=[k_hb[b], k_s4[b], k_rpb], writes=[k_yb[b]])
                S.dma("sp", y_d[r0:r0 + C, :], yb[b][0:C, :], reads=[k_yb[b]])
            S.dma("sp", cap_d[:, :], stg_cap[:, :, :].rearrange("p a b -> p (a b)"), reads=[k_stg])
            S.dma("sp", cas_d[:, :], stg_cas[:, :, :, :].rearrange("p a b c -> p (a b c)"), reads=[k_stg])
            S.dma("sp", cqp_d[:, :], stg_cqp[:, :, :].rearrange("p a b -> p (a b)"), reads=[k_stg])
            S.dma("sp", cqs_d[:, :], stg_cqs[:, :, :, :].rearrange("p a b c -> p (a b c)"), reads=[k_stg])
            S.barrier()
            S.emit()
    return nc


def _consts():
    c = np.zeros((128, C_END), np.float32)
    i = np.arange(128)
    c[:, C_ID:C_ID + 128] = np.eye(128, dtype=np.float32)
    c[:, C_UP:C_UP + 128] = (i[:, None] <= i[None, :]).astype(np.float32)
    c[:, C_ONES:C_ONES + 128] = 1.0
    incl = i[None, :] >= i[:, None]
    strict = i[None, :] > i[:, None]
    c[:, C_MP:C_MP + 128] = np.where(incl, 0.0, NEG)
    c[:, C_MP + 128:C_MP + 256] = np.where(strict, 0.0, NEG)
    j = np.arange(64)
    same = (j[:, None] // 4) == (j[None, :] // 4)
    c[:64, C_US:C_US + 64] = (same & (j[:, None] <= j[None, :])).astype(np.float32)
    c[:64, C_OS:C_OS + 64] = same.astype(np.float32)
    c[:64, C_MS:C_MS + 64] = np.where(same & (j[None, :] >= j[:, None]), 0.0, NEG)
    c[:64, C_MS + 64:C_MS + 128] = np.where(same & (j[None, :] > j[:, None]), 0.0, NEG)
    c[64:, C_MS:C_MS + 128] = NEG
    bsel = (j[:, None] // 4 == np.arange(16)[None, :]).astype(np.float32)
    c[:64, C_BSEL:C_BSEL + 16] = bsel
    bm = np.tile(bsel.T.reshape(1, 16 * 64), (128, 1))
    c[:, C_BM:C_BM + 1024] = bm
    return c


def _blk(w, c0):
    return np.ascontiguousarray(w[:, c0:c0 + 128].reshape(8, 128, 128).transpose(1, 0, 2)).reshape(128, 1024)


_PROG = {}


def kernel(x_prompt, x_sample, state_conv_a, state_conv_qkv, state_delta, w_in, conv_a_w, conv_b_w, a_log, dt_bias,
           onorm_w, w_out_a, w_out_b, w_o, norm_w, final_norm_w):
    f = np.float32
    w_in0 = np.asarray(w_in[0], f)
    woa = np.asarray(w_out_a[0], f)
    wob = np.asarray(w_out_b[0], f)
    blocks = []
    for c in range(8):
        for off in (O_CA, O_HA, O_ZA, O_BA):
            blocks.append(_blk(w_in0, off + c * 128))
    for j in range(8):
        blocks.append(_blk(w_in0, O_GA + j * 128))
        blocks.append(_blk(woa, j * 128))
    for h in range(8):
        for off in (O_K, O_Q, O_V, O_ZB):
            blocks.append(_blk(w_in0, off + h * 128))
    for j in range(8):
        blocks.append(_blk(w_in0, O_GB + j * 128))
        blocks.append(_blk(wob, j * 128))
    wstream = np.stack(blocks)
    wba = np.ascontiguousarray(w_in0[:, O_BETA:O_BETA + 16].reshape(8, 128, 16).transpose(1, 0, 2)).reshape(128, 128)
    wo = np.ascontiguousarray(np.asarray(w_o[0], f).reshape(8, 128, 1024).transpose(1, 0, 2)).reshape(128, 8192)
    pp = np.zeros((128, 136), f)
    pp[:, 0:8] = np.asarray(norm_w[0], f).reshape(8, 128).T
    pp[:, 8:32] = np.asarray(conv_a_w[0], f).reshape(3, 8, 128).transpose(2, 1, 0).reshape(128, 24)
    pp[:, 32:128] = np.asarray(conv_b_w[0], f).reshape(4, 24, 128).transpose(2, 1, 0).reshape(128, 96)
    pp[:, 128] = np.asarray(onorm_w[0], f)
    rp = np.zeros((128, 1040), f)
    rp[:, 0:1024] = np.asarray(final_norm_w, f)[None, :]
    rp[:, 1024:1032] = np.asarray(a_log[0], f)[None, :]
    rp[:, 1032:1040] = np.asarray(dt_bias[0], f)[None, :]
    cst = _consts()
    in_maps = []
    for i in range(NCORES):
        xs = np.asarray(x_sample[16 * i:16 * i + 16], f).reshape(NS, D)
        x_all = np.concatenate([np.asarray(x_prompt[i], f), xs], axis=0)
        xT = np.ascontiguousarray(x_all.T).reshape(8, 128, NT)
        sca = np.ascontiguousarray(np.asarray(state_conv_a[0, 16 * i:16 * i + 16], f).reshape(16, 2, 8, 128).transpose(3, 2, 0, 1)).reshape(128, 256)
        scq = np.ascontiguousarray(np.asarray(state_conv_qkv[0, 16 * i:16 * i + 16], f).reshape(16, 3, 24, 128).transpose(3, 2, 0, 1)).reshape(128, 1152)
        sdl = np.ascontiguousarray(np.asarray(state_delta[0, 16 * i:16 * i + 16], f).transpose(1, 2, 0, 3)).reshape(8, 128, 2048)
        in_maps.append({"xT": xT, "xtok": x_all, "wstream": wstream, "wba": wba, "wo": wo, "pp": pp, "rp": rp, "cst": cst,
                        "sca": sca, "scq": scq, "sdl": sdl})
    if "nc" not in _PROG:
        _PROG["nc"] = build_program()
    res = run_bass_kernel_spmd(_PROG["nc"], in_maps, core_ids=list(range(NCORES)))
    R = res.results
    y_prompt = np.stack([R[i]["y"][:T] for i in range(NCORES)])
    y_sample = np.concatenate([R[i]["y"][T:].reshape(16, 4, D) for i in range(NCORES)], axis=0)
    ncap = np.stack([R[i]["ca_p"].reshape(128, 8, 2).transpose(2, 1, 0).reshape(2, 1024) for i in range(NCORES)])[None]
    ncqp = np.stack([R[i]["cq_p"].reshape(128, 24, 3).transpose(2, 1, 0).reshape(3, 3072) for i in range(NCORES)])[None]
    ndp = np.stack([R[i]["dl_p"] for i in range(NCORES)])[None]
    ncas = np.concatenate([R[i]["ca_s"].reshape(128, 8, 16, 2).transpose(2, 3, 1, 0).reshape(16, 2, 1024) for i in range(NCORES)], axis=0)[None]
    ncqs = np.concatenate([R[i]["cq_s"].reshape(128, 24, 16, 3).transpose(2, 3, 1, 0).reshape(16, 3, 3072) for i in range(NCORES)], axis=0)[None]
    nds = np.concatenate([R[i]["dl_s"].reshape(8, 128, 16, 128).transpose(2, 0, 1, 3) for i in range(NCORES)], axis=0)[None]
    return (y_prompt.astype(f), y_sample.astype(f), ncap.astype(f), ncqp.astype(f), ndp.astype(f),
            ncas.astype(f), ncqs.astype(f), nds.astype(f))
```
